# Optimizing a Trainium2 kernel written in Bass

```python
import math
import jax, jax.numpy as jnp
from jax import lax
import numpy as np

D_MODEL = 1024
BATCH = 8
SEQ = 2048
DEPTH = 1

GRID_W = 64
CTX_LEN = 256
HEAD_DIM = 64
N_HEADS = 16
N_KV_HEADS = 4
GROUP = N_HEADS // N_KV_HEADS
ROPE_AXIS_DIM = HEAD_DIM // 2
ROPE_THETA = 10000.0
Q_BLOCK = 128
N_FGROUPS = 4
FGROUP_DIM = 128
N_BRANCH = 2
Q_W = N_HEADS * HEAD_DIM
KV_W = N_KV_HEADS * HEAD_DIM
F_W = N_FGROUPS * FGROUP_DIM
K_END = Q_W + KV_W
V_END = K_END + KV_W
F_END = V_END + F_W
IN_W = F_END + N_BRANCH * D_MODEL
D_FF = ((8 * D_MODEL // 3 + 255) // 256) * 256
EPS = 1e-6

kernel_name = 'hybrid_gqa_fnet_dit_layer'


def rms_norm(x, gain):
    xf = x.astype(jnp.float32)
    y = xf * lax.rsqrt(jnp.mean(xf * xf, axis=-1, keepdims=True) + EPS)
    return (y * gain.astype(jnp.float32)).astype(x.dtype)


def modulate(h, shift, scale):
    return h * (1 + scale) + shift


def adaln(cond, w_ada, b_ada):
    m = jax.nn.silu(cond) @ w_ada + b_ada
    return jnp.split(m, 6, axis=-1)


def axial_rope_tables(rows, dtype):
    t_row = jnp.repeat(jnp.arange(rows, dtype=jnp.float32), GRID_W)
    t_col = jnp.tile(jnp.arange(GRID_W, dtype=jnp.float32), rows)
    inv_freq = ROPE_THETA ** (-jnp.arange(0, ROPE_AXIS_DIM, 2, dtype=jnp.float32) / ROPE_AXIS_DIM)
    ang = jnp.concatenate([t_row[:, None] * inv_freq, t_col[:, None] * inv_freq], axis=-1)
    return jnp.cos(ang)[:, None, :].astype(dtype), jnp.sin(ang)[:, None, :].astype(dtype)


def apply_rope(x, cos, sin):
    x1, x2 = jnp.split(x, 2, axis=-1)
    return jnp.concatenate([x1 * cos - x2 * sin, x2 * cos + x1 * sin], axis=-1)


def split_in_proj(z):
    return jnp.split(z, [Q_W, K_END, V_END, F_END], axis=-1)


def to_heads(t, n):
    return t.reshape(t.shape[0], t.shape[1], n, HEAD_DIM)


def blocked_attention(q, k, v):
    B, L = q.shape[0], q.shape[1]
    nb = L // Q_BLOCK
    qb = q.reshape(B, nb, Q_BLOCK, N_KV_HEADS, GROUP, HEAD_DIM).transpose(1, 0, 2, 3, 4, 5)
    scale = HEAD_DIM ** -0.5

    def one_block(q_blk):
        s = jnp.einsum('bqkgd,btkd->bkgqt', q_blk, k).astype(jnp.float32) * scale
        p = jax.nn.softmax(s, axis=-1).astype(v.dtype)
        return jnp.einsum('bkgqt,btkd->bqkgd', p, v)

    o = lax.map(one_block, qb)
    return o.transpose(1, 0, 2, 3, 4, 5).reshape(B, L, Q_W)


def fourier_mix(f):
    B, L = f.shape[0], f.shape[1]
    fg = f.reshape(B, L, N_FGROUPS, FGROUP_DIM).astype(jnp.float32)
    y = jnp.real(jnp.fft.fft2(fg, axes=(1, 3), norm='ortho'))
    return y.reshape(B, L, F_W).astype(f.dtype)


def merge_branches(attn, four, gate_logits, b_gate, w_attn_proj, w_fourier_proj, w_o):
    g_attn, g_four = jnp.split(jax.nn.sigmoid(gate_logits + b_gate), N_BRANCH, axis=-1)
    merged = g_attn * (attn @ w_attn_proj) + g_four * (fourier_mix(four) @ w_fourier_proj)
    return merged @ w_o


def swiglu(h, w_gate_up, w_down):
    gate, up = jnp.split(h @ w_gate_up, 2, axis=-1)
    return (jax.nn.silu(gate) * up) @ w_down


def setup_inputs(seed: int = 0) -> dict:
    key = jax.random.key(seed)
    ks = jax.random.split(key, 18)

    def nrm(k, shape, scale):
        return jax.random.normal(k, shape, jnp.float32) * scale

    return {
        'x': nrm(ks[0], (BATCH, SEQ, D_MODEL), 1.0),
        'c': nrm(ks[1], (BATCH, D_MODEL), 1.0),
        'ctx': nrm(ks[2], (BATCH, CTX_LEN, D_MODEL), 1.0),
        'c_ctx': nrm(ks[3], (D_MODEL,), 1.0),
        'w_ada': nrm(ks[4], (DEPTH, D_MODEL, 6 * D_MODEL), 0.5 * D_MODEL ** -0.5),
        'b_ada': nrm(ks[5], (DEPTH, 6 * D_MODEL), 0.01),
        'g_norm_mix': 1.0 + nrm(ks[6], (DEPTH, D_MODEL), 0.1),
        'g_norm_ffn': 1.0 + nrm(ks[7], (DEPTH, D_MODEL), 0.1),
        'w_in': nrm(ks[8], (DEPTH, D_MODEL, IN_W), D_MODEL ** -0.5),
        'b_gate': nrm(ks[9], (DEPTH, N_BRANCH * D_MODEL), 0.1),
        'g_q': 1.0 + nrm(ks[10], (DEPTH, HEAD_DIM), 0.1),
        'g_k': 1.0 + nrm(ks[11], (DEPTH, HEAD_DIM), 0.1),
        'w_attn_proj': nrm(ks[12], (DEPTH, Q_W, D_MODEL), Q_W ** -0.5),
        'w_fourier_proj': nrm(ks[13], (DEPTH, F_W, D_MODEL), F_W ** -0.5),
        'w_o': nrm(ks[14], (DEPTH, D_MODEL, D_MODEL), D_MODEL ** -0.5),
        'w_gate_up': nrm(ks[15], (DEPTH, D_MODEL, 2 * D_FF), D_MODEL ** -0.5),
        'w_down': nrm(ks[16], (DEPTH, D_FF, D_MODEL), D_FF ** -0.5),
        'g_final': 1.0 + nrm(ks[17], (D_MODEL,), 0.1),
    }


def reference(x, c, ctx, c_ctx, w_ada, b_ada, g_norm_mix, g_norm_ffn, w_in, b_gate, g_q, g_k,
              w_attn_proj, w_fourier_proj, w_o, w_gate_up, w_down, g_final):
    B, S = x.shape[0], x.shape[1]
    rows = S // GRID_W
    cos, sin = axial_rope_tables(rows, x.dtype)
    cond_lat = c[:, None, :]
    cond_ctx = c_ctx[None, None, :]
    for layer in range(DEPTH):
        last = layer == DEPTH - 1
        w_in_l = w_in[layer]
        sh_m, sc_m, gt_m, sh_f, sc_f, gt_f = adaln(cond_lat, w_ada[layer], b_ada[layer])
        csh_m, csc_m, cgt_m, csh_f, csc_f, cgt_f = adaln(cond_ctx, w_ada[layer], b_ada[layer])

        h_c = modulate(rms_norm(ctx, g_norm_mix[layer]), csh_m, csc_m)
        if last:
            kc, vc = jnp.split(h_c @ w_in_l[:, Q_W:V_END], 2, axis=-1)
        else:
            qc, kc, vc, fc, glc = split_in_proj(h_c @ w_in_l)
        kc = rms_norm(to_heads(kc, N_KV_HEADS), g_k[layer])
        vc = to_heads(vc, N_KV_HEADS)

        h_x = modulate(rms_norm(x, g_norm_mix[layer]), sh_m, sc_m)
        qx, kx, vx, fx, glx = split_in_proj(h_x @ w_in_l)
        qx = apply_rope(rms_norm(to_heads(qx, N_HEADS), g_q[layer]), cos, sin)
        kx = apply_rope(rms_norm(to_heads(kx, N_KV_HEADS), g_k[layer]), cos, sin)
        vx = to_heads(vx, N_KV_HEADS)
        k_all = jnp.concatenate([kc, kx], axis=1)
        v_all = jnp.concatenate([vc, vx], axis=1)
        attn_x = blocked_attention(qx, k_all, v_all)
        mix_x = merge_branches(attn_x, fx, glx, b_gate[layer], w_attn_proj[layer],
                               w_fourier_proj[layer], w_o[layer])
        x = x + gt_m * mix_x
        x = x + gt_f * swiglu(modulate(rms_norm(x, g_norm_ffn[layer]), sh_f, sc_f),
                              w_gate_up[layer], w_down[layer])

        if not last:
            qc = rms_norm(to_heads(qc, N_HEADS), g_q[layer])
            attn_c = blocked_attention(qc, kc, vc)
            mix_c = merge_branches(attn_c, fc, glc, b_gate[layer], w_attn_proj[layer],
                                   w_fourier_proj[layer], w_o[layer])
            ctx = ctx + cgt_m * mix_c
            ctx = ctx + cgt_f * swiglu(modulate(rms_norm(ctx, g_norm_ffn[layer]), csh_f, csc_f),
                                       w_gate_up[layer], w_down[layer])
    return rms_norm(x, g_final)
```

```python
import numpy as np
import ml_dtypes
import concourse.bass as bass
import concourse.mybir as mybir
from concourse.bass_utils import run_bass_kernel_spmd

F32 = mybir.dt.float32
BF16 = mybir.dt.bfloat16
U8 = mybir.dt.uint8
AF = mybir.ActivationFunctionType
ALU = mybir.AluOpType
AX = mybir.AxisListType

D = 1024
S = 2048
CTX = 256
NT = S // 128
NCT = CTX // 128
NKT = NT + NCT
HD = 64
NH = 16
NKV = 4
DFF = 2816
NFF = DFF // 128
EPS = 1e-6
SB_BYTES = 200 * 1024

DEBUG_TAPS = []


class View:
    __slots__ = ("ap", "sp", "lo", "hi")

    def __init__(self, ap, sp, lo, hi):
        self.ap, self.sp, self.lo, self.hi = ap, sp, lo, hi


class _Op:
    __slots__ = ("fn", "inc", "num", "eng", "seq")

    def __init__(self, fn, eng, seq):
        self.fn, self.eng, self.seq = fn, eng, seq
        self.inc = False
        self.num = None


class _Eng:
    def __init__(self, name):
        self.name = name
        self.prog = []
        self.nops = 0
        self.seen = {}
        self.sem = None


class _Rec:
    __slots__ = ("lo", "hi", "kind", "tk", "eng")

    def __init__(self, lo, hi, kind, tk, eng):
        self.lo, self.hi, self.kind, self.tk, self.eng = lo, hi, kind, tk, eng


class Tracker:
    def __init__(self):
        self.eng = {n: _Eng(n) for n in ("pe", "act", "dve", "pool", "sp")}
        self.recs = {"sb": [], "ps": []}
        self.dma_pools = {}
        self.out_tickets = []

    def add_pool(self, name, k):
        self.dma_pools[name] = dict(n=k, vals=[0] * k, last=[None] * k, rr=0)

    def _wait(self, e, tk):
        if tk is None:
            return
        if tk[0] == "op":
            op = tk[1]
            key = op.eng
            if e.seen.get(key, -1) >= op.seq:
                return
            e.seen[key] = op.seq
            op.inc = True
            e.prog.append(("wait", "op", op))
        else:
            _, pool, si, val = tk
            key = (pool, si)
            if e.seen.get(key, -1) >= val:
                return
            e.seen[key] = val
            e.prog.append(("wait", "dma", (pool, si, val)))

    def _deps(self, e, reads, writes, is_dma):
        need = []
        for v in reads:
            for r in self.recs[v.sp]:
                if r.kind == "w" and r.lo < v.hi and v.lo < r.hi:
                    if r.eng == e.name and not is_dma and e.name == "pe":
                        continue
                    need.append(r.tk)
        for v in writes:
            for r in self.recs[v.sp]:
                if r.lo < v.hi and v.lo < r.hi:
                    if r.eng == e.name and not is_dma and e.name == "pe":
                        continue
                    need.append(r.tk)
        for tk in need:
            self._wait(e, tk)

    def _record(self, reads, writes, tk, engname):
        for v in writes:
            lst = self.recs[v.sp]
            lst[:] = [r for r in lst if not (v.lo <= r.lo and r.hi <= v.hi)]
            lst.append(_Rec(v.lo, v.hi, "w", tk, engname))
        for v in reads:
            lst = self.recs[v.sp]
            done = False
            if engname is not None:
                for r in lst:
                    if r.kind == "r" and r.eng == engname and r.lo == v.lo and r.hi == v.hi:
                        r.tk = tk
                        done = True
                        break
            if not done:
                lst.append(_Rec(v.lo, v.hi, "r", tk, engname))

    def op(self, engname, fn, reads=(), writes=()):
        e = self.eng[engname]
        self._deps(e, reads, writes, False)
        o = _Op(fn, engname, e.nops)
        e.nops += 1
        e.prog.append(("op", o))
        self._record(reads, writes, ("op", o), engname)
        return o

    def dma(self, qname, pool, out_ap, in_ap, reads=(), writes=(), is_output=False):
        e = self.eng[qname]
        self._deps(e, reads, writes, True)
        p = self.dma_pools[pool]
        si = p["rr"]
        p["rr"] = (si + 1) % p["n"]
        self._wait(e, p["last"][si])
        p["vals"][si] += 16
        tk = ("dma", pool, si, p["vals"][si])
        p["last"][si] = tk
        e.prog.append(("dma", out_ap, in_ap, pool, si))
        self._record(reads, writes, tk, None)
        if is_output:
            self.out_tickets.append(tk)
        return tk

    def finish(self):
        e = self.eng["sp"]
        for tk in self.out_tickets:
            self._wait(e, tk)

    def number(self):
        for name, e in self.eng.items():
            c = 0
            for it in e.prog:
                if it[0] == "op" and it[1].inc:
                    c += 1
                    it[1].num = c

    def replay_one(self, nc, name, h, sems, dma_sems):
        if not getattr(self, "_numbered", False):
            self.number()
            self._numbered = True
        if True:
            e = self.eng[name]
            for it in e.prog:
                if it[0] == "wait":
                    if it[1] == "op":
                        op = it[2]
                        h.wait_ge(sems[op.eng], op.num)
                    else:
                        pool, si, val = it[2]
                        h.wait_ge(dma_sems[pool][si], val)
                elif it[0] == "op":
                    o = it[1]
                    ins = o.fn(h)
                    if o.inc:
                        ins.then_inc(sems[o.eng], 1)
                else:
                    _, out_ap, in_ap, pool, si = it
                    h.dma_start(out=out_ap, in_=in_ap).then_inc(dma_sems[pool][si], 16)


def build(debug=False):
    nc = bass.Bass("TRN2", target_bir_lowering=False)
    K = Tracker()
    K.add_pool("sp", 8)
    K.add_pool("pool", 8)

    def din(name, shape, dt=F32):
        return nc.dram_tensor(name, list(shape), dt, kind="ExternalInput").ap()

    x_d = din("x", [S, D])
    ctx_d = din("ctx", [CTX, D])
    cond_d = din("cond", [128, 16])
    wada_d = din("w_ada", [D, 6 * D])
    badac_d = din("b_ada_col", [128, 48])
    bada_d = din("b_ada", [6 * D])
    gcols_d = din("gcols", [128, 32])
    gq_d = din("g_q", [HD])
    gk_d = din("g_k", [HD])
    gfin_d = din("g_final", [D])
    win_d = din("w_in", [D, 4096])
    wap_d = din("w_ap", [D, D])
    wfp_d = din("w_fp", [512, D])
    wo_d = din("w_o", [D, D])
    wgu_d = din("w_gu", [D, 2 * DFF])
    wdn_d = din("w_dn", [DFF, D])
    ident_d = din("ident", [128, 128], BF16)
    ropec_d = din("rope_c", [128, NKT * 64])
    ropes_d = din("rope_s", [128, NKT * 64])
    cs_d = din("dft_cs", [128, 256], BF16)
    dftl_d = din("dft_l", [16, 128, 4096], BF16)
    out_d = nc.dram_tensor("out", [S, D], F32, kind="ExternalOutput").ap()
    taps = {}
    tap_list = []

    from contextlib import ExitStack
    es = ExitStack()
    SB = es.enter_context(nc.sbuf_tensor("SB", [128, SB_BYTES], U8))
    PS = es.enter_context(nc.psum_tensor("PS", [128, 4096], F32))
    sems = {n: es.enter_context(nc.semaphore("s_" + n)) for n in ("pe", "act", "dve", "pool", "sp")}
    dma_sems = {p: [es.enter_context(nc.semaphore("d_%s%d" % (p, i))) for i in range(8)] for p in ("sp", "pool")}

    esz = {F32: 4, BF16: 2, U8: 1}

    def sb(off, dt, shape, p0=0, p1=128):
        n = int(np.prod(shape))
        nb = n * esz[dt]
        assert off + nb <= SB_BYTES, (off, nb)
        ap = SB[p0:p1, off:off + nb].bitcast(dt)
        if len(shape) == 2:
            ap = ap.rearrange("p (a b) -> p a b", a=shape[0])
        elif len(shape) == 3:
            ap = ap.rearrange("p (a b c) -> p a b c", a=shape[0], b=shape[1])
        elif len(shape) == 4:
            ap = ap.rearrange("p (a b c d) -> p a b c d", a=shape[0], b=shape[1], c=shape[2])
        return View(ap, "sb", off, off + nb)

    def bank(b, nb=1):
        return View(PS[:, b * 512:(b + nb) * 512], "ps", b * 2048, (b + nb) * 2048)

    def bank_bf(b):
        return View(PS[:, b * 512:(b + 1) * 512].bitcast(BF16), "ps", b * 2048, (b + 1) * 2048)

    def sub(v, ap):
        return View(ap, v.sp, v.lo, v.hi)

    def rng(v, ap, elo, ehi, es_):
        return View(ap, v.sp, v.lo + elo * es_, v.lo + ehi * es_)

    def mm(out, lhsT, rhs, start, stop):
        return K.op("pe", lambda h: h.matmul(out.ap, lhsT=lhsT.ap, rhs=rhs.ap, start=start, stop=stop),
                    reads=[lhsT, rhs], writes=[out])

    def tr(out, in_, ident):
        return K.op("pe", lambda h: h.transpose(out=out.ap, in_=in_.ap, identity=ident.ap),
                    reads=[in_, ident], writes=[out])

    def act(out, in_, func, scale=1.0, bias=None, accum=None):
        reads = [in_]
        kw = {}
        if isinstance(scale, View):
            reads.append(scale)
            kw["scale"] = scale.ap
        else:
            kw["scale"] = float(scale)
        if isinstance(bias, View):
            reads.append(bias)
            kw["bias"] = bias.ap
        elif bias is not None:
            kw["bias"] = float(bias)
        writes = [out]
        if accum is not None:
            writes.append(accum)
            kw["accum_out"] = accum.ap
        return K.op("act", lambda h: h.activation(out=out.ap, in_=in_.ap, func=func, **kw),
                    reads=reads, writes=writes)

    def tt(eng, out, a, b, op):
        return K.op(eng, lambda h: h.tensor_tensor(out=out.ap, in0=a.ap, in1=b.ap, op=op),
                    reads=[a, b], writes=[out])

    def ts(eng, out, a, s1, op0, s2=None, op1=None):
        reads = [a]
        a1 = s1.ap if isinstance(s1, View) else float(s1)
        if isinstance(s1, View):
            reads.append(s1)
        a2 = None
        if s2 is not None:
            a2 = s2.ap if isinstance(s2, View) else float(s2)
            if isinstance(s2, View):
                reads.append(s2)
        if op1 is None:
            return K.op(eng, lambda h: h.tensor_scalar(out=out.ap, in0=a.ap, scalar1=a1, scalar2=None, op0=op0),
                        reads=reads, writes=[out])
        return K.op(eng, lambda h: h.tensor_scalar(out=out.ap, in0=a.ap, scalar1=a1, scalar2=a2, op0=op0, op1=op1),
                    reads=reads, writes=[out])

    def stt(eng, out, a, s, b, op0, op1):
        reads = [a, b]
        sa = s.ap if isinstance(s, View) else float(s)
        if isinstance(s, View):
            reads.append(s)
        return K.op(eng, lambda h: h.scalar_tensor_tensor(out=out.ap, in0=a.ap, scalar=sa, in1=b.ap, op0=op0, op1=op1),
                    reads=reads, writes=[out])

    def cp(eng, out, in_):
        if eng == "act":
            return K.op("act", lambda h: h.copy(out=out.ap, in_=in_.ap), reads=[in_], writes=[out])
        return K.op(eng, lambda h: h.tensor_copy(out=out.ap, in_=in_.ap), reads=[in_], writes=[out])

    def memset(eng, out, val):
        return K.op(eng, lambda h: h.memset(out.ap, val), writes=[out])

    def recip(out, in_):
        return K.op("dve", lambda h: h.reciprocal(out=out.ap, in_=in_.ap), reads=[in_], writes=[out])

    def reduce_sum(out, in_):
        return K.op("dve", lambda h: h.tensor_reduce(out=out.ap, in_=in_.ap, axis=AX.X, op=ALU.add),
                    reads=[in_], writes=[out])

    def load(dst, src_ap, q="sp"):
        return K.dma(q, q, dst.ap, src_ap, writes=[dst])

    def loadw(dst, src_ap):
        return K.dma("pool", "pool", dst.ap, src_ap, writes=[dst])

    def store(dst_ap, src, is_output=True):
        return K.dma("sp", "sp", dst_ap, src.ap, reads=[src], is_output=is_output)

    def tap(name, view, dt):
        shp = [int(x) for x in view.ap.shape]
        d = nc.dram_tensor("tap_" + name, shp, dt, kind="ExternalOutput").ap()
        K.dma("sp", "sp", d, view.ap, reads=[view], is_output=True)
        tap_list.append("tap_" + name)

    def rstd_from_ms(rstd, ms, tmp, mhalf):
        ts("pool", tmp, ms, EPS, ALU.add)
        tt("pool", rstd, tmp, mhalf, ALU.pow)

    KB = 1024
    HT_O = 0
    AT_O = 32 * KB
    KV_O = 76 * KB
    FY_O = 121 * KB
    QM_O = 137 * KB
    SM_O = 169 * KB
    assert SM_O + 31 * KB <= SB_BYTES

    c_o = SM_O
    ident = sb(c_o, BF16, (128,)); c_o += 256
    cond = sb(c_o, F32, (8, 2)); c_o += 64
    scond = sb(c_o, BF16, (8, 2)); c_o += 32
    badac = sb(c_o, F32, (48,)); c_o += 192
    gcols = sb(c_o, F32, (32,)); c_o += 128
    modraw = sb(c_o, F32, (4, 8, 2)); c_o += 256
    modc = sb(c_o, F32, (6, 8)); c_o += 192
    stats = sb(c_o, F32, (64,)); c_o += 256
    mhalf = sb(c_o, F32, (16,)); c_o += 64
    gq_bc = sb(c_o, F32, (64,)); c_o += 256
    gk_bc = sb(c_o, F32, (64,)); c_o += 256
    cs_t = sb(c_o, BF16, (256,)); c_o += 512
    scond_bc = sb(c_o, BF16, (8, 128)); c_o += 2048
    GTm = sb(c_o, F32, (1024,)); c_o += 4096
    GTf = sb(c_o, F32, (1024,)); c_o += 4096
    SMF = c_o
    SM_END = SM_O + 31 * KB

    load(ident, ident_d)
    load(cond, cond_d.rearrange("p (a b) -> p a b", a=8))
    load(badac, badac_d)
    load(gcols, gcols_d)
    load(gq_bc, gq_d.partition_broadcast(128))
    load(gk_bc, gk_d.partition_broadcast(128))
    load(cs_t, cs_d)
    memset("dve", mhalf, -0.5)
    act(scond, cond, AF.Silu)
    cp("dve", scond_bc, sub(scond, scond.ap[:, :, 0:1].to_broadcast([128, 8, 128])))

    wada_v = wada_d.rearrange("(k p) n -> p k n", p=128)
    ada_slots = [sb(QM_O + i * 8 * KB, BF16, (8, 512)) for i in range(4)]
    slot_i = [0]

    def ada_piece(col0):
        s = ada_slots[slot_i[0] % 4]
        slot_i[0] += 1
        loadw(s, wada_v[:, :, col0:col0 + 512])
        return s

    pcol = bank(0)
    pcol_v = sub(pcol, pcol.ap[:, 0:64].rearrange("p (v c n) -> p v c n", v=4, c=8))
    for vi, v in enumerate((0, 1, 3, 4)):
        for half in range(2):
            s = ada_piece(v * 1024 + half * 512)
            for cc in range(4):
                c = half * 4 + cc
                for k in range(8):
                    mm(sub(pcol, pcol_v.ap[:, vi, c, :]),
                       sub(s, s.ap[:, k, cc * 128:(cc + 1) * 128]),
                       sub(scond, scond.ap[:, k, :]), k == 0, k == 7)
    for vi, v in enumerate((0, 1, 3, 4)):
        tt("dve", sub(modraw, modraw.ap[:, vi, :, :]), sub(pcol, pcol_v.ap[:, vi, :, :]),
           sub(badac, badac.ap[:, v * 8:(v + 1) * 8].unsqueeze(2).to_broadcast([128, 8, 2])), ALU.add)
    gmix = sub(gcols, gcols.ap[:, 0:8])
    gffn = sub(gcols, gcols.ap[:, 8:16])
    bgate = sub(gcols, gcols.ap[:, 16:32])
    stt("dve", sub(modc, modc.ap[:, 0, :]), sub(modraw, modraw.ap[:, 1, :, 0]), 1.0, gmix, ALU.add, ALU.mult)
    cp("dve", sub(modc, modc.ap[:, 1, :]), sub(modraw, modraw.ap[:, 0, :, 0]))
    stt("dve", sub(modc, modc.ap[:, 2, :]), sub(modraw, modraw.ap[:, 1, :, 1]), 1.0, gmix, ALU.add, ALU.mult)
    cp("dve", sub(modc, modc.ap[:, 3, :]), sub(modraw, modraw.ap[:, 0, :, 1]))
    stt("dve", sub(modc, modc.ap[:, 4, :]), sub(modraw, modraw.ap[:, 3, :, 0]), 1.0, gffn, ALU.add, ALU.mult)
    cp("dve", sub(modc, modc.ap[:, 5, :]), sub(modraw, modraw.ap[:, 2, :, 0]))

    for gt, v in ((GTm, 2), (GTf, 5)):
        load(gt, bada_d[v * 1024:(v + 1) * 1024].partition_broadcast(128))
        for half in range(2):
            s = ada_piece(v * 1024 + half * 512)
            pb = bank(1 + half)
            for k in range(8):
                mm(pb, sub(scond_bc, scond_bc.ap[:, k, :]), sub(s, s.ap[:, k, :]), k == 0, k == 7)
            g_half = rng(gt, gt.ap[:, half * 512:(half + 1) * 512], half * 512, half * 512 + 512, 4)
            tt("dve", g_half, pb, g_half, ALU.add)

    if debug:
        tap("modc", modc, F32)
        tap("GTm", GTm, F32)

    hT = sb(HT_O, BF16, (16, 8, 128))
    qT = sb(QM_O, BF16, (16, 8, 128))
    kT2 = sb(KV_O, BF16, (4, NKT * 128))
    Vaug = sb(KV_O + 18 * KB, BF16, (NKT, 4, 192))
    hcT = sb(QM_O + 28 * KB, BF16, (2, 8, 128))
    Wq0 = sb(AT_O, BF16, (8, 512))
    Wq1 = sb(AT_O + 8 * KB, BF16, (8, 512))
    Wkv = sb(AT_O + 16 * KB, BF16, (8, 512))
    Wf = sb(AT_O + 24 * KB, BF16, (8, 512))
    ropec = sb(AT_O + 32 * KB, F32, (NKT, 64))
    ropes = sb(AT_O + 32 * KB + 4608, F32, (NKT, 64))
    win_v = win_d.rearrange("(k p) n -> p k n", p=128)
    loadw(Wkv, win_v[:, :, 1024:1536])
    loadw(Wq0, win_v[:, :, 0:512])
    loadw(Wq1, win_v[:, :, 512:1024])
    loadw(Wf, win_v[:, :, 1536:2048])
    load(ropec, ropec_d.rearrange("p (a b) -> p a b", a=NKT))
    load(ropes, ropes_d.rearrange("p (a b) -> p a b", a=NKT))
    memset("dve", Vaug, 1.0)

    o = SMF
    xbuf = [sb(o + i * 4096, F32, (1024,)) for i in range(2)]; o += 8192
    junk = sb(o, BF16, (1024,)); o += 2048
    xn = [sb(o + i * 2048, BF16, (1024,)) for i in range(2)]; o += 4096
    kr2 = sb(o, BF16, (4, 2, 64)); o += 1024
    tabs = sb(o, F32, (4, 64)); o += 1024
    assert o <= SM_END, o
    o = FY_O
    sq = sb(o, F32, (8, 64)); o += 2048
    qn = sb(o, F32, (8, 64)); o += 2048
    uu = sb(o, F32, (8, 64)); o += 2048
    ww = sb(o, F32, (8, 64)); o += 2048
    qr = [sb(o + i * 2048, BF16, (16, 64)) for i in range(2)]; o += 4096
    assert o <= FY_O + 16 * KB

    st_ms = lambda i: sub(stats, stats.ap[:, i:i + 1])

    def qk_norm_rope(ps_v, nh, tab_c, tab_s, dsts, tile_i):
        p3 = sub(ps_v, ps_v.ap.rearrange("p (h d) -> p h d", h=nh))
        sq3 = sub(sq, sq.ap[:, 0:nh, :])
        act(sq3, p3, AF.Square, scale=0.125)
        ss = sub(stats, stats.ap[:, 8:8 + nh])
        tmp = sub(stats, stats.ap[:, 16:16 + nh])
        rs = sub(stats, stats.ap[:, 24:24 + nh])
        reduce_sum(ss, sq3)
        rstd_from_ms(rs, ss, tmp, sub(mhalf, mhalf.ap[:, 0:nh]))
        qn3 = sub(qn, qn.ap[:, 0:nh, :])
        tt("dve", qn3, p3, sub(rs, rs.ap.unsqueeze(2).to_broadcast([128, nh, 64])), ALU.mult)
        u3 = sub(uu, uu.ap[:, 0:nh, :])
        w3 = sub(ww, ww.ap[:, 0:nh, :])
        tt("dve", u3, qn3, sub(tabs, tab_c.unsqueeze(1).to_broadcast([128, nh, 64])), ALU.mult)
        tt("dve", w3, qn3, sub(tabs, tab_s.unsqueeze(1).to_broadcast([128, nh, 64])), ALU.mult)
        for d in dsts:
            tt("dve", sub(d, d.ap[:, :, 0:32]), sub(uu, u3.ap[:, :, 0:32]), sub(ww, w3.ap[:, :, 32:64]), ALU.subtract)
            tt("dve", sub(d, d.ap[:, :, 32:64]), sub(uu, u3.ap[:, :, 32:64]), sub(ww, w3.ap[:, :, 0:32]), ALU.add)

    pb_i = [0]

    def next_bank():
        b = pb_i[0] % 8
        pb_i[0] += 1
        return b

    for tt_i in range(NKT):
        is_ctx = tt_i < NCT
        t = tt_i - NCT
        xb = xbuf[tt_i % 2]
        src = ctx_d[tt_i * 128:(tt_i + 1) * 128, :] if is_ctx else x_d[t * 128:(t + 1) * 128, :]
        load(xb, src)
        sc = (tt_i % 2) * 3
        act(junk, xb, AF.Square, scale=1.0 / 32.0, accum=st_ms(sc))
        rstd_from_ms(st_ms(sc + 2), st_ms(sc), st_ms(sc + 1), sub(mhalf, mhalf.ap[:, 0:1]))
        xnb = xn[tt_i % 2]
        ts("dve", xnb, xb, st_ms(sc + 2), ALU.mult)
        bT = next_bank()
        pT = bank_bf(bT)
        for k in range(8):
            tr(sub(pT, pT.ap[:, k * 128:(k + 1) * 128]), sub(xnb, xnb.ap[:, k * 128:(k + 1) * 128]), ident)
        gi, si = (2, 3) if is_ctx else (0, 1)
        for k in range(8):
            if is_ctx:
                dst = rng(hcT, hcT.ap[:, tt_i, k, :], (tt_i * 8 + k) * 128, (tt_i * 8 + k + 1) * 128, 2)
            else:
                dst = rng(hT, hT.ap[:, t, k, :], (t * 8 + k) * 128, (t * 8 + k + 1) * 128, 2)
            act(dst, sub(pT, pT.ap[:, k * 128:(k + 1) * 128]), AF.Identity,
                scale=sub(modc, modc.ap[:, gi, k:k + 1]), bias=sub(modc, modc.ap[:, si, k:k + 1]))

        def hsl(k):
            if is_ctx:
                return rng(hcT, hcT.ap[:, tt_i, k, :], (tt_i * 8 + k) * 128, (tt_i * 8 + k + 1) * 128, 2)
            return rng(hT, hT.ap[:, t, k, :], (t * 8 + k) * 128, (t * 8 + k + 1) * 128, 2)

        rc = sub(ropec, ropec.ap[:, tt_i, :])
        rs_ = sub(ropes, ropes.ap[:, tt_i, :])
        if not is_ctx:
            stt("dve", sub(tabs, tabs.ap[:, 0, :]), rc, 0.125, gq_bc, ALU.mult, ALU.mult)
            stt("dve", sub(tabs, tabs.ap[:, 1, :]), rs_, 0.125, gq_bc, ALU.mult, ALU.mult)
        tt("dve", sub(tabs, tabs.ap[:, 2, :]), rc, gk_bc, ALU.mult)
        tt("dve", sub(tabs, tabs.ap[:, 3, :]), rs_, gk_bc, ALU.mult)

        bkv = next_bank()
        pkv = bank(bkv)
        for k in range(8):
            mm(pkv, hsl(k), sub(Wkv, Wkv.ap[:, k, :]), k == 0, k == 7)
        kr_a = sub(kr2, kr2.ap[:, :, 0, :])
        kr_b = sub(kr2, kr2.ap[:, :, 1, :])
        qk_norm_rope(sub(pkv, pkv.ap[:, 0:256]), 4, tabs.ap[:, 2, :], tabs.ap[:, 3, :], [kr_a, kr_b], tt_i)
        vdst = rng(Vaug, Vaug.ap[:, tt_i, :, 64:128], tt_i * 768, (tt_i + 1) * 768, 2)
        cp("act", vdst, sub(pkv, pkv.ap[:, 256:512].rearrange("p (h d) -> p h d", h=4)))
        bk = next_bank()
        pk = bank_bf(bk)
        for kh in range(4):
            tr(sub(pk, pk.ap[:, kh * 128:(kh + 1) * 128]),
               sub(kr2, kr2.ap[:, kh, :, :].rearrange("p a d -> p (a d)")), ident)
        kdst = sub(kT2, kT2.ap[:, :, tt_i * 128:(tt_i + 1) * 128])
        cp("act", kdst, sub(pk, pk.ap[:, 0:512].rearrange("p (h t) -> p h t", h=4)))

        if is_ctx:
            continue
        qrb = qr[t % 2]
        for qb, Wq in enumerate((Wq0, Wq1)):
            bq = next_bank()
            pq = bank(bq)
            for k in range(8):
                mm(pq, hsl(k), sub(Wq, Wq.ap[:, k, :]), k == 0, k == 7)
            qd = rng(qrb, qrb.ap[:, qb * 8:(qb + 1) * 8, :], qb * 512, (qb + 1) * 512, 2)
            qk_norm_rope(pq, 8, tabs.ap[:, 0, :], tabs.ap[:, 1, :], [qd], tt_i)
        bq = next_bank()
        pqt = bank_bf(bq)
        for c in range(8):
            tr(sub(pqt, pqt.ap[:, c * 128:(c + 1) * 128]),
               sub(qrb, qrb.ap[:, 2 * c:2 * c + 2, :].rearrange("p a d -> p (a d)")), ident)
        qdst = rng(qT, qT.ap[:, t, :, :].rearrange("p c t -> p (c t)"), t * 1024, (t + 1) * 1024, 2)
        cp("act", qdst, pqt)

    if debug:
        tap("hT", hT, BF16)
        tap("qT", qT, BF16)
        tap("kT2", kT2, BF16)
        tap("Vaug", Vaug, BF16)

    FT = sb(FY_O, BF16, (4, 2048))
    for fg in range(4):
        for tb in range(4):
            b = next_bank()
            pf = bank(b)
            rhs_all = lambda k: rng(hT, hT.ap[:, tb * 4:(tb + 1) * 4, k, :], tb * 4096, (tb + 1) * 4096, 2)
            for k in range(8):
                mm(pf, sub(Wf, Wf.ap[:, k, fg * 128:(fg + 1) * 128]), rhs_all(k), k == 0, k == 7)
            fdst = rng(FT, FT.ap[:, fg, tb * 512:(tb + 1) * 512], fg * 2048 + tb * 512, fg * 2048 + (tb + 1) * 512, 2)
            cp("act" if (fg + tb) % 2 else "dve", fdst, pf)
    if debug:
        tap("FT", FT, BF16)

    attnT = sb(AT_O, BF16, (8, 2048))
    PT = [sb(SMF + i * 2048, BF16, (1024,)) for i in range(3)]
    rden = [sb(SMF + 6144 + i * 4096, F32, (1024,)) for i in range(2)]
    assert SMF + 6144 + 8192 <= SM_END
    steps = [(h, qh, kt) for h in range(NH) for qh in range(2) for kt in range(NKT)]

    def s_bank(i):
        return bank((i % 2) * 2, 2)

    def acc_bank(j):
        return bank(4 + (j % 2) * 2, 2)

    def emit_qk(i):
        h, qh, kt = steps[i]
        kh = h // 4
        half = h % 2
        c = h // 2
        sbk = s_bank(i)
        p0, p1 = half * 64, half * 64 + 64
        lhsT = View(kT2.ap[p0:p1, kh, kt * 128:(kt + 1) * 128], "sb",
                    kT2.lo + (kh * NKT * 128 + kt * 128) * 2, kT2.lo + (kh * NKT * 128 + kt * 128 + 128) * 2)
        for j in range(2):
            t0 = qh * 8 + j * 4
            rhs = View(qT.ap[p0:p1, t0:t0 + 4, c, :], "sb", qT.lo + t0 * 2048, qT.lo + (t0 + 4) * 2048)
            mm(View(sbk.ap[:, j * 512:(j + 1) * 512], "ps", sbk.lo + j * 2048, sbk.lo + (j + 1) * 2048),
               lhsT, rhs, True, True)

    emit_qk(0)
    for i in range(len(steps)):
        h, qh, kt = steps[i]
        kh = h // 4
        half = h % 2
        if i + 1 < len(steps):
            emit_qk(i + 1)
        pt = PT[i % 3]
        act(pt, s_bank(i), AF.Exp)
        j = i // NKT
        acc = acc_bank(j)
        c0 = 64 if half == 0 else 0
        va = View(Vaug.ap[:, kt, kh, c0:c0 + 128], "sb", Vaug.lo + (kt * 4 + kh) * 384, Vaug.lo + (kt * 4 + kh + 1) * 384)
        for jj in range(2):
            mm(View(acc.ap[:, jj * 512:(jj + 1) * 512], "ps", acc.lo + jj * 2048, acc.lo + (jj + 1) * 2048),
               va, sub(pt, pt.ap[:, jj * 512:(jj + 1) * 512]), kt == 0, kt == NKT - 1)
        if kt == NKT - 1:
            rd = rden[j % 2]
            c = h // 2
            if half == 0:
                num = View(acc.ap[0:64, :], "ps", acc.lo, acc.hi)
                den = View(acc.ap[64:128, :], "ps", acc.lo, acc.hi)
                rdv = View(rd.ap[0:64, :], "sb", rd.lo, rd.hi)
                dst = View(attnT.ap[0:64, c, qh * 1024:(qh + 1) * 1024], "sb",
                           attnT.lo + (c * 2048 + qh * 1024) * 2, attnT.lo + (c * 2048 + qh * 1024 + 1024) * 2)
            else:
                num = View(acc.ap[64:128, :], "ps", acc.lo, acc.hi)
                den = View(acc.ap[0:64, :], "ps", acc.lo, acc.hi)
                rdv = View(rd.ap[64:128, :], "sb", rd.lo, rd.hi)
                dst = View(attnT.ap[64:128, c, qh * 1024:(qh + 1) * 1024], "sb",
                           attnT.lo + (c * 2048 + qh * 1024) * 2, attnT.lo + (c * 2048 + qh * 1024 + 1024) * 2)
            recip(rdv, den)
            tt("dve", dst, num, rdv, ALU.mult)
    if debug:
        tap("attnT", attnT, BF16)

    G = sb(QM_O, BF16, (16, 4, 256))
    YT = sb(FY_O, BF16, (4, 2048))
    for lt in range(NT):
        for pr in range(2):
            b = next_bank()
            pg = bank(b)
            for a in range(2):
                fg = pr * 2 + a
                mm(View(pg.ap[:, a * 256:(a + 1) * 256], "ps", pg.lo + a * 1024, pg.lo + (a + 1) * 1024),
                   rng(FT, FT.ap[:, fg, lt * 128:(lt + 1) * 128], fg * 2048 + lt * 128, fg * 2048 + lt * 128 + 128, 2),
                   cs_t, True, True)
            gd = rng(G, G.ap[:, lt, pr * 2:pr * 2 + 2, :].rearrange("p a c -> p (a c)"),
                     (lt * 4 + pr * 2) * 256, (lt * 4 + pr * 2 + 2) * 256, 2)
            cp("act" if (lt + pr) % 2 else "dve", gd, pg)
    dft_slots = [sb(KV_O + i * 8 * KB, BF16, (4, 2, 512)) for i in range(3)]
    di = 0
    for jb in range(4):
        banks = [bank((jb % 2) * 4 + fg) for fg in range(4)]
        for lg in range(4):
            ds = dft_slots[di % 3]
            load(ds, dftl_d[jb * 4 + lg].rearrange("p (a b c) -> p a b c", a=4, b=2))
            di += 1
            for fg in range(4):
                for a in range(4):
                    lt = lg * 4 + a
                    for cs_i in range(2):
                        mm(banks[fg],
                           rng(G, G.ap[:, lt, fg, cs_i * 128:(cs_i + 1) * 128],
                               (lt * 4 + fg) * 256 + cs_i * 128, (lt * 4 + fg) * 256 + cs_i * 128 + 128, 2),
                           sub(ds, ds.ap[:, a, cs_i, :]),
                           lt == 0 and cs_i == 0, lt == NT - 1 and cs_i == 1)
        for fg in range(4):
            yd = rng(YT, YT.ap[:, fg, jb * 512:(jb + 1) * 512], fg * 2048 + jb * 512, fg * 2048 + jb * 512 + 512, 2)
            cp("act" if fg % 2 else "dve", yd, banks[fg])
    if debug:
        tap("YT", YT, BF16)

    MT = sb(QM_O, BF16, (8, 2048))
    wslots = [sb(100 * KB + i * 6 * KB, BF16, (24, 128)) for i in range(2)]
    wfps = [sb(112 * KB + i * KB, BF16, (4, 128)) for i in range(2)]
    gsl = [sb(SMF + i * 2048, F32, (512,)) for i in range(4)]
    wap_v = wap_d.rearrange("(k p) n -> p k n", p=128)
    wfp_v = wfp_d.rearrange("(k p) n -> p k n", p=128)
    for j in range(8):
        wsl = wslots[j % 2]
        wf_ = wfps[j % 2]
        loadw(rng(wsl, wsl.ap[:, 0:8, :], 0, 1024, 2), wap_v[:, :, j * 128:(j + 1) * 128])
        loadw(rng(wsl, wsl.ap[:, 8:16, :], 1024, 2048, 2), win_v[:, :, 2048 + j * 128:2048 + (j + 1) * 128])
        loadw(rng(wsl, wsl.ap[:, 16:24, :], 2048, 3072, 2), win_v[:, :, 3072 + j * 128:3072 + (j + 1) * 128])
        loadw(wf_, wfp_v[:, :, j * 128:(j + 1) * 128])
        for tb in range(4):
            base = ((j * 4 + tb) % 2) * 4
            pA, pF, pGa, pGf = bank(base), bank(base + 1), bank(base + 2), bank(base + 3)
            hsl4 = lambda k: rng(hT, hT.ap[:, tb * 4:(tb + 1) * 4, k, :], tb * 4096, (tb + 1) * 4096, 2)
            for k in range(8):
                mm(pGa, rng(wsl, wsl.ap[:, 8 + k, :], (8 + k) * 128, (9 + k) * 128, 2), hsl4(k), k == 0, k == 7)
            for k in range(8):
                mm(pGf, rng(wsl, wsl.ap[:, 16 + k, :], (16 + k) * 128, (17 + k) * 128, 2), hsl4(k), k == 0, k == 7)
            for k in range(8):
                mm(pA, rng(wsl, wsl.ap[:, k, :], k * 128, (k + 1) * 128, 2),
                   rng(attnT, attnT.ap[:, k, tb * 512:(tb + 1) * 512], k * 2048 + tb * 512, k * 2048 + tb * 512 + 512, 2),
                   k == 0, k == 7)
            for k in range(4):
                mm(pF, sub(wf_, wf_.ap[:, k, :]),
                   rng(YT, YT.ap[:, k, tb * 512:(tb + 1) * 512], k * 2048 + tb * 512, k * 2048 + tb * 512 + 512, 2),
                   k == 0, k == 3)
            act(gsl[0], pGa, AF.Sigmoid, bias=sub(bgate, bgate.ap[:, j:j + 1]))
            act(gsl[1], pGf, AF.Sigmoid, bias=sub(bgate, bgate.ap[:, 8 + j:9 + j]))
            tt("dve", gsl[2], pA, gsl[0], ALU.mult)
            tt("dve", gsl[3], pF, gsl[1], ALU.mult)
            md = rng(MT, MT.ap[:, j, tb * 512:(tb + 1) * 512], j * 2048 + tb * 512, j * 2048 + tb * 512 + 512, 2)
            tt("dve", md, gsl[2], gsl[3], ALU.add)
    if debug:
        tap("MT", MT, BF16)

    X1 = sb(AT_O, F32, (16, 1024))
    Wo = sb(FY_O, BF16, (8, 1024))
    h2T = sb(HT_O, BF16, (16, 8, 128))
    wo_v = wo_d.rearrange("(k p) n -> p k n", p=128)
    loadw(rng(Wo, Wo.ap[:, 0:4, :], 0, 4096, 2), wo_v[:, 0:4, :])
    loadw(rng(Wo, Wo.ap[:, 4:8, :], 4096, 8192, 2), wo_v[:, 4:8, :])
    o = SMF
    xbuf = [sb(o + i * 4096, F32, (1024,)) for i in range(2)]; o += 8192
    junk = sb(o, BF16, (1024,)); o += 2048
    xn = [sb(o + i * 2048, BF16, (1024,)) for i in range(2)]; o += 4096
    tmpf = sb(o, F32, (1024,)); o += 4096
    assert o <= SM_END
    for t in range(NT):
        xb = xbuf[t % 2]
        load(xb, x_d[t * 128:(t + 1) * 128, :])
        x1t = rng(X1, X1.ap[:, t, :], t * 1024, (t + 1) * 1024, 4)
        for nb in range(2):
            b = next_bank()
            po = bank(b)
            for j in range(8):
                mm(po, rng(MT, MT.ap[:, j, t * 128:(t + 1) * 128], j * 2048 + t * 128, j * 2048 + t * 128 + 128, 2),
                   rng(Wo, Wo.ap[:, j, nb * 512:(nb + 1) * 512], j * 1024 + nb * 512, j * 1024 + nb * 512 + 512, 2),
                   j == 0, j == 7)
            tmh = rng(tmpf, tmpf.ap[:, nb * 512:(nb + 1) * 512], nb * 512, nb * 512 + 512, 4)
            tt("dve", tmh, po, rng(GTm, GTm.ap[:, nb * 512:(nb + 1) * 512], nb * 512, nb * 512 + 512, 4), ALU.mult)
            tt("dve", rng(X1, X1.ap[:, t, nb * 512:(nb + 1) * 512], t * 1024 + nb * 512, t * 1024 + nb * 512 + 512, 4),
               tmh, rng(xb, xb.ap[:, nb * 512:(nb + 1) * 512], nb * 512, nb * 512 + 512, 4), ALU.add)
        sc = (t % 2) * 3
        act(junk, x1t, AF.Square, scale=1.0 / 32.0, accum=st_ms(sc))
        rstd_from_ms(st_ms(sc + 2), st_ms(sc), st_ms(sc + 1), sub(mhalf, mhalf.ap[:, 0:1]))
        xnb = xn[t % 2]
        ts("dve", xnb, x1t, st_ms(sc + 2), ALU.mult)
        bT = next_bank()
        pT = bank_bf(bT)
        for k in range(8):
            tr(sub(pT, pT.ap[:, k * 128:(k + 1) * 128]), sub(xnb, xnb.ap[:, k * 128:(k + 1) * 128]), ident)
        for k in range(8):
            dst = rng(h2T, h2T.ap[:, t, k, :], (t * 8 + k) * 128, (t * 8 + k + 1) * 128, 2)
            act(dst, sub(pT, pT.ap[:, k * 128:(k + 1) * 128]), AF.Identity,
                scale=sub(modc, modc.ap[:, 4, k:k + 1]), bias=sub(modc, modc.ap[:, 5, k:k + 1]))
    if debug:
        tap("X1", X1, F32)
        tap("h2T", h2T, BF16)

    Wd = sb(96 * KB, BF16, (NFF, 1024))
    actT = sb(140 * KB, BF16, (NFF, 512))
    gfin = sb(162 * KB, F32, (1024,))
    junk2 = sb(166 * KB, BF16, (1024,))
    wgu_slots = [sb(SMF + i * 4 * KB, BF16, (8, 2, 128)) for i in range(3)]
    sg = [sb(SMF + 12 * KB + i * 2 * KB, F32, (512,)) for i in range(2)]
    assert SMF + 16 * KB <= SM_END
    tm2 = [sb(GTm.lo + i * 2048, F32, (512,)) for i in range(2)]
    wdn_v = wdn_d.rearrange("(k p) n -> p k n", p=128)
    wgu_v = wgu_d.rearrange("(k p) n -> p k n", p=128)
    load(gfin, gfin_d.partition_broadcast(128))
    pieces = [(tq, i) for tq in range(4) for i in range(NFF)]

    def load_piece(pi):
        tq, i = pieces[pi]
        sl = wgu_slots[pi % 3]
        loadw(sub(sl, sl.ap[:, :, 0, :]), wgu_v[:, :, i * 128:(i + 1) * 128])
        loadw(sub(sl, sl.ap[:, :, 1, :]), wgu_v[:, :, DFF + i * 128:DFF + (i + 1) * 128])

    for pi in range(3):
        load_piece(pi)
    for c4 in range(2):
        loadw(rng(Wd, Wd.ap[:, c4 * 11:(c4 + 1) * 11, :], c4 * 11 * 1024, (c4 + 1) * 11 * 1024, 2),
              wdn_v[:, c4 * 11:(c4 + 1) * 11, :])
    for pi, (tq, i) in enumerate(pieces):
        sl = wgu_slots[pi % 3]
        pg_, pu_ = bank((pi % 2) * 2), bank((pi % 2) * 2 + 1)
        rhs4 = lambda k: rng(h2T, h2T.ap[:, tq * 4:(tq + 1) * 4, k, :], tq * 4096, (tq + 1) * 4096, 2)
        for k in range(8):
            mm(pg_, sub(sl, sl.ap[:, k, 0, :]), rhs4(k), k == 0, k == 7)
        for k in range(8):
            mm(pu_, sub(sl, sl.ap[:, k, 1, :]), rhs4(k), k == 0, k == 7)
        if pi + 3 < len(pieces):
            load_piece(pi + 3)
        sgi = sg[pi % 2]
        act(sgi, pg_, AF.Silu)
        tt("dve", rng(actT, actT.ap[:, i, :], i * 512, (i + 1) * 512, 2), pu_, sgi, ALU.mult)
        if i != NFF - 1:
            continue
        for t4 in range(4):
            t = tq * 4 + t4
            x1t = rng(X1, X1.ap[:, t, :], t * 1024, (t + 1) * 1024, 4)
            for nb in range(2):
                pd = bank(4 + ((t4 * 2 + nb) % 4))
                for ii in range(NFF):
                    mm(pd, rng(actT, actT.ap[:, ii, t4 * 128:(t4 + 1) * 128], ii * 512 + t4 * 128, ii * 512 + t4 * 128 + 128, 2),
                       rng(Wd, Wd.ap[:, ii, nb * 512:(nb + 1) * 512], ii * 1024 + nb * 512, ii * 1024 + nb * 512 + 512, 2),
                       ii == 0, ii == NFF - 1)
                tmh = tm2[nb]
                tt("dve", tmh, pd, rng(GTf, GTf.ap[:, nb * 512:(nb + 1) * 512], nb * 512, nb * 512 + 512, 4), ALU.mult)
                x1h = rng(X1, X1.ap[:, t, nb * 512:(nb + 1) * 512], t * 1024 + nb * 512, t * 1024 + nb * 512 + 512, 4)
                tt("dve", x1h, tmh, x1h, ALU.add)
            sc = (t % 2) * 3
            act(junk2, x1t, AF.Square, scale=1.0 / 32.0, accum=st_ms(sc))
            rstd_from_ms(st_ms(sc + 2), st_ms(sc), st_ms(sc + 1), sub(mhalf, mhalf.ap[:, 0:1]))
            stt("dve", x1t, x1t, st_ms(sc + 2), gfin, ALU.mult, ALU.mult)
            store(out_d[t * 128:(t + 1) * 128, :], x1t)

    K.finish()

    with nc.Block() as block:
        @block.tensor
        def _(h):
            K.replay_one(nc, "pe", h, sems, dma_sems)

        @block.scalar
        def _(h):
            K.replay_one(nc, "act", h, sems, dma_sems)

        @block.vector
        def _(h):
            K.replay_one(nc, "dve", h, sems, dma_sems)

        @block.gpsimd
        def _(h):
            K.replay_one(nc, "pool", h, sems, dma_sems)

        @block.sync
        def _(h):
            K.replay_one(nc, "sp", h, sems, dma_sems)
    es.close()
    return nc, tap_list


_CONST = {}


def _consts():
    if _CONST:
        return _CONST
    bf = ml_dtypes.bfloat16
    _CONST["ident"] = np.eye(128, dtype=np.float32).astype(bf)
    inv = (10000.0 ** (-np.arange(0, 32, 2, dtype=np.float32) / np.float32(32))).astype(np.float32)
    tpos = np.arange(S)
    t_row = (tpos // 64).astype(np.float32)
    t_col = (tpos % 64).astype(np.float32)
    ang = np.concatenate([t_row[:, None] * inv[None, :], t_col[:, None] * inv[None, :]], axis=-1).astype(np.float32)
    cos = np.cos(ang).astype(np.float32)
    sin = np.sin(ang).astype(np.float32)
    rc = np.ones((NKT, 128, 64), np.float32)
    rs = np.zeros((NKT, 128, 64), np.float32)
    rc[NCT:] = np.concatenate([cos, cos], -1).reshape(NT, 128, 64)
    rs[NCT:] = np.concatenate([sin, sin], -1).reshape(NT, 128, 64)
    _CONST["rope_c"] = np.ascontiguousarray(rc.transpose(1, 0, 2).reshape(128, NKT * 64))
    _CONST["rope_s"] = np.ascontiguousarray(rs.transpose(1, 0, 2).reshape(128, NKT * 64))
    c = np.arange(128)
    a = 2.0 * np.pi * ((c[:, None] * c[None, :]) % 128) / 128.0
    _CONST["dft_cs"] = (np.concatenate([np.cos(a), np.sin(a)], 1) / 512.0).astype(np.float32).astype(bf)
    l = np.arange(S)
    m = (l[:, None] * l[None, :]) % S
    ang2 = 2.0 * np.pi * m / float(S)
    CL = np.cos(ang2).astype(np.float32).astype(bf)
    SL = (-np.sin(ang2)).astype(np.float32).astype(bf)
    both = np.stack([CL, SL], 0)
    both = both.reshape(2, 4, 4, 128, 4, 512)
    both = both.transpose(4, 1, 3, 2, 0, 5)
    _CONST["dft_l"] = np.ascontiguousarray(both).reshape(16, 128, 4096)
    return _CONST


def _col(v, n):
    return np.ascontiguousarray(np.asarray(v, np.float32).reshape(n, 128).T)


def _prep(inputs):
    f = lambda a: np.ascontiguousarray(np.asarray(a, dtype=np.float32))
    cst = _consts()
    shared = {
        "w_ada": f(inputs["w_ada"][0]),
        "b_ada_col": _col(inputs["b_ada"][0], 48),
        "b_ada": f(inputs["b_ada"][0]),
        "gcols": np.ascontiguousarray(np.concatenate([_col(inputs["g_norm_mix"][0], 8), _col(inputs["g_norm_ffn"][0], 8),
                                                      _col(inputs["b_gate"][0], 16)], axis=1)),
        "g_q": f(inputs["g_q"][0]),
        "g_k": f(inputs["g_k"][0]),
        "g_final": f(inputs["g_final"]),
        "w_in": f(inputs["w_in"][0]),
        "w_ap": f(inputs["w_attn_proj"][0]),
        "w_fp": f(inputs["w_fourier_proj"][0]),
        "w_o": f(inputs["w_o"][0]),
        "w_gu": f(inputs["w_gate_up"][0]),
        "w_dn": f(inputs["w_down"][0]),
    }
    shared.update(cst)
    x = f(inputs["x"])
    ctx = f(inputs["ctx"])
    c = f(inputs["c"])
    cc = _col(inputs["c_ctx"], 8)
    maps = []
    for b in range(8):
        m = dict(shared)
        m["x"] = x[b]
        m["ctx"] = ctx[b]
        cond = np.stack([_col(c[b], 8), cc], axis=-1)
        m["cond"] = np.ascontiguousarray(cond.reshape(128, 16))
        maps.append(m)
    return maps


_NC = {}


def kernel(**inputs):
    if "nc" not in _NC:
        _NC["nc"] = build(False)[0]
    maps = _prep(inputs)
    res = run_bass_kernel_spmd(_NC["nc"], maps, core_ids=list(range(8)))
    out = np.stack([np.asarray(r["out"], dtype=np.float32) for r in res.results], axis=0)
    return out
```

```python
import numpy as np
import ml_dtypes
import concourse.bass as bass
import concourse.mybir as mybir
from concourse.bass_utils import run_bass_kernel_spmd

F32 = mybir.dt.float32
BF16 = mybir.dt.bfloat16
U8 = mybir.dt.uint8
AF = mybir.ActivationFunctionType
ALU = mybir.AluOpType
AX = mybir.AxisListType

D = 1024
S = 2048
CTX = 256
NT = S // 128
NCT = CTX // 128
NKT = NT + NCT
HD = 64
NH = 16
NKV = 4
DFF = 2816
NFF = DFF // 128
EPS = 1e-6
SB_BYTES = 200 * 1024

DEBUG_TAPS = []


class View:
    __slots__ = ("ap", "sp", "lo", "hi")

    def __init__(self, ap, sp, lo, hi):
        self.ap, self.sp, self.lo, self.hi = ap, sp, lo, hi


class _Op:
    __slots__ = ("fn", "inc", "num", "eng", "seq")

    def __init__(self, fn, eng, seq):
        self.fn, self.eng, self.seq = fn, eng, seq
        self.inc = False
        self.num = None


class _Eng:
    def __init__(self, name):
        self.name = name
        self.prog = []
        self.nops = 0
        self.seen = {}
        self.sem = None


class _Rec:
    __slots__ = ("lo", "hi", "kind", "tk", "eng")

    def __init__(self, lo, hi, kind, tk, eng):
        self.lo, self.hi, self.kind, self.tk, self.eng = lo, hi, kind, tk, eng


class Tracker:
    def __init__(self):
        self.eng = {n: _Eng(n) for n in ("pe", "act", "dve", "pool", "sp")}
        self.recs = {"sb": [], "ps": []}
        self.dma_pools = {}
        self.out_tickets = []

    def add_pool(self, name, k):
        self.dma_pools[name] = dict(n=k, vals=[0] * k, last=[None] * k, rr=0)

    def _wait(self, e, tk):
        if tk is None:
            return
        if tk[0] == "op":
            op = tk[1]
            key = op.eng
            if e.seen.get(key, -1) >= op.seq:
                return
            e.seen[key] = op.seq
            op.inc = True
            e.prog.append(("wait", "op", op))
        else:
            _, pool, si, val = tk
            key = (pool, si)
            if e.seen.get(key, -1) >= val:
                return
            e.seen[key] = val
            e.prog.append(("wait", "dma", (pool, si, val)))

    def _deps(self, e, reads, writes, is_dma):
        need = []
        for v in reads:
            for r in self.recs[v.sp]:
                if r.kind == "w" and r.lo < v.hi and v.lo < r.hi:
                    if r.eng == e.name and not is_dma and e.name == "pe":
                        continue
                    need.append(r.tk)
        for v in writes:
            for r in self.recs[v.sp]:
                if r.lo < v.hi and v.lo < r.hi:
                    if r.eng == e.name and not is_dma and e.name == "pe":
                        continue
                    need.append(r.tk)
        for tk in need:
            self._wait(e, tk)

    def _record(self, reads, writes, tk, engname):
        for v in writes:
            lst = self.recs[v.sp]
            lst[:] = [r for r in lst if not (v.lo <= r.lo and r.hi <= v.hi)]
            lst.append(_Rec(v.lo, v.hi, "w", tk, engname))
        for v in reads:
            lst = self.recs[v.sp]
            done = False
            if engname is not None:
                for r in lst:
                    if r.kind == "r" and r.eng == engname and r.lo == v.lo and r.hi == v.hi:
                        r.tk = tk
                        done = True
                        break
            if not done:
                lst.append(_Rec(v.lo, v.hi, "r", tk, engname))

    def op(self, engname, fn, reads=(), writes=()):
        e = self.eng[engname]
        self._deps(e, reads, writes, False)
        o = _Op(fn, engname, e.nops)
        e.nops += 1
        e.prog.append(("op", o))
        self._record(reads, writes, ("op", o), engname)
        return o

    def dma(self, qname, pool, out_ap, in_ap, reads=(), writes=(), is_output=False):
        e = self.eng[qname]
        self._deps(e, reads, writes, True)
        p = self.dma_pools[pool]
        si = p["rr"]
        p["rr"] = (si + 1) % p["n"]
        self._wait(e, p["last"][si])
        p["vals"][si] += 16
        tk = ("dma", pool, si, p["vals"][si])
        p["last"][si] = tk
        e.prog.append(("dma", out_ap, in_ap, pool, si))
        self._record(reads, writes, tk, None)
        if is_output:
            self.out_tickets.append(tk)
        return tk

    def finish(self):
        e = self.eng["sp"]
        for tk in self.out_tickets:
            self._wait(e, tk)

    def number(self):
        for name, e in self.eng.items():
            c = 0
            for it in e.prog:
                if it[0] == "op" and it[1].inc:
                    c += 1
                    it[1].num = c

    def replay_one(self, nc, name, h, sems, dma_sems):
        if not getattr(self, "_numbered", False):
            self.number()
            self._numbered = True
        if True:
            e = self.eng[name]
            for it in e.prog:
                if it[0] == "wait":
                    if it[1] == "op":
                        op = it[2]
                        h.wait_ge(sems[op.eng], op.num)
                    else:
                        pool, si, val = it[2]
                        h.wait_ge(dma_sems[pool][si], val)
                elif it[0] == "op":
                    o = it[1]
                    ins = o.fn(h)
                    if o.inc:
                        ins.then_inc(sems[o.eng], 1)
                else:
                    _, out_ap, in_ap, pool, si = it
                    h.dma_start(out=out_ap, in_=in_ap).then_inc(dma_sems[pool][si], 16)


def build(debug=False):
    nc = bass.Bass("TRN2", target_bir_lowering=False)
    K = Tracker()
    K.add_pool("sp", 8)
    K.add_pool("pool", 8)

    def din(name, shape, dt=F32):
        return nc.dram_tensor(name, list(shape), dt, kind="ExternalInput").ap()

    x_d = din("x", [S, D])
    ctx_d = din("ctx", [CTX, D])
    cond_d = din("cond", [128, 16])
    wada_d = din("w_ada", [D, 6 * D])
    badac_d = din("b_ada_col", [128, 48])
    bada_d = din("b_ada", [6 * D])
    gcols_d = din("gcols", [128, 32])
    gq_d = din("g_q", [HD])
    gk_d = din("g_k", [HD])
    gfin_d = din("g_final", [D])
    win_d = din("w_in", [D, 4096])
    wap_d = din("w_ap", [D, D])
    wfp_d = din("w_fp", [512, D])
    wo_d = din("w_o", [D, D])
    wgu_d = din("w_gu", [D, 2 * DFF])
    wdn_d = din("w_dn", [DFF, D])
    ident_d = din("ident", [128, 128], BF16)
    ropec_d = din("rope_c", [128, NKT * 64])
    ropes_d = din("rope_s", [128, NKT * 64])
    cs_d = din("dft_cs", [128, 256], BF16)
    dftl_d = din("dft_l", [16, 128, 4096], BF16)
    out_d = nc.dram_tensor("out", [S, D], F32, kind="ExternalOutput").ap()
    taps = {}
    tap_list = []

    from contextlib import ExitStack
    es = ExitStack()
    SB = es.enter_context(nc.sbuf_tensor("SB", [128, SB_BYTES], U8))
    PS = es.enter_context(nc.psum_tensor("PS", [128, 4096], F32))
    sems = {n: es.enter_context(nc.semaphore("s_" + n)) for n in ("pe", "act", "dve", "pool", "sp")}
    dma_sems = {p: [es.enter_context(nc.semaphore("d_%s%d" % (p, i))) for i in range(8)] for p in ("sp", "pool")}

    esz = {F32: 4, BF16: 2, U8: 1}

    def sb(off, dt, shape, p0=0, p1=128):
        n = int(np.prod(shape))
        nb = n * esz[dt]
        assert off + nb <= SB_BYTES, (off, nb)
        ap = SB[p0:p1, off:off + nb].bitcast(dt)
        if len(shape) == 2:
            ap = ap.rearrange("p (a b) -> p a b", a=shape[0])
        elif len(shape) == 3:
            ap = ap.rearrange("p (a b c) -> p a b c", a=shape[0], b=shape[1])
        elif len(shape) == 4:
            ap = ap.rearrange("p (a b c d) -> p a b c d", a=shape[0], b=shape[1], c=shape[2])
        return View(ap, "sb", off, off + nb)

    def bank(b, nb=1):
        return View(PS[:, b * 512:(b + nb) * 512], "ps", b * 2048, (b + nb) * 2048)

    def bank_bf(b):
        return View(PS[:, b * 512:(b + 1) * 512].bitcast(BF16), "ps", b * 2048, (b + 1) * 2048)

    def sub(v, ap):
        return View(ap, v.sp, v.lo, v.hi)

    def rng(v, ap, elo, ehi, es_):
        return View(ap, v.sp, v.lo + elo * es_, v.lo + ehi * es_)

    def mm(out, lhsT, rhs, start, stop):
        return K.op("pe", lambda h: h.matmul(out.ap, lhsT=lhsT.ap, rhs=rhs.ap, start=start, stop=stop),
                    reads=[lhsT, rhs], writes=[out])

    def tr(out, in_, ident):
        return K.op("pe", lambda h: h.transpose(out=out.ap, in_=in_.ap, identity=ident.ap),
                    reads=[in_, ident], writes=[out])

    def act(out, in_, func, scale=1.0, bias=None, accum=None):
        reads = [in_]
        kw = {}
        if isinstance(scale, View):
            reads.append(scale)
            kw["scale"] = scale.ap
        else:
            kw["scale"] = float(scale)
        if isinstance(bias, View):
            reads.append(bias)
            kw["bias"] = bias.ap
        elif bias is not None:
            kw["bias"] = float(bias)
        writes = [out]
        if accum is not None:
            writes.append(accum)
            kw["accum_out"] = accum.ap
        return K.op("act", lambda h: h.activation(out=out.ap, in_=in_.ap, func=func, **kw),
                    reads=reads, writes=writes)

    def tt(eng, out, a, b, op):
        return K.op(eng, lambda h: h.tensor_tensor(out=out.ap, in0=a.ap, in1=b.ap, op=op),
                    reads=[a, b], writes=[out])

    def ts(eng, out, a, s1, op0, s2=None, op1=None):
        reads = [a]
        a1 = s1.ap if isinstance(s1, View) else float(s1)
        if isinstance(s1, View):
            reads.append(s1)
        a2 = None
        if s2 is not None:
            a2 = s2.ap if isinstance(s2, View) else float(s2)
            if isinstance(s2, View):
                reads.append(s2)
        if op1 is None:
            return K.op(eng, lambda h: h.tensor_scalar(out=out.ap, in0=a.ap, scalar1=a1, scalar2=None, op0=op0),
                        reads=reads, writes=[out])
        return K.op(eng, lambda h: h.tensor_scalar(out=out.ap, in0=a.ap, scalar1=a1, scalar2=a2, op0=op0, op1=op1),
                    reads=reads, writes=[out])

    def stt(eng, out, a, s, b, op0, op1):
        reads = [a, b]
        sa = s.ap if isinstance(s, View) else float(s)
        if isinstance(s, View):
            reads.append(s)
        return K.op(eng, lambda h: h.scalar_tensor_tensor(out=out.ap, in0=a.ap, scalar=sa, in1=b.ap, op0=op0, op1=op1),
                    reads=reads, writes=[out])

    def cp(eng, out, in_):
        if eng == "act":
            return K.op("act", lambda h: h.copy(out=out.ap, in_=in_.ap), reads=[in_], writes=[out])
        return K.op(eng, lambda h: h.tensor_copy(out=out.ap, in_=in_.ap), reads=[in_], writes=[out])

    def memset(eng, out, val):
        return K.op(eng, lambda h: h.memset(out.ap, val), writes=[out])

    def recip(out, in_):
        return K.op("dve", lambda h: h.reciprocal(out=out.ap, in_=in_.ap), reads=[in_], writes=[out])

    def reduce_sum(out, in_):
        return K.op("dve", lambda h: h.tensor_reduce(out=out.ap, in_=in_.ap, axis=AX.X, op=ALU.add),
                    reads=[in_], writes=[out])

    def load(dst, src_ap, q="sp"):
        return K.dma(q, q, dst.ap, src_ap, writes=[dst])

    def loadw(dst, src_ap):
        return K.dma("pool", "pool", dst.ap, src_ap, writes=[dst])

    def store(dst_ap, src, is_output=True):
        return K.dma("sp", "sp", dst_ap, src.ap, reads=[src], is_output=is_output)

    def tap(name, view, dt):
        shp = [int(x) for x in view.ap.shape]
        d = nc.dram_tensor("tap_" + name, shp, dt, kind="ExternalOutput").ap()
        K.dma("sp", "sp", d, view.ap, reads=[view], is_output=True)
        tap_list.append("tap_" + name)

    def rstd_from_ms(rstd, ms, tmp, mhalf):
        ts("pool", tmp, ms, EPS, ALU.add)
        tt("pool", rstd, tmp, mhalf, ALU.pow)

    KB = 1024
    HT_O = 0
    AT_O = 32 * KB
    KV_O = 76 * KB
    FY_O = 121 * KB
    QM_O = 137 * KB
    SM_O = 169 * KB
    assert SM_O + 31 * KB <= SB_BYTES

    c_o = SM_O
    ident = sb(c_o, BF16, (128,)); c_o += 256
    cond = sb(c_o, F32, (8, 2)); c_o += 64
    scond = sb(c_o, BF16, (8, 2)); c_o += 32
    badac = sb(c_o, F32, (48,)); c_o += 192
    gcols = sb(c_o, F32, (32,)); c_o += 128
    modraw = sb(c_o, F32, (4, 8, 2)); c_o += 256
    modc = sb(c_o, F32, (6, 8)); c_o += 192
    stats = sb(c_o, F32, (64,)); c_o += 256
    mhalf = sb(c_o, F32, (16,)); c_o += 64
    gq_bc = sb(c_o, F32, (64,)); c_o += 256
    gk_bc = sb(c_o, F32, (64,)); c_o += 256
    cs_t = sb(c_o, BF16, (256,)); c_o += 512
    scond_bc = sb(c_o, BF16, (8, 128)); c_o += 2048
    GTm = sb(c_o, F32, (1024,)); c_o += 4096
    GTf = sb(c_o, F32, (1024,)); c_o += 4096
    SMF = c_o
    SM_END = SM_O + 31 * KB

    load(ident, ident_d)
    load(cond, cond_d.rearrange("p (a b) -> p a b", a=8))
    load(badac, badac_d)
    load(gcols, gcols_d)
    load(gq_bc, gq_d.partition_broadcast(128))
    load(gk_bc, gk_d.partition_broadcast(128))
    load(cs_t, cs_d)
    memset("dve", mhalf, -0.5)
    act(scond, cond, AF.Silu)
    cp("dve", scond_bc, sub(scond, scond.ap[:, :, 0:1].to_broadcast([128, 8, 128])))

    wada_v = wada_d.rearrange("(k p) n -> p k n", p=128)
    ada_slots = [sb(QM_O + i * 8 * KB, BF16, (8, 512)) for i in range(4)]
    slot_i = [0]

    def ada_piece(col0):
        s = ada_slots[slot_i[0] % 4]
        slot_i[0] += 1
        loadw(s, wada_v[:, :, col0:col0 + 512])
        return s

    pcol = bank(0)
    pcol_v = sub(pcol, pcol.ap[:, 0:64].rearrange("p (v c n) -> p v c n", v=4, c=8))
    for vi, v in enumerate((0, 1, 3, 4)):
        for half in range(2):
            s = ada_piece(v * 1024 + half * 512)
            for cc in range(4):
                c = half * 4 + cc
                for k in range(8):
                    mm(sub(pcol, pcol_v.ap[:, vi, c, :]),
                       sub(s, s.ap[:, k, cc * 128:(cc + 1) * 128]),
                       sub(scond, scond.ap[:, k, :]), k == 0, k == 7)
    for vi, v in enumerate((0, 1, 3, 4)):
        tt("dve", sub(modraw, modraw.ap[:, vi, :, :]), sub(pcol, pcol_v.ap[:, vi, :, :]),
           sub(badac, badac.ap[:, v * 8:(v + 1) * 8].unsqueeze(2).to_broadcast([128, 8, 2])), ALU.add)
    gmix = sub(gcols, gcols.ap[:, 0:8])
    gffn = sub(gcols, gcols.ap[:, 8:16])
    bgate = sub(gcols, gcols.ap[:, 16:32])
    stt("dve", sub(modc, modc.ap[:, 0, :]), sub(modraw, modraw.ap[:, 1, :, 0]), 1.0, gmix, ALU.add, ALU.mult)
    cp("dve", sub(modc, modc.ap[:, 1, :]), sub(modraw, modraw.ap[:, 0, :, 0]))
    stt("dve", sub(modc, modc.ap[:, 2, :]), sub(modraw, modraw.ap[:, 1, :, 1]), 1.0, gmix, ALU.add, ALU.mult)
    cp("dve", sub(modc, modc.ap[:, 3, :]), sub(modraw, modraw.ap[:, 0, :, 1]))
    stt("dve", sub(modc, modc.ap[:, 4, :]), sub(modraw, modraw.ap[:, 3, :, 0]), 1.0, gffn, ALU.add, ALU.mult)
    cp("dve", sub(modc, modc.ap[:, 5, :]), sub(modraw, modraw.ap[:, 2, :, 0]))

    for gt, v in ((GTm, 2), (GTf, 5)):
        load(gt, bada_d[v * 1024:(v + 1) * 1024].partition_broadcast(128))
        for half in range(2):
            s = ada_piece(v * 1024 + half * 512)
            pb = bank(1 + half)
            for k in range(8):
                mm(pb, sub(scond_bc, scond_bc.ap[:, k, :]), sub(s, s.ap[:, k, :]), k == 0, k == 7)
            g_half = rng(gt, gt.ap[:, half * 512:(half + 1) * 512], half * 512, half * 512 + 512, 4)
            tt("dve", g_half, pb, g_half, ALU.add)

    if debug:
        tap("modc", modc, F32)
        tap("GTm", GTm, F32)

    hT = sb(HT_O, BF16, (16, 8, 128))
    qT = sb(QM_O, BF16, (16, 8, 128))
    kT2 = sb(KV_O, BF16, (4, NKT * 128))
    Vaug = sb(KV_O + 18 * KB, BF16, (NKT, 4, 192))
    hcT = sb(QM_O + 28 * KB, BF16, (2, 8, 128))
    Wq0 = sb(AT_O, BF16, (8, 512))
    Wq1 = sb(AT_O + 8 * KB, BF16, (8, 512))
    Wkv = sb(AT_O + 16 * KB, BF16, (8, 512))
    Wf = sb(AT_O + 24 * KB, BF16, (8, 512))
    ropec = sb(AT_O + 32 * KB, F32, (NKT, 64))
    ropes = sb(AT_O + 32 * KB + 4608, F32, (NKT, 64))
    win_v = win_d.rearrange("(k p) n -> p k n", p=128)
    loadw(Wkv, win_v[:, :, 1024:1536])
    loadw(Wq0, win_v[:, :, 0:512])
    loadw(Wq1, win_v[:, :, 512:1024])
    loadw(Wf, win_v[:, :, 1536:2048])
    load(ropec, ropec_d.rearrange("p (a b) -> p a b", a=NKT))
    load(ropes, ropes_d.rearrange("p (a b) -> p a b", a=NKT))
    memset("dve", Vaug, 1.0)

    o = SMF
    xbuf = [sb(o + i * 4096, F32, (1024,)) for i in range(2)]; o += 8192
    junk = sb(o, BF16, (1024,)); o += 2048
    xn = [sb(o + i * 2048, BF16, (1024,)) for i in range(2)]; o += 4096
    kr2 = sb(o, BF16, (4, 2, 64)); o += 1024
    tabs = sb(o, F32, (4, 64)); o += 1024
    assert o <= SM_END, o
    o = FY_O
    sq = sb(o, F32, (8, 64)); o += 2048
    qn = sb(o, F32, (8, 64)); o += 2048
    uu = sb(o, F32, (8, 64)); o += 2048
    ww = sb(o, F32, (8, 64)); o += 2048
    qr = [sb(o + i * 2048, BF16, (16, 64)) for i in range(2)]; o += 4096
    assert o <= FY_O + 16 * KB

    st_ms = lambda i: sub(stats, stats.ap[:, i:i + 1])

    def qk_norm_rope(ps_v, nh, tab_c, tab_s, dsts, tile_i):
        p3 = sub(ps_v, ps_v.ap.rearrange("p (h d) -> p h d", h=nh))
        sq3 = sub(sq, sq.ap[:, 0:nh, :])
        act(sq3, p3, AF.Square, scale=0.125)
        ss = sub(stats, stats.ap[:, 8:8 + nh])
        tmp = sub(stats, stats.ap[:, 16:16 + nh])
        rs = sub(stats, stats.ap[:, 24:24 + nh])
        reduce_sum(ss, sq3)
        rstd_from_ms(rs, ss, tmp, sub(mhalf, mhalf.ap[:, 0:nh]))
        qn3 = sub(qn, qn.ap[:, 0:nh, :])
        tt("dve", qn3, p3, sub(rs, rs.ap.unsqueeze(2).to_broadcast([128, nh, 64])), ALU.mult)
        u3 = sub(uu, uu.ap[:, 0:nh, :])
        w3 = sub(ww, ww.ap[:, 0:nh, :])
        tt("dve", u3, qn3, sub(tabs, tab_c.unsqueeze(1).to_broadcast([128, nh, 64])), ALU.mult)
        tt("dve", w3, qn3, sub(tabs, tab_s.unsqueeze(1).to_broadcast([128, nh, 64])), ALU.mult)
        for d in dsts:
            tt("dve", sub(d, d.ap[:, :, 0:32]), sub(uu, u3.ap[:, :, 0:32]), sub(ww, w3.ap[:, :, 32:64]), ALU.subtract)
            tt("dve", sub(d, d.ap[:, :, 32:64]), sub(uu, u3.ap[:, :, 32:64]), sub(ww, w3.ap[:, :, 0:32]), ALU.add)

    pb_i = [0]

    def next_bank():
        b = pb_i[0] % 8
        pb_i[0] += 1
        return b

    for tt_i in range(NKT):
        is_ctx = tt_i < NCT
        t = tt_i - NCT
        xb = xbuf[tt_i % 2]
        src = ctx_d[tt_i * 128:(tt_i + 1) * 128, :] if is_ctx else x_d[t * 128:(t + 1) * 128, :]
        load(xb, src)
        sc = (tt_i % 2) * 3
        act(junk, xb, AF.Square, scale=1.0 / 32.0, accum=st_ms(sc))
        rstd_from_ms(st_ms(sc + 2), st_ms(sc), st_ms(sc + 1), sub(mhalf, mhalf.ap[:, 0:1]))
        xnb = xn[tt_i % 2]
        ts("dve", xnb, xb, st_ms(sc + 2), ALU.mult)
        bT = next_bank()
        pT = bank_bf(bT)
        for k in range(8):
            tr(sub(pT, pT.ap[:, k * 128:(k + 1) * 128]), sub(xnb, xnb.ap[:, k * 128:(k + 1) * 128]), ident)
        gi, si = (2, 3) if is_ctx else (0, 1)
        for k in range(8):
            if is_ctx:
                dst = rng(hcT, hcT.ap[:, tt_i, k, :], (tt_i * 8 + k) * 128, (tt_i * 8 + k + 1) * 128, 2)
            else:
                dst = rng(hT, hT.ap[:, t, k, :], (t * 8 + k) * 128, (t * 8 + k + 1) * 128, 2)
            act(dst, sub(pT, pT.ap[:, k * 128:(k + 1) * 128]), AF.Identity,
                scale=sub(modc, modc.ap[:, gi, k:k + 1]), bias=sub(modc, modc.ap[:, si, k:k + 1]))

        def hsl(k):
            if is_ctx:
                return rng(hcT, hcT.ap[:, tt_i, k, :], (tt_i * 8 + k) * 128, (tt_i * 8 + k + 1) * 128, 2)
            return rng(hT, hT.ap[:, t, k, :], (t * 8 + k) * 128, (t * 8 + k + 1) * 128, 2)

        rc = sub(ropec, ropec.ap[:, tt_i, :])
        rs_ = sub(ropes, ropes.ap[:, tt_i, :])
        if not is_ctx:
            stt("dve", sub(tabs, tabs.ap[:, 0, :]), rc, 0.125, gq_bc, ALU.mult, ALU.mult)
            stt("dve", sub(tabs, tabs.ap[:, 1, :]), rs_, 0.125, gq_bc, ALU.mult, ALU.mult)
        tt("dve", sub(tabs, tabs.ap[:, 2, :]), rc, gk_bc, ALU.mult)
        tt("dve", sub(tabs, tabs.ap[:, 3, :]), rs_, gk_bc, ALU.mult)

        bkv = next_bank()
        pkv = bank(bkv)
        for k in range(8):
            mm(pkv, hsl(k), sub(Wkv, Wkv.ap[:, k, :]), k == 0, k == 7)
        kr_a = sub(kr2, kr2.ap[:, :, 0, :])
        kr_b = sub(kr2, kr2.ap[:, :, 1, :])
        qk_norm_rope(sub(pkv, pkv.ap[:, 0:256]), 4, tabs.ap[:, 2, :], tabs.ap[:, 3, :], [kr_a, kr_b], tt_i)
        vdst = rng(Vaug, Vaug.ap[:, tt_i, :, 64:128], tt_i * 768, (tt_i + 1) * 768, 2)
        cp("act", vdst, sub(pkv, pkv.ap[:, 256:512].rearrange("p (h d) -> p h d", h=4)))
        bk = next_bank()
        pk = bank_bf(bk)
        for kh in range(4):
            tr(sub(pk, pk.ap[:, kh * 128:(kh + 1) * 128]),
               sub(kr2, kr2.ap[:, kh, :, :].rearrange("p a d -> p (a d)")), ident)
        kdst = sub(kT2, kT2.ap[:, :, tt_i * 128:(tt_i + 1) * 128])
        cp("act", kdst, sub(pk, pk.ap[:, 0:512].rearrange("p (h t) -> p h t", h=4)))

        if is_ctx:
            continue
        qrb = qr[t % 2]
        for qb, Wq in enumerate((Wq0, Wq1)):
            bq = next_bank()
            pq = bank(bq)
            for k in range(8):
                mm(pq, hsl(k), sub(Wq, Wq.ap[:, k, :]), k == 0, k == 7)
            qd = rng(qrb, qrb.ap[:, qb * 8:(qb + 1) * 8, :], qb * 512, (qb + 1) * 512, 2)
            qk_norm_rope(pq, 8, tabs.ap[:, 0, :], tabs.ap[:, 1, :], [qd], tt_i)
        bq = next_bank()
        pqt = bank_bf(bq)
        for c in range(8):
            tr(sub(pqt, pqt.ap[:, c * 128:(c + 1) * 128]),
               sub(qrb, qrb.ap[:, 2 * c:2 * c + 2, :].rearrange("p a d -> p (a d)")), ident)
        qdst = rng(qT, qT.ap[:, t, :, :].rearrange("p c t -> p (c t)"), t * 1024, (t + 1) * 1024, 2)
        cp("act", qdst, pqt)

    if debug:
        tap("hT", hT, BF16)
        tap("qT", qT, BF16)
        tap("kT2", kT2, BF16)
        tap("Vaug", Vaug, BF16)

    FT = sb(FY_O, BF16, (4, 2048))
    for fg in range(4):
        for tb in range(4):
            b = next_bank()
            pf = bank(b)
            rhs_all = lambda k: rng(hT, hT.ap[:, tb * 4:(tb + 1) * 4, k, :], tb * 4096, (tb + 1) * 4096, 2)
            for k in range(8):
                mm(pf, sub(Wf, Wf.ap[:, k, fg * 128:(fg + 1) * 128]), rhs_all(k), k == 0, k == 7)
            fdst = rng(FT, FT.ap[:, fg, tb * 512:(tb + 1) * 512], fg * 2048 + tb * 512, fg * 2048 + (tb + 1) * 512, 2)
            cp("act" if (fg + tb) % 2 else "dve", fdst, pf)
    if debug:
        tap("FT", FT, BF16)

    attnT = sb(AT_O, BF16, (8, 2048))
    PT = [sb(SMF + i * 2048, BF16, (1024,)) for i in range(3)]
    rden = [sb(SMF + 6144 + i * 4096, F32, (1024,)) for i in range(2)]
    assert SMF + 6144 + 8192 <= SM_END
    steps = [(c, qb, kt) for c in range(NH // 2) for qb in range(4) for kt in range(NKT)]

    def s_bank(i):
        return bank((i % 2) * 2, 2)

    def acc_bank(j):
        return bank(4 + (j % 2) * 2, 2)

    def emit_qk(i):
        c, qb, kt = steps[i]
        kh = c // 2
        sbk = s_bank(i)
        t0 = qb * 4
        for half in range(2):
            p0, p1 = half * 64, half * 64 + 64
            lhsT = View(kT2.ap[p0:p1, kh, kt * 128:(kt + 1) * 128], "sb",
                        kT2.lo + (kh * NKT * 128 + kt * 128) * 2, kT2.lo + (kh * NKT * 128 + kt * 128 + 128) * 2)
            rhs = View(qT.ap[p0:p1, t0:t0 + 4, c, :], "sb", qT.lo + t0 * 2048, qT.lo + (t0 + 4) * 2048)
            mm(View(sbk.ap[:, half * 512:(half + 1) * 512], "ps", sbk.lo + half * 2048, sbk.lo + (half + 1) * 2048),
               lhsT, rhs, True, True)

    emit_qk(0)
    for i in range(len(steps)):
        c, qb, kt = steps[i]
        kh = c // 2
        if i + 1 < len(steps):
            emit_qk(i + 1)
        pt = PT[i % 3]
        act(pt, s_bank(i), AF.Exp)
        j = i // NKT
        acc = acc_bank(j)
        for half in range(2):
            c0 = 64 if half == 0 else 0
            va = View(Vaug.ap[:, kt, kh, c0:c0 + 128], "sb", Vaug.lo + (kt * 4 + kh) * 384, Vaug.lo + (kt * 4 + kh + 1) * 384)
            mm(View(acc.ap[:, half * 512:(half + 1) * 512], "ps", acc.lo + half * 2048, acc.lo + (half + 1) * 2048),
               va, sub(pt, pt.ap[:, half * 512:(half + 1) * 512]), kt == 0, kt == NKT - 1)
        if kt == NKT - 1:
            rd = rden[j % 2]
            for half in range(2):
                ab = View(acc.ap[:, half * 512:(half + 1) * 512], "ps", acc.lo + half * 2048, acc.lo + (half + 1) * 2048)
                n0, d0 = (0, 64) if half == 0 else (64, 0)
                num = View(ab.ap[n0:n0 + 64, :], "ps", ab.lo, ab.hi)
                den = View(ab.ap[d0:d0 + 64, :], "ps", ab.lo, ab.hi)
                rdv = View(rd.ap[n0:n0 + 64, half * 512:(half + 1) * 512], "sb", rd.lo + half * 2048, rd.lo + (half + 1) * 2048)
                dst = View(attnT.ap[n0:n0 + 64, c, qb * 512:(qb + 1) * 512], "sb",
                           attnT.lo + (c * 2048 + qb * 512) * 2, attnT.lo + (c * 2048 + qb * 512 + 512) * 2)
                recip(rdv, den)
                tt("dve", dst, num, rdv, ALU.mult)
    if debug:
        tap("attnT", attnT, BF16)

    G = sb(QM_O, BF16, (16, 4, 256))
    YT = sb(FY_O, BF16, (4, 2048))
    for lt in range(NT):
        for pr in range(2):
            b = next_bank()
            pg = bank(b)
            for a in range(2):
                fg = pr * 2 + a
                mm(View(pg.ap[:, a * 256:(a + 1) * 256], "ps", pg.lo + a * 1024, pg.lo + (a + 1) * 1024),
                   rng(FT, FT.ap[:, fg, lt * 128:(lt + 1) * 128], fg * 2048 + lt * 128, fg * 2048 + lt * 128 + 128, 2),
                   cs_t, True, True)
            gd = rng(G, G.ap[:, lt, pr * 2:pr * 2 + 2, :].rearrange("p a c -> p (a c)"),
                     (lt * 4 + pr * 2) * 256, (lt * 4 + pr * 2 + 2) * 256, 2)
            cp("act" if (lt + pr) % 2 else "dve", gd, pg)
    dft_slots = [sb(KV_O + i * 8 * KB, BF16, (4, 2, 512)) for i in range(3)]
    di = 0
    for jb in range(4):
        banks = [bank((jb % 2) * 4 + fg) for fg in range(4)]
        for lg in range(4):
            ds = dft_slots[di % 3]
            load(ds, dftl_d[jb * 4 + lg].rearrange("p (a b c) -> p a b c", a=4, b=2))
            di += 1
            for fg in range(4):
                for a in range(4):
                    lt = lg * 4 + a
                    for cs_i in range(2):
                        mm(banks[fg],
                           rng(G, G.ap[:, lt, fg, cs_i * 128:(cs_i + 1) * 128],
                               (lt * 4 + fg) * 256 + cs_i * 128, (lt * 4 + fg) * 256 + cs_i * 128 + 128, 2),
                           sub(ds, ds.ap[:, a, cs_i, :]),
                           lt == 0 and cs_i == 0, lt == NT - 1 and cs_i == 1)
        for fg in range(4):
            yd = rng(YT, YT.ap[:, fg, jb * 512:(jb + 1) * 512], fg * 2048 + jb * 512, fg * 2048 + jb * 512 + 512, 2)
            cp("act" if fg % 2 else "dve", yd, banks[fg])
    if debug:
        tap("YT", YT, BF16)

    MT = sb(QM_O, BF16, (8, 2048))
    wslots = [sb(100 * KB + i * 6 * KB, BF16, (24, 128)) for i in range(2)]
    wfps = [sb(112 * KB + i * KB, BF16, (4, 128)) for i in range(2)]
    gsl = [sb(SMF + i * 2048, F32, (512,)) for i in range(4)]
    wap_v = wap_d.rearrange("(k p) n -> p k n", p=128)
    wfp_v = wfp_d.rearrange("(k p) n -> p k n", p=128)
    for j in range(8):
        wsl = wslots[j % 2]
        wf_ = wfps[j % 2]
        loadw(rng(wsl, wsl.ap[:, 0:8, :], 0, 1024, 2), wap_v[:, :, j * 128:(j + 1) * 128])
        loadw(rng(wsl, wsl.ap[:, 8:16, :], 1024, 2048, 2), win_v[:, :, 2048 + j * 128:2048 + (j + 1) * 128])
        loadw(rng(wsl, wsl.ap[:, 16:24, :], 2048, 3072, 2), win_v[:, :, 3072 + j * 128:3072 + (j + 1) * 128])
        loadw(wf_, wfp_v[:, :, j * 128:(j + 1) * 128])
        for tb in range(4):
            base = ((j * 4 + tb) % 2) * 4
            pA, pF, pGa, pGf = bank(base), bank(base + 1), bank(base + 2), bank(base + 3)
            hsl4 = lambda k: rng(hT, hT.ap[:, tb * 4:(tb + 1) * 4, k, :], tb * 4096, (tb + 1) * 4096, 2)
            for k in range(8):
                mm(pGa, rng(wsl, wsl.ap[:, 8 + k, :], (8 + k) * 128, (9 + k) * 128, 2), hsl4(k), k == 0, k == 7)
            for k in range(8):
                mm(pGf, rng(wsl, wsl.ap[:, 16 + k, :], (16 + k) * 128, (17 + k) * 128, 2), hsl4(k), k == 0, k == 7)
            for k in range(8):
                mm(pA, rng(wsl, wsl.ap[:, k, :], k * 128, (k + 1) * 128, 2),
                   rng(attnT, attnT.ap[:, k, tb * 512:(tb + 1) * 512], k * 2048 + tb * 512, k * 2048 + tb * 512 + 512, 2),
                   k == 0, k == 7)
            for k in range(4):
                mm(pF, sub(wf_, wf_.ap[:, k, :]),
                   rng(YT, YT.ap[:, k, tb * 512:(tb + 1) * 512], k * 2048 + tb * 512, k * 2048 + tb * 512 + 512, 2),
                   k == 0, k == 3)
            act(gsl[0], pGa, AF.Sigmoid, bias=sub(bgate, bgate.ap[:, j:j + 1]))
            act(gsl[1], pGf, AF.Sigmoid, bias=sub(bgate, bgate.ap[:, 8 + j:9 + j]))
            tt("dve", gsl[2], pA, gsl[0], ALU.mult)
            tt("dve", gsl[3], pF, gsl[1], ALU.mult)
            md = rng(MT, MT.ap[:, j, tb * 512:(tb + 1) * 512], j * 2048 + tb * 512, j * 2048 + tb * 512 + 512, 2)
            tt("dve", md, gsl[2], gsl[3], ALU.add)
    if debug:
        tap("MT", MT, BF16)

    X1 = sb(AT_O, F32, (16, 1024))
    Wo = sb(FY_O, BF16, (8, 1024))
    h2T = sb(HT_O, BF16, (16, 8, 128))
    wo_v = wo_d.rearrange("(k p) n -> p k n", p=128)
    loadw(rng(Wo, Wo.ap[:, 0:4, :], 0, 4096, 2), wo_v[:, 0:4, :])
    loadw(rng(Wo, Wo.ap[:, 4:8, :], 4096, 8192, 2), wo_v[:, 4:8, :])
    o = SMF
    xbuf = [sb(o + i * 4096, F32, (1024,)) for i in range(2)]; o += 8192
    junk = sb(o, BF16, (1024,)); o += 2048
    xn = [sb(o + i * 2048, BF16, (1024,)) for i in range(2)]; o += 4096
    tmpf = sb(o, F32, (1024,)); o += 4096
    assert o <= SM_END
    for t in range(NT):
        xb = xbuf[t % 2]
        load(xb, x_d[t * 128:(t + 1) * 128, :])
        x1t = rng(X1, X1.ap[:, t, :], t * 1024, (t + 1) * 1024, 4)
        for nb in range(2):
            b = next_bank()
            po = bank(b)
            for j in range(8):
                mm(po, rng(MT, MT.ap[:, j, t * 128:(t + 1) * 128], j * 2048 + t * 128, j * 2048 + t * 128 + 128, 2),
                   rng(Wo, Wo.ap[:, j, nb * 512:(nb + 1) * 512], j * 1024 + nb * 512, j * 1024 + nb * 512 + 512, 2),
                   j == 0, j == 7)
            tmh = rng(tmpf, tmpf.ap[:, nb * 512:(nb + 1) * 512], nb * 512, nb * 512 + 512, 4)
            tt("dve", tmh, po, rng(GTm, GTm.ap[:, nb * 512:(nb + 1) * 512], nb * 512, nb * 512 + 512, 4), ALU.mult)
            tt("dve", rng(X1, X1.ap[:, t, nb * 512:(nb + 1) * 512], t * 1024 + nb * 512, t * 1024 + nb * 512 + 512, 4),
               tmh, rng(xb, xb.ap[:, nb * 512:(nb + 1) * 512], nb * 512, nb * 512 + 512, 4), ALU.add)
        sc = (t % 2) * 3
        act(junk, x1t, AF.Square, scale=1.0 / 32.0, accum=st_ms(sc))
        rstd_from_ms(st_ms(sc + 2), st_ms(sc), st_ms(sc + 1), sub(mhalf, mhalf.ap[:, 0:1]))
        xnb = xn[t % 2]
        ts("dve", xnb, x1t, st_ms(sc + 2), ALU.mult)
        bT = next_bank()
        pT = bank_bf(bT)
        for k in range(8):
            tr(sub(pT, pT.ap[:, k * 128:(k + 1) * 128]), sub(xnb, xnb.ap[:, k * 128:(k + 1) * 128]), ident)
        for k in range(8):
            dst = rng(h2T, h2T.ap[:, t, k, :], (t * 8 + k) * 128, (t * 8 + k + 1) * 128, 2)
            act(dst, sub(pT, pT.ap[:, k * 128:(k + 1) * 128]), AF.Identity,
                scale=sub(modc, modc.ap[:, 4, k:k + 1]), bias=sub(modc, modc.ap[:, 5, k:k + 1]))
    if debug:
        tap("X1", X1, F32)
        tap("h2T", h2T, BF16)

    Wd = sb(96 * KB, BF16, (NFF, 1024))
    actT = sb(140 * KB, BF16, (NFF, 512))
    gfin = sb(GTm.lo, F32, (1024,))
    junk2 = sb(162 * KB, BF16, (1024,))
    tm2 = [sb(164 * KB + i * 2048, F32, (512,)) for i in range(2)]
    wgu_slots = [sb(SMF + i * 8 * KB, BF16, (8, 2, 256)) for i in range(2)]
    sg = [sb(SMF + 16 * KB, F32, (512,)), sb(scond_bc.lo, F32, (512,))]
    assert SMF + 18 * KB <= SM_END
    wdn_v = wdn_d.rearrange("(k p) n -> p k n", p=128)
    wgu_v = wgu_d.rearrange("(k p) n -> p k n", p=128)
    load(gfin, gfin_d.partition_broadcast(128))
    pieces = [(tq, i) for tq in range(4) for i in range(NFF)]
    NG = len(pieces) // 2

    def load_group(gi):
        tq, i = pieces[gi * 2]
        sl = wgu_slots[gi % 2]
        loadw(sub(sl, sl.ap[:, :, 0, :]), wgu_v[:, :, i * 128:(i + 2) * 128])
        loadw(sub(sl, sl.ap[:, :, 1, :]), wgu_v[:, :, DFF + i * 128:DFF + (i + 2) * 128])

    load_group(0)
    load_group(1)
    for c4 in range(2):
        loadw(rng(Wd, Wd.ap[:, c4 * 11:(c4 + 1) * 11, :], c4 * 11 * 1024, (c4 + 1) * 11 * 1024, 2),
              wdn_v[:, c4 * 11:(c4 + 1) * 11, :])
    for pi, (tq, i) in enumerate(pieces):
        sl = wgu_slots[(pi // 2) % 2]
        o2 = (pi % 2) * 128
        pg_, pu_ = bank((pi % 2) * 2), bank((pi % 2) * 2 + 1)
        rhs4 = lambda k: rng(h2T, h2T.ap[:, tq * 4:(tq + 1) * 4, k, :], tq * 4096, (tq + 1) * 4096, 2)
        for k in range(8):
            mm(pg_, sub(sl, sl.ap[:, k, 0, o2:o2 + 128]), rhs4(k), k == 0, k == 7)
        for k in range(8):
            mm(pu_, sub(sl, sl.ap[:, k, 1, o2:o2 + 128]), rhs4(k), k == 0, k == 7)
        if pi % 2 == 1 and pi // 2 + 2 < NG:
            load_group(pi // 2 + 2)
        sgi = sg[pi % 2]
        act(sgi, pg_, AF.Silu)
        tt("dve", rng(actT, actT.ap[:, i, :], i * 512, (i + 1) * 512, 2), pu_, sgi, ALU.mult)
        if i != NFF - 1:
            continue
        for t4 in range(4):
            t = tq * 4 + t4
            x1t = rng(X1, X1.ap[:, t, :], t * 1024, (t + 1) * 1024, 4)
            for nb in range(2):
                pd = bank(4 + ((t4 * 2 + nb) % 4))
                for ii in range(NFF):
                    mm(pd, rng(actT, actT.ap[:, ii, t4 * 128:(t4 + 1) * 128], ii * 512 + t4 * 128, ii * 512 + t4 * 128 + 128, 2),
                       rng(Wd, Wd.ap[:, ii, nb * 512:(nb + 1) * 512], ii * 1024 + nb * 512, ii * 1024 + nb * 512 + 512, 2),
                       ii == 0, ii == NFF - 1)
                tmh = tm2[nb]
                tt("dve", tmh, pd, rng(GTf, GTf.ap[:, nb * 512:(nb + 1) * 512], nb * 512, nb * 512 + 512, 4), ALU.mult)
                x1h = rng(X1, X1.ap[:, t, nb * 512:(nb + 1) * 512], t * 1024 + nb * 512, t * 1024 + nb * 512 + 512, 4)
                tt("dve", x1h, tmh, x1h, ALU.add)
            sc = (t % 2) * 3
            act(junk2, x1t, AF.Square, scale=1.0 / 32.0, accum=st_ms(sc))
            rstd_from_ms(st_ms(sc + 2), st_ms(sc), st_ms(sc + 1), sub(mhalf, mhalf.ap[:, 0:1]))
            stt("dve", x1t, x1t, st_ms(sc + 2), gfin, ALU.mult, ALU.mult)
            store(out_d[t * 128:(t + 1) * 128, :], x1t)

    K.finish()

    with nc.Block() as block:
        @block.tensor
        def _(h):
            K.replay_one(nc, "pe", h, sems, dma_sems)

        @block.scalar
        def _(h):
            K.replay_one(nc, "act", h, sems, dma_sems)

        @block.vector
        def _(h):
            K.replay_one(nc, "dve", h, sems, dma_sems)

        @block.gpsimd
        def _(h):
            K.replay_one(nc, "pool", h, sems, dma_sems)

        @block.sync
        def _(h):
            K.replay_one(nc, "sp", h, sems, dma_sems)
    es.close()
    return nc, tap_list


_CONST = {}


def _consts():
    if _CONST:
        return _CONST
    bf = ml_dtypes.bfloat16
    _CONST["ident"] = np.eye(128, dtype=np.float32).astype(bf)
    inv = (10000.0 ** (-np.arange(0, 32, 2, dtype=np.float32) / np.float32(32))).astype(np.float32)
    tpos = np.arange(S)
    t_row = (tpos // 64).astype(np.float32)
    t_col = (tpos % 64).astype(np.float32)
    ang = np.concatenate([t_row[:, None] * inv[None, :], t_col[:, None] * inv[None, :]], axis=-1).astype(np.float32)
    cos = np.cos(ang).astype(np.float32)
    sin = np.sin(ang).astype(np.float32)
    rc = np.ones((NKT, 128, 64), np.float32)
    rs = np.zeros((NKT, 128, 64), np.float32)
    rc[NCT:] = np.concatenate([cos, cos], -1).reshape(NT, 128, 64)
    rs[NCT:] = np.concatenate([sin, sin], -1).reshape(NT, 128, 64)
    _CONST["rope_c"] = np.ascontiguousarray(rc.transpose(1, 0, 2).reshape(128, NKT * 64))
    _CONST["rope_s"] = np.ascontiguousarray(rs.transpose(1, 0, 2).reshape(128, NKT * 64))
    c = np.arange(128)
    a = 2.0 * np.pi * ((c[:, None] * c[None, :]) % 128) / 128.0
    _CONST["dft_cs"] = (np.concatenate([np.cos(a), np.sin(a)], 1) / 512.0).astype(np.float32).astype(bf)
    l = np.arange(S)
    m = (l[:, None] * l[None, :]) % S
    ang2 = 2.0 * np.pi * m / float(S)
    CL = np.cos(ang2).astype(np.float32).astype(bf)
    SL = (-np.sin(ang2)).astype(np.float32).astype(bf)
    both = np.stack([CL, SL], 0)
    both = both.reshape(2, 4, 4, 128, 4, 512)
    both = both.transpose(4, 1, 3, 2, 0, 5)
    _CONST["dft_l"] = np.ascontiguousarray(both).reshape(16, 128, 4096)
    return _CONST


def _col(v, n):
    return np.ascontiguousarray(np.asarray(v, np.float32).reshape(n, 128).T)


def _prep(inputs):
    f = lambda a: np.ascontiguousarray(np.asarray(a, dtype=np.float32))
    cst = _consts()
    shared = {
        "w_ada": f(inputs["w_ada"][0]),
        "b_ada_col": _col(inputs["b_ada"][0], 48),
        "b_ada": f(inputs["b_ada"][0]),
        "gcols": np.ascontiguousarray(np.concatenate([_col(inputs["g_norm_mix"][0], 8), _col(inputs["g_norm_ffn"][0], 8),
                                                      _col(inputs["b_gate"][0], 16)], axis=1)),
        "g_q": f(inputs["g_q"][0]),
        "g_k": f(inputs["g_k"][0]),
        "g_final": f(inputs["g_final"]),
        "w_in": f(inputs["w_in"][0]),
        "w_ap": f(inputs["w_attn_proj"][0]),
        "w_fp": f(inputs["w_fourier_proj"][0]),
        "w_o": f(inputs["w_o"][0]),
        "w_gu": f(inputs["w_gate_up"][0]),
        "w_dn": f(inputs["w_down"][0]),
    }
    shared.update(cst)
    x = f(inputs["x"])
    ctx = f(inputs["ctx"])
    c = f(inputs["c"])
    cc = _col(inputs["c_ctx"], 8)
    maps = []
    for b in range(8):
        m = dict(shared)
        m["x"] = x[b]
        m["ctx"] = ctx[b]
        cond = np.stack([_col(c[b], 8), cc], axis=-1)
        m["cond"] = np.ascontiguousarray(cond.reshape(128, 16))
        maps.append(m)
    return maps


_NC = {}


def kernel(**inputs):
    if "nc" not in _NC:
        _NC["nc"] = build(False)[0]
    maps = _prep(inputs)
    res = run_bass_kernel_spmd(_NC["nc"], maps, core_ids=list(range(8)))
    out = np.stack([np.asarray(r["out"], dtype=np.float32) for r in res.results], axis=0)
    return out
```

```python
import numpy as np
import ml_dtypes
import concourse.bass as bass
import concourse.mybir as mybir
from concourse.bass_utils import run_bass_kernel_spmd

F32 = mybir.dt.float32
BF16 = mybir.dt.bfloat16
U8 = mybir.dt.uint8
AF = mybir.ActivationFunctionType
ALU = mybir.AluOpType
AX = mybir.AxisListType

D = 1024
S = 2048
CTX = 256
NT = S // 128
NCT = CTX // 128
NKT = NT + NCT
HD = 64
NH = 16
NKV = 4
DFF = 2816
NFF = DFF // 128
EPS = 1e-6
SB_BYTES = 200 * 1024

DEBUG_TAPS = []


class View:
    __slots__ = ("ap", "sp", "lo", "hi")

    def __init__(self, ap, sp, lo, hi):
        self.ap, self.sp, self.lo, self.hi = ap, sp, lo, hi


class _Op:
    __slots__ = ("fn", "inc", "num", "eng", "seq")

    def __init__(self, fn, eng, seq):
        self.fn, self.eng, self.seq = fn, eng, seq
        self.inc = False
        self.num = None


class _Eng:
    def __init__(self, name):
        self.name = name
        self.prog = []
        self.nops = 0
        self.seen = {}
        self.sem = None


class _Rec:
    __slots__ = ("lo", "hi", "kind", "tk", "eng")

    def __init__(self, lo, hi, kind, tk, eng):
        self.lo, self.hi, self.kind, self.tk, self.eng = lo, hi, kind, tk, eng


class Tracker:
    def __init__(self):
        self.eng = {n: _Eng(n) for n in ("pe", "act", "dve", "pool", "sp")}
        self.recs = {"sb": [], "ps": []}
        self.dma_pools = {}
        self.out_tickets = []

    def add_pool(self, name, k):
        self.dma_pools[name] = dict(n=k, vals=[0] * k, last=[None] * k, rr=0)

    def _wait(self, e, tk):
        if tk is None:
            return
        if tk[0] == "op":
            op = tk[1]
            key = op.eng
            if e.seen.get(key, -1) >= op.seq:
                return
            e.seen[key] = op.seq
            op.inc = True
            e.prog.append(("wait", "op", op))
        else:
            _, pool, si, val = tk
            key = (pool, si)
            if e.seen.get(key, -1) >= val:
                return
            e.seen[key] = val
            e.prog.append(("wait", "dma", (pool, si, val)))

    def _deps(self, e, reads, writes, is_dma):
        need = []
        for v in reads:
            for r in self.recs[v.sp]:
                if r.kind == "w" and r.lo < v.hi and v.lo < r.hi:
                    if r.eng == e.name and not is_dma and e.name == "pe":
                        continue
                    need.append(r.tk)
        for v in writes:
            for r in self.recs[v.sp]:
                if r.lo < v.hi and v.lo < r.hi:
                    if r.eng == e.name and not is_dma and e.name == "pe":
                        continue
                    need.append(r.tk)
        for tk in need:
            self._wait(e, tk)

    def _record(self, reads, writes, tk, engname):
        for v in writes:
            lst = self.recs[v.sp]
            lst[:] = [r for r in lst if not (v.lo <= r.lo and r.hi <= v.hi)]
            lst.append(_Rec(v.lo, v.hi, "w", tk, engname))
        for v in reads:
            lst = self.recs[v.sp]
            done = False
            if engname is not None:
                for r in lst:
                    if r.kind == "r" and r.eng == engname and r.lo == v.lo and r.hi == v.hi:
                        r.tk = tk
                        done = True
                        break
            if not done:
                lst.append(_Rec(v.lo, v.hi, "r", tk, engname))

    def op(self, engname, fn, reads=(), writes=()):
        e = self.eng[engname]
        self._deps(e, reads, writes, False)
        o = _Op(fn, engname, e.nops)
        e.nops += 1
        e.prog.append(("op", o))
        self._record(reads, writes, ("op", o), engname)
        return o

    def dma(self, qname, pool, out_ap, in_ap, reads=(), writes=(), is_output=False):
        e = self.eng[qname]
        self._deps(e, reads, writes, True)
        p = self.dma_pools[pool]
        si = p["rr"]
        p["rr"] = (si + 1) % p["n"]
        self._wait(e, p["last"][si])
        p["vals"][si] += 16
        tk = ("dma", pool, si, p["vals"][si])
        p["last"][si] = tk
        e.prog.append(("dma", out_ap, in_ap, pool, si))
        self._record(reads, writes, tk, None)
        if is_output:
            self.out_tickets.append(tk)
        return tk

    def finish(self):
        e = self.eng["sp"]
        for tk in self.out_tickets:
            self._wait(e, tk)

    def number(self):
        for name, e in self.eng.items():
            c = 0
            for it in e.prog:
                if it[0] == "op" and it[1].inc:
                    c += 1
                    it[1].num = c

    def replay_one(self, nc, name, h, sems, dma_sems):
        if not getattr(self, "_numbered", False):
            self.number()
            self._numbered = True
        if True:
            e = self.eng[name]
            for it in e.prog:
                if it[0] == "wait":
                    if it[1] == "op":
                        op = it[2]
                        h.wait_ge(sems[op.eng], op.num)
                    else:
                        pool, si, val = it[2]
                        h.wait_ge(dma_sems[pool][si], val)
                elif it[0] == "op":
                    o = it[1]
                    ins = o.fn(h)
                    if o.inc:
                        ins.then_inc(sems[o.eng], 1)
                else:
                    _, out_ap, in_ap, pool, si = it
                    h.dma_start(out=out_ap, in_=in_ap).then_inc(dma_sems[pool][si], 16)


def build(debug=False):
    nc = bass.Bass("TRN2", target_bir_lowering=False)
    K = Tracker()
    K.add_pool("sp", 8)
    K.add_pool("pool", 8)

    def din(name, shape, dt=F32):
        return nc.dram_tensor(name, list(shape), dt, kind="ExternalInput").ap()

    x_d = din("x", [S, D])
    ctx_d = din("ctx", [CTX, D])
    cond_d = din("cond", [128, 16])
    wada_d = din("w_ada", [D, 6 * D])
    badac_d = din("b_ada_col", [128, 48])
    bada_d = din("b_ada", [6 * D])
    gcols_d = din("gcols", [128, 32])
    gq_d = din("g_q", [HD])
    gk_d = din("g_k", [HD])
    gfin_d = din("g_final", [D])
    win_d = din("w_in", [D, 4096])
    wap_d = din("w_ap", [D, D])
    wfp_d = din("w_fp", [512, D])
    wo_d = din("w_o", [D, D])
    wgu_d = din("w_gu", [D, 2 * DFF])
    wdn_d = din("w_dn", [DFF, D])
    ident_d = din("ident", [128, 128], BF16)
    ropec_d = din("rope_c", [128, NKT * 64])
    ropes_d = din("rope_s", [128, NKT * 64])
    cs_d = din("dft_cs", [128, 256], BF16)
    dftl_d = din("dft_l", [16, 128, 4096], BF16)
    out_d = nc.dram_tensor("out", [S, D], F32, kind="ExternalOutput").ap()
    taps = {}
    tap_list = []

    from contextlib import ExitStack
    es = ExitStack()
    SB = es.enter_context(nc.sbuf_tensor("SB", [128, SB_BYTES], U8))
    PS = es.enter_context(nc.psum_tensor("PS", [128, 4096], F32))
    sems = {n: es.enter_context(nc.semaphore("s_" + n)) for n in ("pe", "act", "dve", "pool", "sp")}
    dma_sems = {p: [es.enter_context(nc.semaphore("d_%s%d" % (p, i))) for i in range(8)] for p in ("sp", "pool")}

    esz = {F32: 4, BF16: 2, U8: 1}

    def sb(off, dt, shape, p0=0, p1=128):
        n = int(np.prod(shape))
        nb = n * esz[dt]
        assert off + nb <= SB_BYTES, (off, nb)
        ap = SB[p0:p1, off:off + nb].bitcast(dt)
        if len(shape) == 2:
            ap = ap.rearrange("p (a b) -> p a b", a=shape[0])
        elif len(shape) == 3:
            ap = ap.rearrange("p (a b c) -> p a b c", a=shape[0], b=shape[1])
        elif len(shape) == 4:
            ap = ap.rearrange("p (a b c d) -> p a b c d", a=shape[0], b=shape[1], c=shape[2])
        return View(ap, "sb", off, off + nb)

    def bank(b, nb=1):
        return View(PS[:, b * 512:(b + nb) * 512], "ps", b * 2048, (b + nb) * 2048)

    def bank_bf(b):
        return View(PS[:, b * 512:(b + 1) * 512].bitcast(BF16), "ps", b * 2048, (b + 1) * 2048)

    def sub(v, ap):
        return View(ap, v.sp, v.lo, v.hi)

    def rng(v, ap, elo, ehi, es_):
        return View(ap, v.sp, v.lo + elo * es_, v.lo + ehi * es_)

    def mm(out, lhsT, rhs, start, stop):
        return K.op("pe", lambda h: h.matmul(out.ap, lhsT=lhsT.ap, rhs=rhs.ap, start=start, stop=stop),
                    reads=[lhsT, rhs], writes=[out])

    def tr(out, in_, ident):
        return K.op("pe", lambda h: h.transpose(out=out.ap, in_=in_.ap, identity=ident.ap),
                    reads=[in_, ident], writes=[out])

    def act(out, in_, func, scale=1.0, bias=None, accum=None):
        reads = [in_]
        kw = {}
        if isinstance(scale, View):
            reads.append(scale)
            kw["scale"] = scale.ap
        else:
            kw["scale"] = float(scale)
        if isinstance(bias, View):
            reads.append(bias)
            kw["bias"] = bias.ap
        elif bias is not None:
            kw["bias"] = float(bias)
        writes = [out]
        if accum is not None:
            writes.append(accum)
            kw["accum_out"] = accum.ap
        return K.op("act", lambda h: h.activation(out=out.ap, in_=in_.ap, func=func, **kw),
                    reads=reads, writes=writes)

    def tt(eng, out, a, b, op):
        return K.op(eng, lambda h: h.tensor_tensor(out=out.ap, in0=a.ap, in1=b.ap, op=op),
                    reads=[a, b], writes=[out])

    def ts(eng, out, a, s1, op0, s2=None, op1=None):
        reads = [a]
        a1 = s1.ap if isinstance(s1, View) else float(s1)
        if isinstance(s1, View):
            reads.append(s1)
        a2 = None
        if s2 is not None:
            a2 = s2.ap if isinstance(s2, View) else float(s2)
            if isinstance(s2, View):
                reads.append(s2)
        if op1 is None:
            return K.op(eng, lambda h: h.tensor_scalar(out=out.ap, in0=a.ap, scalar1=a1, scalar2=None, op0=op0),
                        reads=reads, writes=[out])
        return K.op(eng, lambda h: h.tensor_scalar(out=out.ap, in0=a.ap, scalar1=a1, scalar2=a2, op0=op0, op1=op1),
                    reads=reads, writes=[out])

    def stt(eng, out, a, s, b, op0, op1):
        reads = [a, b]
        sa = s.ap if isinstance(s, View) else float(s)
        if isinstance(s, View):
            reads.append(s)
        return K.op(eng, lambda h: h.scalar_tensor_tensor(out=out.ap, in0=a.ap, scalar=sa, in1=b.ap, op0=op0, op1=op1),
                    reads=reads, writes=[out])

    def cp(eng, out, in_):
        if eng == "act":
            return K.op("act", lambda h: h.copy(out=out.ap, in_=in_.ap), reads=[in_], writes=[out])
        return K.op(eng, lambda h: h.tensor_copy(out=out.ap, in_=in_.ap), reads=[in_], writes=[out])

    def memset(eng, out, val):
        return K.op(eng, lambda h: h.memset(out.ap, val), writes=[out])

    def recip(out, in_):
        return K.op("dve", lambda h: h.reciprocal(out=out.ap, in_=in_.ap), reads=[in_], writes=[out])

    def reduce_sum(out, in_):
        return K.op("dve", lambda h: h.tensor_reduce(out=out.ap, in_=in_.ap, axis=AX.X, op=ALU.add),
                    reads=[in_], writes=[out])

    def load(dst, src_ap, q="sp"):
        return K.dma(q, q, dst.ap, src_ap, writes=[dst])

    def loadw(dst, src_ap):
        return K.dma("pool", "pool", dst.ap, src_ap, writes=[dst])

    def store(dst_ap, src, is_output=True):
        return K.dma("sp", "sp", dst_ap, src.ap, reads=[src], is_output=is_output)

    def tap(name, view, dt):
        shp = [int(x) for x in view.ap.shape]
        d = nc.dram_tensor("tap_" + name, shp, dt, kind="ExternalOutput").ap()
        K.dma("sp", "sp", d, view.ap, reads=[view], is_output=True)
        tap_list.append("tap_" + name)

    def rstd_from_ms(rstd, ms, tmp, mhalf):
        ts("pool", tmp, ms, EPS, ALU.add)
        tt("pool", rstd, tmp, mhalf, ALU.pow)

    KB = 1024
    HT_O = 0
    AT_O = 32 * KB
    KV_O = 76 * KB
    FY_O = 121 * KB
    QM_O = 137 * KB
    SM_O = 169 * KB
    assert SM_O + 31 * KB <= SB_BYTES

    c_o = SM_O
    ident = sb(c_o, BF16, (128,)); c_o += 256
    cond = sb(c_o, F32, (8, 2)); c_o += 64
    scond = sb(c_o, BF16, (8, 2)); c_o += 32
    badac = sb(c_o, F32, (48,)); c_o += 192
    gcols = sb(c_o, F32, (32,)); c_o += 128
    modraw = sb(c_o, F32, (4, 8, 2)); c_o += 256
    modc = sb(c_o, F32, (6, 8)); c_o += 192
    stats = sb(c_o, F32, (64,)); c_o += 256
    mhalf = sb(c_o, F32, (16,)); c_o += 64
    gq_bc = sb(c_o, F32, (64,)); c_o += 256
    gk_bc = sb(c_o, F32, (64,)); c_o += 256
    cs_t = sb(c_o, BF16, (256,)); c_o += 512
    scond_bc = sb(c_o, BF16, (8, 128)); c_o += 2048
    GTm = sb(c_o, F32, (1024,)); c_o += 4096
    GTf = sb(c_o, F32, (1024,)); c_o += 4096
    SMF = c_o
    SM_END = SM_O + 31 * KB

    load(ident, ident_d)
    load(cond, cond_d.rearrange("p (a b) -> p a b", a=8))
    load(badac, badac_d)
    load(gcols, gcols_d)
    load(gq_bc, gq_d.partition_broadcast(128))
    load(gk_bc, gk_d.partition_broadcast(128))
    load(cs_t, cs_d)
    memset("dve", mhalf, -0.5)
    act(scond, cond, AF.Silu)
    cp("dve", scond_bc, sub(scond, scond.ap[:, :, 0:1].to_broadcast([128, 8, 128])))

    wada_v = wada_d.rearrange("(k p) n -> p k n", p=128)
    ada_slots = [sb(QM_O + i * 8 * KB, BF16, (8, 512)) for i in range(4)]
    slot_i = [0]

    def ada_piece(col0):
        s = ada_slots[slot_i[0] % 4]
        slot_i[0] += 1
        loadw(s, wada_v[:, :, col0:col0 + 512])
        return s

    pcol = bank(0)
    pcol_v = sub(pcol, pcol.ap[:, 0:64].rearrange("p (v c n) -> p v c n", v=4, c=8))
    for vi, v in enumerate((0, 1, 3, 4)):
        for half in range(2):
            s = ada_piece(v * 1024 + half * 512)
            for cc in range(4):
                c = half * 4 + cc
                for k in range(8):
                    mm(sub(pcol, pcol_v.ap[:, vi, c, :]),
                       sub(s, s.ap[:, k, cc * 128:(cc + 1) * 128]),
                       sub(scond, scond.ap[:, k, :]), k == 0, k == 7)
    for vi, v in enumerate((0, 1, 3, 4)):
        tt("dve", sub(modraw, modraw.ap[:, vi, :, :]), sub(pcol, pcol_v.ap[:, vi, :, :]),
           sub(badac, badac.ap[:, v * 8:(v + 1) * 8].unsqueeze(2).to_broadcast([128, 8, 2])), ALU.add)
    gmix = sub(gcols, gcols.ap[:, 0:8])
    gffn = sub(gcols, gcols.ap[:, 8:16])
    bgate = sub(gcols, gcols.ap[:, 16:32])
    stt("dve", sub(modc, modc.ap[:, 0, :]), sub(modraw, modraw.ap[:, 1, :, 0]), 1.0, gmix, ALU.add, ALU.mult)
    cp("dve", sub(modc, modc.ap[:, 1, :]), sub(modraw, modraw.ap[:, 0, :, 0]))
    stt("dve", sub(modc, modc.ap[:, 2, :]), sub(modraw, modraw.ap[:, 1, :, 1]), 1.0, gmix, ALU.add, ALU.mult)
    cp("dve", sub(modc, modc.ap[:, 3, :]), sub(modraw, modraw.ap[:, 0, :, 1]))
    stt("dve", sub(modc, modc.ap[:, 4, :]), sub(modraw, modraw.ap[:, 3, :, 0]), 1.0, gffn, ALU.add, ALU.mult)
    cp("dve", sub(modc, modc.ap[:, 5, :]), sub(modraw, modraw.ap[:, 2, :, 0]))

    for gt, v in ((GTm, 2), (GTf, 5)):
        load(gt, bada_d[v * 1024:(v + 1) * 1024].partition_broadcast(128))
        for half in range(2):
            s = ada_piece(v * 1024 + half * 512)
            pb = bank(1 + half)
            for k in range(8):
                mm(pb, sub(scond_bc, scond_bc.ap[:, k, :]), sub(s, s.ap[:, k, :]), k == 0, k == 7)
            g_half = rng(gt, gt.ap[:, half * 512:(half + 1) * 512], half * 512, half * 512 + 512, 4)
            tt("dve", g_half, pb, g_half, ALU.add)

    if debug:
        tap("modc", modc, F32)
        tap("GTm", GTm, F32)

    hT = sb(HT_O, BF16, (16, 8, 128))
    qT = sb(QM_O, BF16, (16, 8, 128))
    kT2 = sb(KV_O, BF16, (4, NKT * 128))
    Vaug = sb(KV_O + 18 * KB, BF16, (NKT, 4, 192))
    hcT = sb(QM_O + 28 * KB, BF16, (2, 8, 128))
    Wq0 = sb(AT_O, BF16, (8, 512))
    Wq1 = sb(AT_O + 8 * KB, BF16, (8, 512))
    Wkv = sb(AT_O + 16 * KB, BF16, (8, 512))
    Wf = sb(AT_O + 24 * KB, BF16, (8, 512))
    ropec = sb(AT_O + 32 * KB, F32, (NKT, 64))
    ropes = sb(AT_O + 32 * KB + 4608, F32, (NKT, 64))
    win_v = win_d.rearrange("(k p) n -> p k n", p=128)
    loadw(Wkv, win_v[:, :, 1024:1536])
    loadw(Wq0, win_v[:, :, 0:512])
    loadw(Wq1, win_v[:, :, 512:1024])
    loadw(Wf, win_v[:, :, 1536:2048])
    load(ropec, ropec_d.rearrange("p (a b) -> p a b", a=NKT))
    load(ropes, ropes_d.rearrange("p (a b) -> p a b", a=NKT))
    memset("dve", Vaug, 1.0)

    o = SMF
    xbuf = [sb(o + i * 4096, F32, (1024,)) for i in range(2)]; o += 8192
    junk = sb(o, BF16, (1024,)); o += 2048
    xn = [sb(o + i * 2048, BF16, (1024,)) for i in range(2)]; o += 4096
    kr2 = [sb(o + i * 1024, BF16, (4, 2, 64)) for i in range(2)]; o += 2048
    tabs2 = [sb(o + i * 1024, F32, (4, 64)) for i in range(2)]; o += 2048
    assert o <= SM_END, o
    o = FY_O
    sq = sb(o, F32, (8, 64)); o += 2048
    qn = sb(o, F32, (8, 64)); o += 2048
    uu = sb(o, F32, (8, 64)); o += 2048
    ww = sb(o, F32, (8, 64)); o += 2048
    qr = [sb(o + i * 2048, BF16, (16, 64)) for i in range(2)]; o += 4096
    assert o <= FY_O + 16 * KB

    st_ms = lambda i: sub(stats, stats.ap[:, i:i + 1])

    def hsl(tt_i, k):
        if tt_i < NCT:
            return rng(hcT, hcT.ap[:, tt_i, k, :], (tt_i * 8 + k) * 128, (tt_i * 8 + k + 1) * 128, 2)
        t = tt_i - NCT
        return rng(hT, hT.ap[:, t, k, :], (t * 8 + k) * 128, (t * 8 + k + 1) * 128, 2)

    def P1(tt_i):
        is_ctx = tt_i < NCT
        xb = xbuf[tt_i % 2]
        src = ctx_d[tt_i * 128:(tt_i + 1) * 128, :] if is_ctx else x_d[(tt_i - NCT) * 128:(tt_i - NCT + 1) * 128, :]
        load(xb, src)
        sc = (tt_i % 2) * 3
        act(junk, xb, AF.Square, scale=1.0 / 32.0, accum=st_ms(sc))
        rstd_from_ms(st_ms(sc + 2), st_ms(sc), st_ms(sc + 1), sub(mhalf, mhalf.ap[:, 0:1]))
        xnb = xn[tt_i % 2]
        ts("dve", xnb, xb, st_ms(sc + 2), ALU.mult)
        pT = bank_bf(6)
        for k in range(8):
            tr(sub(pT, pT.ap[:, k * 128:(k + 1) * 128]), sub(xnb, xnb.ap[:, k * 128:(k + 1) * 128]), ident)
        gi, si = (2, 3) if is_ctx else (0, 1)
        for k in range(8):
            act(hsl(tt_i, k), sub(pT, pT.ap[:, k * 128:(k + 1) * 128]), AF.Identity,
                scale=sub(modc, modc.ap[:, gi, k:k + 1]), bias=sub(modc, modc.ap[:, si, k:k + 1]))

    def MM(tt_i):
        base = (tt_i % 2) * 3
        pkv = bank(base)
        for k in range(8):
            mm(pkv, hsl(tt_i, k), sub(Wkv, Wkv.ap[:, k, :]), k == 0, k == 7)
        if tt_i < NCT:
            return
        for qb, Wq in enumerate((Wq0, Wq1)):
            pq = bank(base + 1 + qb)
            for k in range(8):
                mm(pq, hsl(tt_i, k), sub(Wq, Wq.ap[:, k, :]), k == 0, k == 7)

    blk_i = [0]

    def make_block(ps_v, nh, tab_c, tab_s, tabv, dsts):
        par = blk_i[0] % 2
        blk_i[0] += 1
        b0 = 8 + par * 24
        p3 = sub(ps_v, ps_v.ap.rearrange("p (h d) -> p h d", h=nh))
        ss = sub(stats, stats.ap[:, b0:b0 + nh])
        tmp = sub(stats, stats.ap[:, b0 + 8:b0 + 8 + nh])
        rs = sub(stats, stats.ap[:, b0 + 16:b0 + 16 + nh])

        def X():
            sq3 = sub(sq, sq.ap[:, 0:nh, :])
            act(sq3, p3, AF.Square, scale=0.125)
            reduce_sum(ss, sq3)
            rstd_from_ms(rs, ss, tmp, sub(mhalf, mhalf.ap[:, 0:nh]))

        def Y():
            qn3 = sub(qn, qn.ap[:, 0:nh, :])
            tt("dve", qn3, p3, sub(rs, rs.ap.unsqueeze(2).to_broadcast([128, nh, 64])), ALU.mult)
            u3 = sub(uu, uu.ap[:, 0:nh, :])
            w3 = sub(ww, ww.ap[:, 0:nh, :])
            tt("dve", u3, qn3, sub(tabv, tab_c.unsqueeze(1).to_broadcast([128, nh, 64])), ALU.mult)
            tt("dve", w3, qn3, sub(tabv, tab_s.unsqueeze(1).to_broadcast([128, nh, 64])), ALU.mult)
            for d in dsts:
                tt("dve", sub(d, d.ap[:, :, 0:32]), sub(uu, u3.ap[:, :, 0:32]), sub(ww, w3.ap[:, :, 32:64]), ALU.subtract)
                tt("dve", sub(d, d.ap[:, :, 32:64]), sub(uu, u3.ap[:, :, 32:64]), sub(ww, w3.ap[:, :, 0:32]), ALU.add)
        return X, Y

    pending = []

    def run_block(X, Y, after=None):
        X()
        if pending:
            py, pafter = pending.pop()
            py()
            if pafter:
                pafter()
        pending.append((Y, after))

    def POST_a(tt_i):
        is_ctx = tt_i < NCT
        t = tt_i - NCT
        base = (tt_i % 2) * 3
        pkv = bank(base)
        tabv = tabs2[tt_i % 2]
        krb = kr2[tt_i % 2]
        rc = sub(ropec, ropec.ap[:, tt_i, :])
        rs_ = sub(ropes, ropes.ap[:, tt_i, :])
        if not is_ctx:
            stt("dve", sub(tabv, tabv.ap[:, 0, :]), rc, 0.125, gq_bc, ALU.mult, ALU.mult)
            stt("dve", sub(tabv, tabv.ap[:, 1, :]), rs_, 0.125, gq_bc, ALU.mult, ALU.mult)
        tt("dve", sub(tabv, tabv.ap[:, 2, :]), rc, gk_bc, ALU.mult)
        tt("dve", sub(tabv, tabv.ap[:, 3, :]), rs_, gk_bc, ALU.mult)
        kr_a = sub(krb, krb.ap[:, :, 0, :])
        kr_b = sub(krb, krb.ap[:, :, 1, :])
        Xk, Yk = make_block(sub(pkv, pkv.ap[:, 0:256]), 4, tabv.ap[:, 2, :], tabv.ap[:, 3, :], tabv, [kr_a, kr_b])

        def Xk2():
            Xk()
            vdst = rng(Vaug, Vaug.ap[:, tt_i, :, 64:128], tt_i * 768, (tt_i + 1) * 768, 2)
            cp("act", vdst, sub(pkv, pkv.ap[:, 256:512].rearrange("p (h d) -> p h d", h=4)))

        def Fk():
            pk = bank_bf(7)
            for kh in range(4):
                tr(sub(pk, pk.ap[:, kh * 128:(kh + 1) * 128]),
                   sub(krb, krb.ap[:, kh, :, :].rearrange("p a d -> p (a d)")), ident)
            kdst = sub(kT2, kT2.ap[:, :, tt_i * 128:(tt_i + 1) * 128])
            cp("act", kdst, sub(pk, pk.ap[:, 0:512].rearrange("p (h t) -> p h t", h=4)))

        run_block(Xk2, Yk, Fk)

    def POST_b(tt_i):
        if tt_i < NCT:
            return
        t = tt_i - NCT
        base = (tt_i % 2) * 3
        tabv = tabs2[tt_i % 2]
        qrb = qr[t % 2]

        def Fq():
            pqt = bank_bf(7)
            for c in range(8):
                tr(sub(pqt, pqt.ap[:, c * 128:(c + 1) * 128]),
                   sub(qrb, qrb.ap[:, 2 * c:2 * c + 2, :].rearrange("p a d -> p (a d)")), ident)
            qdst = rng(qT, qT.ap[:, t, :, :].rearrange("p c t -> p (c t)"), t * 1024, (t + 1) * 1024, 2)
            cp("act", qdst, pqt)

        for qb in range(2):
            pq = bank(base + 1 + qb)
            qd = rng(qrb, qrb.ap[:, qb * 8:(qb + 1) * 8, :], qb * 512, (qb + 1) * 512, 2)
            Xq, Yq = make_block(pq, 8, tabv.ap[:, 0, :], tabv.ap[:, 1, :], tabv, [qd])
            run_block(Xq, Yq, Fq if qb == 1 else None)

    for s_ in range(NKT + 2):
        if s_ < NKT:
            P1(s_)
        if s_ >= 2:
            POST_a(s_ - 2)
        if 1 <= s_ <= NKT:
            MM(s_ - 1)
        if s_ >= 2:
            POST_b(s_ - 2)
    while pending:
        py, pafter = pending.pop()
        py()
        if pafter:
            pafter()
    pb_i = [0]

    def next_bank():
        b = pb_i[0] % 8
        pb_i[0] += 1
        return b

    if debug:
        tap("hT", hT, BF16)
        tap("qT", qT, BF16)
        tap("kT2", kT2, BF16)
        tap("Vaug", Vaug, BF16)

    FT = sb(FY_O, BF16, (4, 2048))
    for fg in range(4):
        for tb in range(4):
            b = next_bank()
            pf = bank(b)
            rhs_all = lambda k: rng(hT, hT.ap[:, tb * 4:(tb + 1) * 4, k, :], tb * 4096, (tb + 1) * 4096, 2)
            for k in range(8):
                mm(pf, sub(Wf, Wf.ap[:, k, fg * 128:(fg + 1) * 128]), rhs_all(k), k == 0, k == 7)
            fdst = rng(FT, FT.ap[:, fg, tb * 512:(tb + 1) * 512], fg * 2048 + tb * 512, fg * 2048 + (tb + 1) * 512, 2)
            cp("act" if (fg + tb) % 2 else "dve", fdst, pf)
    if debug:
        tap("FT", FT, BF16)

    attnT = sb(AT_O, BF16, (8, 2048))
    PT = [sb(SMF + i * 2048, BF16, (1024,)) for i in range(3)]
    rden = [sb(SMF + 6144 + i * 4096, F32, (1024,)) for i in range(2)]
    assert SMF + 6144 + 8192 <= SM_END
    steps = [(c, qb, kt) for c in range(NH // 2) for qb in range(4) for kt in range(NKT)]

    def s_bank(i):
        return bank((i % 2) * 2, 2)

    def acc_bank(j):
        return bank(4 + (j % 2) * 2, 2)

    def emit_qk(i):
        c, qb, kt = steps[i]
        kh = c // 2
        sbk = s_bank(i)
        t0 = qb * 4
        for half in range(2):
            p0, p1 = half * 64, half * 64 + 64
            lhsT = View(kT2.ap[p0:p1, kh, kt * 128:(kt + 1) * 128], "sb",
                        kT2.lo + (kh * NKT * 128 + kt * 128) * 2, kT2.lo + (kh * NKT * 128 + kt * 128 + 128) * 2)
            rhs = View(qT.ap[p0:p1, t0:t0 + 4, c, :], "sb", qT.lo + t0 * 2048, qT.lo + (t0 + 4) * 2048)
            mm(View(sbk.ap[:, half * 512:(half + 1) * 512], "ps", sbk.lo + half * 2048, sbk.lo + (half + 1) * 2048),
               lhsT, rhs, True, True)

    emit_qk(0)
    for i in range(len(steps)):
        c, qb, kt = steps[i]
        kh = c // 2
        if i + 1 < len(steps):
            emit_qk(i + 1)
        pt = PT[i % 3]
        act(pt, s_bank(i), AF.Exp)
        j = i // NKT
        acc = acc_bank(j)
        for half in range(2):
            c0 = 64 if half == 0 else 0
            va = View(Vaug.ap[:, kt, kh, c0:c0 + 128], "sb", Vaug.lo + (kt * 4 + kh) * 384, Vaug.lo + (kt * 4 + kh + 1) * 384)
            mm(View(acc.ap[:, half * 512:(half + 1) * 512], "ps", acc.lo + half * 2048, acc.lo + (half + 1) * 2048),
               va, sub(pt, pt.ap[:, half * 512:(half + 1) * 512]), kt == 0, kt == NKT - 1)
        if kt == NKT - 1:
            rd = rden[j % 2]
            for half in range(2):
                ab = View(acc.ap[:, half * 512:(half + 1) * 512], "ps", acc.lo + half * 2048, acc.lo + (half + 1) * 2048)
                n0, d0 = (0, 64) if half == 0 else (64, 0)
                num = View(ab.ap[n0:n0 + 64, :], "ps", ab.lo, ab.hi)
                den = View(ab.ap[d0:d0 + 64, :], "ps", ab.lo, ab.hi)
                rdv = View(rd.ap[n0:n0 + 64, half * 512:(half + 1) * 512], "sb", rd.lo + half * 2048, rd.lo + (half + 1) * 2048)
                dst = View(attnT.ap[n0:n0 + 64, c, qb * 512:(qb + 1) * 512], "sb",
                           attnT.lo + (c * 2048 + qb * 512) * 2, attnT.lo + (c * 2048 + qb * 512 + 512) * 2)
                recip(rdv, den)
                tt("dve", dst, num, rdv, ALU.mult)
    if debug:
        tap("attnT", attnT, BF16)

    G = sb(QM_O, BF16, (16, 4, 256))
    YT = sb(FY_O, BF16, (4, 2048))
    for lt in range(NT):
        for pr in range(2):
            b = next_bank()
            pg = bank(b)
            for a in range(2):
                fg = pr * 2 + a
                mm(View(pg.ap[:, a * 256:(a + 1) * 256], "ps", pg.lo + a * 1024, pg.lo + (a + 1) * 1024),
                   rng(FT, FT.ap[:, fg, lt * 128:(lt + 1) * 128], fg * 2048 + lt * 128, fg * 2048 + lt * 128 + 128, 2),
                   cs_t, True, True)
            gd = rng(G, G.ap[:, lt, pr * 2:pr * 2 + 2, :].rearrange("p a c -> p (a c)"),
                     (lt * 4 + pr * 2) * 256, (lt * 4 + pr * 2 + 2) * 256, 2)
            cp("act" if (lt + pr) % 2 else "dve", gd, pg)
    dft_slots = [sb(KV_O + i * 8 * KB, BF16, (4, 2, 512)) for i in range(3)]
    di = 0
    for jb in range(4):
        banks = [bank((jb % 2) * 4 + fg) for fg in range(4)]
        for lg in range(4):
            ds = dft_slots[di % 3]
            load(ds, dftl_d[jb * 4 + lg].rearrange("p (a b c) -> p a b c", a=4, b=2))
            di += 1
            for fg in range(4):
                for a in range(4):
                    lt = lg * 4 + a
                    for cs_i in range(2):
                        mm(banks[fg],
                           rng(G, G.ap[:, lt, fg, cs_i * 128:(cs_i + 1) * 128],
                               (lt * 4 + fg) * 256 + cs_i * 128, (lt * 4 + fg) * 256 + cs_i * 128 + 128, 2),
                           sub(ds, ds.ap[:, a, cs_i, :]),
                           lt == 0 and cs_i == 0, lt == NT - 1 and cs_i == 1)
        for fg in range(4):
            yd = rng(YT, YT.ap[:, fg, jb * 512:(jb + 1) * 512], fg * 2048 + jb * 512, fg * 2048 + jb * 512 + 512, 2)
            cp("act" if fg % 2 else "dve", yd, banks[fg])
    if debug:
        tap("YT", YT, BF16)

    MT = sb(QM_O, BF16, (8, 2048))
    wslots = [sb(100 * KB + i * 6 * KB, BF16, (24, 128)) for i in range(2)]
    wfps = [sb(112 * KB + i * KB, BF16, (4, 128)) for i in range(2)]
    gsl = [sb(SMF + i * 2048, F32, (512,)) for i in range(4)]
    wap_v = wap_d.rearrange("(k p) n -> p k n", p=128)
    wfp_v = wfp_d.rearrange("(k p) n -> p k n", p=128)
    for j in range(8):
        wsl = wslots[j % 2]
        wf_ = wfps[j % 2]
        loadw(rng(wsl, wsl.ap[:, 0:8, :], 0, 1024, 2), wap_v[:, :, j * 128:(j + 1) * 128])
        loadw(rng(wsl, wsl.ap[:, 8:16, :], 1024, 2048, 2), win_v[:, :, 2048 + j * 128:2048 + (j + 1) * 128])
        loadw(rng(wsl, wsl.ap[:, 16:24, :], 2048, 3072, 2), win_v[:, :, 3072 + j * 128:3072 + (j + 1) * 128])
        loadw(wf_, wfp_v[:, :, j * 128:(j + 1) * 128])
        for tb in range(4):
            base = ((j * 4 + tb) % 2) * 4
            pA, pF, pGa, pGf = bank(base), bank(base + 1), bank(base + 2), bank(base + 3)
            hsl4 = lambda k: rng(hT, hT.ap[:, tb * 4:(tb + 1) * 4, k, :], tb * 4096, (tb + 1) * 4096, 2)
            for k in range(8):
                mm(pGa, rng(wsl, wsl.ap[:, 8 + k, :], (8 + k) * 128, (9 + k) * 128, 2), hsl4(k), k == 0, k == 7)
            for k in range(8):
                mm(pGf, rng(wsl, wsl.ap[:, 16 + k, :], (16 + k) * 128, (17 + k) * 128, 2), hsl4(k), k == 0, k == 7)
            for k in range(8):
                mm(pA, rng(wsl, wsl.ap[:, k, :], k * 128, (k + 1) * 128, 2),
                   rng(attnT, attnT.ap[:, k, tb * 512:(tb + 1) * 512], k * 2048 + tb * 512, k * 2048 + tb * 512 + 512, 2),
                   k == 0, k == 7)
            for k in range(4):
                mm(pF, sub(wf_, wf_.ap[:, k, :]),
                   rng(YT, YT.ap[:, k, tb * 512:(tb + 1) * 512], k * 2048 + tb * 512, k * 2048 + tb * 512 + 512, 2),
                   k == 0, k == 3)
            act(gsl[0], pGa, AF.Sigmoid, bias=sub(bgate, bgate.ap[:, j:j + 1]))
            act(gsl[1], pGf, AF.Sigmoid, bias=sub(bgate, bgate.ap[:, 8 + j:9 + j]))
            tt("dve", gsl[2], pA, gsl[0], ALU.mult)
            tt("dve", gsl[3], pF, gsl[1], ALU.mult)
            md = rng(MT, MT.ap[:, j, tb * 512:(tb + 1) * 512], j * 2048 + tb * 512, j * 2048 + tb * 512 + 512, 2)
            tt("dve", md, gsl[2], gsl[3], ALU.add)
    if debug:
        tap("MT", MT, BF16)

    X1 = sb(AT_O, F32, (16, 1024))
    Wo = sb(FY_O, BF16, (8, 1024))
    h2T = sb(HT_O, BF16, (16, 8, 128))
    wo_v = wo_d.rearrange("(k p) n -> p k n", p=128)
    loadw(rng(Wo, Wo.ap[:, 0:4, :], 0, 4096, 2), wo_v[:, 0:4, :])
    loadw(rng(Wo, Wo.ap[:, 4:8, :], 4096, 8192, 2), wo_v[:, 4:8, :])
    o = SMF
    xbuf = [sb(o + i * 4096, F32, (1024,)) for i in range(2)]; o += 8192
    junk = sb(o, BF16, (1024,)); o += 2048
    xn = [sb(o + i * 2048, BF16, (1024,)) for i in range(2)]; o += 4096
    tmpf = sb(o, F32, (1024,)); o += 4096
    assert o <= SM_END
    for t in range(NT):
        xb = xbuf[t % 2]
        load(xb, x_d[t * 128:(t + 1) * 128, :])
        x1t = rng(X1, X1.ap[:, t, :], t * 1024, (t + 1) * 1024, 4)
        for nb in range(2):
            b = next_bank()
            po = bank(b)
            for j in range(8):
                mm(po, rng(MT, MT.ap[:, j, t * 128:(t + 1) * 128], j * 2048 + t * 128, j * 2048 + t * 128 + 128, 2),
                   rng(Wo, Wo.ap[:, j, nb * 512:(nb + 1) * 512], j * 1024 + nb * 512, j * 1024 + nb * 512 + 512, 2),
                   j == 0, j == 7)
            tmh = rng(tmpf, tmpf.ap[:, nb * 512:(nb + 1) * 512], nb * 512, nb * 512 + 512, 4)
            tt("dve", tmh, po, rng(GTm, GTm.ap[:, nb * 512:(nb + 1) * 512], nb * 512, nb * 512 + 512, 4), ALU.mult)
            tt("dve", rng(X1, X1.ap[:, t, nb * 512:(nb + 1) * 512], t * 1024 + nb * 512, t * 1024 + nb * 512 + 512, 4),
               tmh, rng(xb, xb.ap[:, nb * 512:(nb + 1) * 512], nb * 512, nb * 512 + 512, 4), ALU.add)
        sc = (t % 2) * 3
        act(junk, x1t, AF.Square, scale=1.0 / 32.0, accum=st_ms(sc))
        rstd_from_ms(st_ms(sc + 2), st_ms(sc), st_ms(sc + 1), sub(mhalf, mhalf.ap[:, 0:1]))
        xnb = xn[t % 2]
        ts("dve", xnb, x1t, st_ms(sc + 2), ALU.mult)
        bT = next_bank()
        pT = bank_bf(bT)
        for k in range(8):
            tr(sub(pT, pT.ap[:, k * 128:(k + 1) * 128]), sub(xnb, xnb.ap[:, k * 128:(k + 1) * 128]), ident)
        for k in range(8):
            dst = rng(h2T, h2T.ap[:, t, k, :], (t * 8 + k) * 128, (t * 8 + k + 1) * 128, 2)
            act(dst, sub(pT, pT.ap[:, k * 128:(k + 1) * 128]), AF.Identity,
                scale=sub(modc, modc.ap[:, 4, k:k + 1]), bias=sub(modc, modc.ap[:, 5, k:k + 1]))
    if debug:
        tap("X1", X1, F32)
        tap("h2T", h2T, BF16)

    Wd = sb(96 * KB, BF16, (NFF, 1024))
    actT = sb(140 * KB, BF16, (NFF, 512))
    gfin = sb(GTm.lo, F32, (1024,))
    junk2 = sb(162 * KB, BF16, (1024,))
    tm2 = [sb(164 * KB + i * 2048, F32, (512,)) for i in range(2)]
    wgu_slots = [sb(SMF + i * 8 * KB, BF16, (8, 2, 256)) for i in range(2)]
    sg = [sb(SMF + 16 * KB, F32, (512,)), sb(scond_bc.lo, F32, (512,))]
    assert SMF + 18 * KB <= SM_END
    wdn_v = wdn_d.rearrange("(k p) n -> p k n", p=128)
    wgu_v = wgu_d.rearrange("(k p) n -> p k n", p=128)
    load(gfin, gfin_d.partition_broadcast(128))
    pieces = [(tq, i) for tq in range(4) for i in range(NFF)]
    NG = len(pieces) // 2

    def load_group(gi):
        tq, i = pieces[gi * 2]
        sl = wgu_slots[gi % 2]
        loadw(sub(sl, sl.ap[:, :, 0, :]), wgu_v[:, :, i * 128:(i + 2) * 128])
        loadw(sub(sl, sl.ap[:, :, 1, :]), wgu_v[:, :, DFF + i * 128:DFF + (i + 2) * 128])

    load_group(0)
    load_group(1)
    for c4 in range(2):
        loadw(rng(Wd, Wd.ap[:, c4 * 11:(c4 + 1) * 11, :], c4 * 11 * 1024, (c4 + 1) * 11 * 1024, 2),
              wdn_v[:, c4 * 11:(c4 + 1) * 11, :])
    for pi, (tq, i) in enumerate(pieces):
        sl = wgu_slots[(pi // 2) % 2]
        o2 = (pi % 2) * 128
        pg_, pu_ = bank((pi % 2) * 2), bank((pi % 2) * 2 + 1)
        rhs4 = lambda k: rng(h2T, h2T.ap[:, tq * 4:(tq + 1) * 4, k, :], tq * 4096, (tq + 1) * 4096, 2)
        for k in range(8):
            mm(pg_, sub(sl, sl.ap[:, k, 0, o2:o2 + 128]), rhs4(k), k == 0, k == 7)
        for k in range(8):
            mm(pu_, sub(sl, sl.ap[:, k, 1, o2:o2 + 128]), rhs4(k), k == 0, k == 7)
        if pi % 2 == 1 and pi // 2 + 2 < NG:
            load_group(pi // 2 + 2)
        sgi = sg[pi % 2]
        act(sgi, pg_, AF.Silu)
        tt("dve", rng(actT, actT.ap[:, i, :], i * 512, (i + 1) * 512, 2), pu_, sgi, ALU.mult)
        if i != NFF - 1:
            continue
        for t4 in range(4):
            t = tq * 4 + t4
            x1t = rng(X1, X1.ap[:, t, :], t * 1024, (t + 1) * 1024, 4)
            for nb in range(2):
                pd = bank(4 + ((t4 * 2 + nb) % 4))
                for ii in range(NFF):
                    mm(pd, rng(actT, actT.ap[:, ii, t4 * 128:(t4 + 1) * 128], ii * 512 + t4 * 128, ii * 512 + t4 * 128 + 128, 2),
                       rng(Wd, Wd.ap[:, ii, nb * 512:(nb + 1) * 512], ii * 1024 + nb * 512, ii * 1024 + nb * 512 + 512, 2),
                       ii == 0, ii == NFF - 1)
                tmh = tm2[nb]
                tt("dve", tmh, pd, rng(GTf, GTf.ap[:, nb * 512:(nb + 1) * 512], nb * 512, nb * 512 + 512, 4), ALU.mult)
                x1h = rng(X1, X1.ap[:, t, nb * 512:(nb + 1) * 512], t * 1024 + nb * 512, t * 1024 + nb * 512 + 512, 4)
                tt("dve", x1h, tmh, x1h, ALU.add)
            sc = (t % 2) * 3
            act(junk2, x1t, AF.Square, scale=1.0 / 32.0, accum=st_ms(sc))
            rstd_from_ms(st_ms(sc + 2), st_ms(sc), st_ms(sc + 1), sub(mhalf, mhalf.ap[:, 0:1]))
            stt("dve", x1t, x1t, st_ms(sc + 2), gfin, ALU.mult, ALU.mult)
            store(out_d[t * 128:(t + 1) * 128, :], x1t)

    K.finish()

    with nc.Block() as block:
        @block.tensor
        def _(h):
            K.replay_one(nc, "pe", h, sems, dma_sems)

        @block.scalar
        def _(h):
            K.replay_one(nc, "act", h, sems, dma_sems)

        @block.vector
        def _(h):
            K.replay_one(nc, "dve", h, sems, dma_sems)

        @block.gpsimd
        def _(h):
            K.replay_one(nc, "pool", h, sems, dma_sems)

        @block.sync
        def _(h):
            K.replay_one(nc, "sp", h, sems, dma_sems)
    es.close()
    return nc, tap_list


_CONST = {}


def _consts():
    if _CONST:
        return _CONST
    bf = ml_dtypes.bfloat16
    _CONST["ident"] = np.eye(128, dtype=np.float32).astype(bf)
    inv = (10000.0 ** (-np.arange(0, 32, 2, dtype=np.float32) / np.float32(32))).astype(np.float32)
    tpos = np.arange(S)
    t_row = (tpos // 64).astype(np.float32)
    t_col = (tpos % 64).astype(np.float32)
    ang = np.concatenate([t_row[:, None] * inv[None, :], t_col[:, None] * inv[None, :]], axis=-1).astype(np.float32)
    cos = np.cos(ang).astype(np.float32)
    sin = np.sin(ang).astype(np.float32)
    rc = np.ones((NKT, 128, 64), np.float32)
    rs = np.zeros((NKT, 128, 64), np.float32)
    rc[NCT:] = np.concatenate([cos, cos], -1).reshape(NT, 128, 64)
    rs[NCT:] = np.concatenate([sin, sin], -1).reshape(NT, 128, 64)
    _CONST["rope_c"] = np.ascontiguousarray(rc.transpose(1, 0, 2).reshape(128, NKT * 64))
    _CONST["rope_s"] = np.ascontiguousarray(rs.transpose(1, 0, 2).reshape(128, NKT * 64))
    c = np.arange(128)
    a = 2.0 * np.pi * ((c[:, None] * c[None, :]) % 128) / 128.0
    _CONST["dft_cs"] = (np.concatenate([np.cos(a), np.sin(a)], 1) / 512.0).astype(np.float32).astype(bf)
    l = np.arange(S)
    m = (l[:, None] * l[None, :]) % S
    ang2 = 2.0 * np.pi * m / float(S)
    CL = np.cos(ang2).astype(np.float32).astype(bf)
    SL = (-np.sin(ang2)).astype(np.float32).astype(bf)
    both = np.stack([CL, SL], 0)
    both = both.reshape(2, 4, 4, 128, 4, 512)
    both = both.transpose(4, 1, 3, 2, 0, 5)
    _CONST["dft_l"] = np.ascontiguousarray(both).reshape(16, 128, 4096)
    return _CONST


def _col(v, n):
    return np.ascontiguousarray(np.asarray(v, np.float32).reshape(n, 128).T)


def _prep(inputs):
    f = lambda a: np.ascontiguousarray(np.asarray(a, dtype=np.float32))
    cst = _consts()
    shared = {
        "w_ada": f(inputs["w_ada"][0]),
        "b_ada_col": _col(inputs["b_ada"][0], 48),
        "b_ada": f(inputs["b_ada"][0]),
        "gcols": np.ascontiguousarray(np.concatenate([_col(inputs["g_norm_mix"][0], 8), _col(inputs["g_norm_ffn"][0], 8),
                                                      _col(inputs["b_gate"][0], 16)], axis=1)),
        "g_q": f(inputs["g_q"][0]),
        "g_k": f(inputs["g_k"][0]),
        "g_final": f(inputs["g_final"]),
        "w_in": f(inputs["w_in"][0]),
        "w_ap": f(inputs["w_attn_proj"][0]),
        "w_fp": f(inputs["w_fourier_proj"][0]),
        "w_o": f(inputs["w_o"][0]),
        "w_gu": f(inputs["w_gate_up"][0]),
        "w_dn": f(inputs["w_down"][0]),
    }
    shared.update(cst)
    x = f(inputs["x"])
    ctx = f(inputs["ctx"])
    c = f(inputs["c"])
    cc = _col(inputs["c_ctx"], 8)
    maps = []
    for b in range(8):
        m = dict(shared)
        m["x"] = x[b]
        m["ctx"] = ctx[b]
        cond = np.stack([_col(c[b], 8), cc], axis=-1)
        m["cond"] = np.ascontiguousarray(cond.reshape(128, 16))
        maps.append(m)
    return maps


_NC = {}


def kernel(**inputs):
    if "nc" not in _NC:
        _NC["nc"] = build(False)[0]
    maps = _prep(inputs)
    res = run_bass_kernel_spmd(_NC["nc"], maps, core_ids=list(range(8)))
    out = np.stack([np.asarray(r["out"], dtype=np.float32) for r in res.results], axis=0)
    return out
```

```python
import numpy as np
import ml_dtypes
import concourse.bass as bass
import concourse.mybir as mybir
from concourse.bass_utils import run_bass_kernel_spmd

F32 = mybir.dt.float32
BF16 = mybir.dt.bfloat16
U8 = mybir.dt.uint8
AF = mybir.ActivationFunctionType
ALU = mybir.AluOpType
AX = mybir.AxisListType

D = 1024
S = 2048
CTX = 256
NT = S // 128
NCT = CTX // 128
NKT = NT + NCT
HD = 64
NH = 16
NKV = 4
DFF = 2816
NFF = DFF // 128
EPS = 1e-6
SB_BYTES = 200 * 1024

DEBUG_TAPS = []


class View:
    __slots__ = ("ap", "sp", "lo", "hi")

    def __init__(self, ap, sp, lo, hi):
        self.ap, self.sp, self.lo, self.hi = ap, sp, lo, hi


class _Node:
    __slots__ = ("id", "eng", "kind", "fn", "cost", "preds", "succs", "npend", "ready", "start", "finish",
                 "pos", "inc", "num", "dma", "nbytes", "is_output", "tk")

    def __init__(self, id_, eng, kind, fn, cost):
        self.id, self.eng, self.kind, self.fn, self.cost = id_, eng, kind, fn, cost
        self.preds = {}
        self.succs = []
        self.npend = 0
        self.ready = 0.0
        self.start = self.finish = 0.0
        self.pos = -1
        self.inc = False
        self.num = None
        self.dma = None
        self.nbytes = 0
        self.is_output = False
        self.tk = None


class _Rec:
    __slots__ = ("lo", "hi", "kind", "node")

    def __init__(self, lo, hi, kind, node):
        self.lo, self.hi, self.kind, self.node = lo, hi, kind, node


class Tracker:
    ENGS = ("pe", "act", "dve", "pool", "sp")
    SYNC_NS = 250.0

    def __init__(self):
        self.nodes = []
        self.recs = {"sb": [], "ps": []}
        self.npools = {}
        self.order = {}

    def add_pool(self, name, k):
        self.npools[name] = k

    def _edges(self, n, reads, writes):
        is_dma = n.kind == "dma"
        for v in reads:
            for r in self.recs[v.sp]:
                if r.kind == "w" and r.lo < v.hi and v.lo < r.hi:
                    self._edge(r.node, n, is_dma)
        for v in writes:
            for r in self.recs[v.sp]:
                if r.lo < v.hi and v.lo < r.hi:
                    self._edge(r.node, n, is_dma)

    def _edge(self, p, n, is_dma):
        if p is n:
            return
        need = True
        if not is_dma and p.kind != "dma" and p.eng == n.eng and n.eng == "pe":
            need = False
        if p in n.preds:
            n.preds[p] = n.preds[p] or need
        else:
            n.preds[p] = need

    def _record(self, n, reads, writes):
        for v in writes:
            lst = self.recs[v.sp]
            lst[:] = [r for r in lst if not (v.lo <= r.lo and r.hi <= v.hi)]
            lst.append(_Rec(v.lo, v.hi, "w", n))
        for v in reads:
            self.recs[v.sp].append(_Rec(v.lo, v.hi, "r", n))

    def op(self, engname, fn, reads=(), writes=(), cost=300.0):
        n = _Node(len(self.nodes), engname, "op", fn, cost)
        self._edges(n, reads, writes)
        self._record(n, reads, writes)
        self.nodes.append(n)
        return n

    def dma(self, qname, pool, out_ap, in_ap, reads=(), writes=(), is_output=False, nbytes=0):
        n = _Node(len(self.nodes), qname, "dma", None, 1000.0 if qname == "pool" else 100.0)
        n.dma = (out_ap, in_ap, pool)
        n.nbytes = nbytes
        n.is_output = is_output
        self._edges(n, reads, writes)
        self._record(n, reads, writes)
        self.nodes.append(n)
        return n

    def schedule(self):
        import heapq
        for n in self.nodes:
            n.npend = len(n.preds)
            for p in n.preds:
                p.succs.append(n)
        free = {e: 0.0 for e in self.ENGS}
        wait_h = {e: [] for e in self.ENGS}
        run_h = {e: [] for e in self.ENGS}
        order = {e: [] for e in self.ENGS}
        dma_bw_free = [0.0]
        for n in self.nodes:
            if n.npend == 0:
                heapq.heappush(wait_h[n.eng], (0.0, n.id, n))
        remaining = len(self.nodes)
        while remaining:
            best = None
            for e in self.ENGS:
                wh, rh = wait_h[e], run_h[e]
                while wh and wh[0][0] <= free[e]:
                    _, i, nd = heapq.heappop(wh)
                    heapq.heappush(rh, (i, nd))
                if rh:
                    cand = (free[e], rh[0][0], e, True)
                elif wh:
                    cand = (wh[0][0], wh[0][1], e, False)
                else:
                    continue
                if best is None or cand[:2] < best[:2]:
                    best = cand
            st, _, e, from_run = best
            if from_run:
                _, n = heapq.heappop(run_h[e])
            else:
                _, _, n = heapq.heappop(wait_h[e])
            n.start = st
            free[e] = st + n.cost
            if n.kind == "dma":
                t_begin = max(st + 1500.0, dma_bw_free[0])
                t_end = t_begin + n.nbytes / 300.0
                dma_bw_free[0] = t_end
                n.finish = t_end + 500.0
            else:
                n.finish = st + n.cost
            n.pos = len(order[e])
            order[e].append(n)
            remaining -= 1
            for sc in n.succs:
                lat = self.SYNC_NS if (sc.preds[n] or sc.eng != n.eng) else 0.0
                if sc.eng == n.eng and n.kind != "dma":
                    lat = 60.0 if sc.preds[n] else 0.0
                r = n.finish + lat
                if r > sc.ready:
                    sc.ready = r
                sc.npend -= 1
                if sc.npend == 0:
                    heapq.heappush(wait_h[sc.eng], (sc.ready, sc.id, sc))
        self.order = order
        self.est_ns = max(n.finish for n in self.nodes)

    def emit_plan(self):
        plan = {e: [] for e in self.ENGS}
        pools = {p: dict(n=k, vals=[0] * k, last=[None] * k, rr=0) for p, k in self.npools.items()}
        for e in self.ENGS:
            for n in self.order[e]:
                if n.kind == "dma":
                    p = pools[n.dma[2]]
                    si = p["rr"]
                    p["rr"] = (si + 1) % p["n"]
                    prev = p["last"][si]
                    p["vals"][si] += 16
                    n.tk = (n.dma[2], si, p["vals"][si], prev)
                    p["last"][si] = n
        out_nodes = []
        for e in self.ENGS:
            seen = {}
            pl = plan[e]

            def wait_on(p):
                if p.kind == "dma":
                    pool, si, val, _ = p.tk
                    key = (pool, si)
                    if seen.get(key, -1) >= val:
                        return
                    seen[key] = val
                    pl.append(("wait_dma", pool, si, val))
                else:
                    if seen.get(p.eng, -1) >= p.pos:
                        return
                    seen[p.eng] = p.pos
                    p.inc = True
                    pl.append(("wait_op", p))

            for n in self.order[e]:
                for p, need in n.preds.items():
                    if need:
                        wait_on(p)
                if n.kind == "dma":
                    prev = n.tk[3]
                    if prev is not None:
                        wait_on(prev)
                    if n.is_output:
                        out_nodes.append(n)
                pl.append(("node", n))
        seen = {}
        for n in out_nodes:
            pool, si, val, _ = n.tk
            if seen.get((pool, si), -1) < val:
                seen[(pool, si)] = val
                plan["sp"].append(("wait_dma", pool, si, val))
        for e in self.ENGS:
            c = 0
            for n in self.order[e]:
                if n.kind == "op" and n.inc:
                    c += 1
                    n.num = c
        self.plan = plan

    def replay_one(self, nc, name, h, sems, dma_sems):
        for it in self.plan[name]:
            if it[0] == "wait_op":
                p = it[1]
                h.wait_ge(sems[p.eng], p.num)
            elif it[0] == "wait_dma":
                _, pool, si, val = it
                h.wait_ge(dma_sems[pool][si], val)
            else:
                n = it[1]
                if n.kind == "dma":
                    out_ap, in_ap, pool = n.dma
                    h.dma_start(out=out_ap, in_=in_ap).then_inc(dma_sems[pool][n.tk[1]], 16)
                else:
                    ins = n.fn(h)
                    if n.inc:
                        ins.then_inc(sems[n.eng], 1)


def build(debug=False):
    nc = bass.Bass("TRN2", target_bir_lowering=False)
    K = Tracker()
    K.add_pool("sp", 8)
    K.add_pool("pool", 8)

    def din(name, shape, dt=F32):
        return nc.dram_tensor(name, list(shape), dt, kind="ExternalInput").ap()

    x_d = din("x", [S, D])
    ctx_d = din("ctx", [CTX, D])
    cond_d = din("cond", [128, 16])
    wada_d = din("w_ada", [D, 6 * D])
    badac_d = din("b_ada_col", [128, 48])
    bada_d = din("b_ada", [6 * D])
    gcols_d = din("gcols", [128, 32])
    gq_d = din("g_q", [HD])
    gk_d = din("g_k", [HD])
    gfin_d = din("g_final", [D])
    win_d = din("w_in", [D, 4096])
    wap_d = din("w_ap", [D, D])
    wfp_d = din("w_fp", [512, D])
    wo_d = din("w_o", [D, D])
    wgu_d = din("w_gu", [D, 2 * DFF])
    wdn_d = din("w_dn", [DFF, D])
    ident_d = din("ident", [128, 128], BF16)
    ropec_d = din("rope_c", [128, NKT * 64])
    ropes_d = din("rope_s", [128, NKT * 64])
    cs_d = din("dft_cs", [128, 256], BF16)
    dftl_d = din("dft_l", [16, 128, 4096], BF16)
    out_d = nc.dram_tensor("out", [S, D], F32, kind="ExternalOutput").ap()
    taps = {}
    tap_list = []

    from contextlib import ExitStack
    es = ExitStack()
    SB = es.enter_context(nc.sbuf_tensor("SB", [128, SB_BYTES], U8))
    PS = es.enter_context(nc.psum_tensor("PS", [128, 4096], F32))
    sems = {n: es.enter_context(nc.semaphore("s_" + n)) for n in ("pe", "act", "dve", "pool", "sp")}
    dma_sems = {p: [es.enter_context(nc.semaphore("d_%s%d" % (p, i))) for i in range(8)] for p in ("sp", "pool")}

    esz = {F32: 4, BF16: 2, U8: 1}

    def sb(off, dt, shape, p0=0, p1=128):
        n = int(np.prod(shape))
        nb = n * esz[dt]
        assert off + nb <= SB_BYTES, (off, nb)
        ap = SB[p0:p1, off:off + nb].bitcast(dt)
        if len(shape) == 2:
            ap = ap.rearrange("p (a b) -> p a b", a=shape[0])
        elif len(shape) == 3:
            ap = ap.rearrange("p (a b c) -> p a b c", a=shape[0], b=shape[1])
        elif len(shape) == 4:
            ap = ap.rearrange("p (a b c d) -> p a b c d", a=shape[0], b=shape[1], c=shape[2])
        return View(ap, "sb", off, off + nb)

    def bank(b, nb=1):
        return View(PS[:, b * 512:(b + nb) * 512], "ps", b * 2048, (b + nb) * 2048)

    def bank_bf(b):
        return View(PS[:, b * 512:(b + 1) * 512].bitcast(BF16), "ps", b * 2048, (b + 1) * 2048)

    def sub(v, ap):
        return View(ap, v.sp, v.lo, v.hi)

    def rng(v, ap, elo, ehi, es_):
        return View(ap, v.sp, v.lo + elo * es_, v.lo + ehi * es_)

    def fsz(v):
        n = 1
        for d in v.ap.shape[1:]:
            n *= int(d)
        return n

    def ecost(eng, v, per=1.04):
        if eng == "act":
            return 224.0 + 0.83 * fsz(v)
        if eng == "pool":
            return 300.0 + 2.0 * fsz(v)
        return 120.0 + per * fsz(v)

    def mm(out, lhsT, rhs, start, stop):
        return K.op("pe", lambda h: h.matmul(out.ap, lhsT=lhsT.ap, rhs=rhs.ap, start=start, stop=stop),
                    reads=[lhsT, rhs], writes=[out], cost=12.0 + 0.42 * fsz(rhs))

    def tr(out, in_, ident):
        return K.op("pe", lambda h: h.transpose(out=out.ap, in_=in_.ap, identity=ident.ap),
                    reads=[in_, ident], writes=[out], cost=66.0)

    def act(out, in_, func, scale=1.0, bias=None, accum=None):
        reads = [in_]
        kw = {}
        if isinstance(scale, View):
            reads.append(scale)
            kw["scale"] = scale.ap
        else:
            kw["scale"] = float(scale)
        if isinstance(bias, View):
            reads.append(bias)
            kw["bias"] = bias.ap
        elif bias is not None:
            kw["bias"] = float(bias)
        writes = [out]
        if accum is not None:
            writes.append(accum)
            kw["accum_out"] = accum.ap
        return K.op("act", lambda h: h.activation(out=out.ap, in_=in_.ap, func=func, **kw),
                    reads=reads, writes=writes, cost=ecost("act", in_))

    def tt(eng, out, a, b, op):
        return K.op(eng, lambda h: h.tensor_tensor(out=out.ap, in0=a.ap, in1=b.ap, op=op),
                    reads=[a, b], writes=[out], cost=(1600.0 if op == ALU.pow else ecost(eng, out)))

    def ts(eng, out, a, s1, op0, s2=None, op1=None):
        reads = [a]
        a1 = s1.ap if isinstance(s1, View) else float(s1)
        if isinstance(s1, View):
            reads.append(s1)
        a2 = None
        if s2 is not None:
            a2 = s2.ap if isinstance(s2, View) else float(s2)
            if isinstance(s2, View):
                reads.append(s2)
        if op1 is None:
            return K.op(eng, lambda h: h.tensor_scalar(out=out.ap, in0=a.ap, scalar1=a1, scalar2=None, op0=op0),
                        reads=reads, writes=[out], cost=ecost(eng, out, 0.6))
        return K.op(eng, lambda h: h.tensor_scalar(out=out.ap, in0=a.ap, scalar1=a1, scalar2=a2, op0=op0, op1=op1),
                    reads=reads, writes=[out], cost=ecost(eng, out, 0.6))

    def stt(eng, out, a, s, b, op0, op1):
        reads = [a, b]
        sa = s.ap if isinstance(s, View) else float(s)
        if isinstance(s, View):
            reads.append(s)
        return K.op(eng, lambda h: h.scalar_tensor_tensor(out=out.ap, in0=a.ap, scalar=sa, in1=b.ap, op0=op0, op1=op1),
                    reads=reads, writes=[out], cost=ecost(eng, out))

    def cp(eng, out, in_):
        if eng == "act":
            return K.op("act", lambda h: h.copy(out=out.ap, in_=in_.ap), reads=[in_], writes=[out], cost=ecost("act", in_))
        return K.op(eng, lambda h: h.tensor_copy(out=out.ap, in_=in_.ap), reads=[in_], writes=[out], cost=ecost(eng, out))

    def memset(eng, out, val):
        return K.op(eng, lambda h: h.memset(out.ap, val), writes=[out], cost=ecost(eng, out, 0.5))

    def recip(out, in_):
        return K.op("dve", lambda h: h.reciprocal(out=out.ap, in_=in_.ap), reads=[in_], writes=[out], cost=ecost("dve", out, 6.4))

    def reduce_sum(out, in_):
        return K.op("dve", lambda h: h.tensor_reduce(out=out.ap, in_=in_.ap, axis=AX.X, op=ALU.add),
                    reads=[in_], writes=[out], cost=ecost("dve", in_))

    def load(dst, src_ap, q="sp"):
        return K.dma(q, q, dst.ap, src_ap, writes=[dst], nbytes=(dst.hi - dst.lo) * 128)

    def loadw(dst, src_ap):
        return K.dma("pool", "pool", dst.ap, src_ap, writes=[dst], nbytes=fsz(dst) * 128 * 4)

    def store(dst_ap, src, is_output=True):
        return K.dma("sp", "sp", dst_ap, src.ap, reads=[src], is_output=is_output, nbytes=(src.hi - src.lo) * 128)

    def tap(name, view, dt):
        shp = [int(x) for x in view.ap.shape]
        d = nc.dram_tensor("tap_" + name, shp, dt, kind="ExternalOutput").ap()
        K.dma("sp", "sp", d, view.ap, reads=[view], is_output=True, nbytes=(view.hi - view.lo) * 128)
        tap_list.append("tap_" + name)

    def rstd_from_ms(rstd, ms, tmp, mhalf):
        ts("pool", tmp, ms, EPS, ALU.add)
        tt("pool", rstd, tmp, mhalf, ALU.pow)

    KB = 1024
    HT_O = 0
    AT_O = 32 * KB
    KV_O = 76 * KB
    FY_O = 121 * KB
    QM_O = 137 * KB
    SM_O = 169 * KB
    assert SM_O + 31 * KB <= SB_BYTES

    c_o = SM_O
    ident = sb(c_o, BF16, (128,)); c_o += 256
    cond = sb(c_o, F32, (8, 2)); c_o += 64
    scond = sb(c_o, BF16, (8, 2)); c_o += 32
    badac = sb(c_o, F32, (48,)); c_o += 192
    gcols = sb(c_o, F32, (32,)); c_o += 128
    modraw = sb(c_o, F32, (4, 8, 2)); c_o += 256
    modc = sb(c_o, F32, (6, 8)); c_o += 192
    stats = sb(c_o, F32, (64,)); c_o += 256
    mhalf = sb(c_o, F32, (16,)); c_o += 64
    gq_bc = sb(c_o, F32, (64,)); c_o += 256
    gk_bc = sb(c_o, F32, (64,)); c_o += 256
    cs_t = sb(c_o, BF16, (256,)); c_o += 512
    scond_bc = sb(c_o, BF16, (8, 128)); c_o += 2048
    GTm = sb(c_o, F32, (1024,)); c_o += 4096
    GTf = sb(c_o, F32, (1024,)); c_o += 4096
    SMF = c_o
    SM_END = SM_O + 31 * KB

    load(ident, ident_d)
    load(cond, cond_d.rearrange("p (a b) -> p a b", a=8))
    load(badac, badac_d)
    load(gcols, gcols_d)
    load(gq_bc, gq_d.partition_broadcast(128))
    load(gk_bc, gk_d.partition_broadcast(128))
    load(cs_t, cs_d)
    memset("dve", mhalf, -0.5)
    act(scond, cond, AF.Silu)
    cp("dve", scond_bc, sub(scond, scond.ap[:, :, 0:1].to_broadcast([128, 8, 128])))

    wada_v = wada_d.rearrange("(k p) n -> p k n", p=128)
    ada_slots = [sb(QM_O + i * 8 * KB, BF16, (8, 512)) for i in range(4)]
    slot_i = [0]

    def ada_piece(col0):
        s = ada_slots[slot_i[0] % 4]
        slot_i[0] += 1
        loadw(s, wada_v[:, :, col0:col0 + 512])
        return s

    gmix = sub(gcols, gcols.ap[:, 0:8])
    gffn = sub(gcols, gcols.ap[:, 8:16])
    bgate = sub(gcols, gcols.ap[:, 16:32])

    def ada_cols(vis, pbank, piece_fn):
        pcol = bank(pbank)
        pcol_v = sub(pcol, pcol.ap[:, 0:64].rearrange("p (v c n) -> p v c n", v=4, c=8))
        for vi in vis:
            v = (0, 1, 3, 4)[vi]
            for half in range(2):
                s_ = piece_fn(v * 1024 + half * 512)
                for cc in range(4):
                    c = half * 4 + cc
                    for k in range(8):
                        mm(sub(pcol, pcol_v.ap[:, vi, c, :]),
                           sub(s_, s_.ap[:, k, cc * 128:(cc + 1) * 128]),
                           sub(scond, scond.ap[:, k, :]), k == 0, k == 7)
        for vi in vis:
            v = (0, 1, 3, 4)[vi]
            tt("dve", sub(modraw, modraw.ap[:, vi, :, :]), sub(pcol, pcol_v.ap[:, vi, :, :]),
               sub(badac, badac.ap[:, v * 8:(v + 1) * 8].unsqueeze(2).to_broadcast([128, 8, 2])), ALU.add)

    ada_cols((0, 1), 0, ada_piece)
    stt("dve", sub(modc, modc.ap[:, 0, :]), sub(modraw, modraw.ap[:, 1, :, 0]), 1.0, gmix, ALU.add, ALU.mult)
    cp("dve", sub(modc, modc.ap[:, 1, :]), sub(modraw, modraw.ap[:, 0, :, 0]))
    stt("dve", sub(modc, modc.ap[:, 2, :]), sub(modraw, modraw.ap[:, 1, :, 1]), 1.0, gmix, ALU.add, ALU.mult)
    cp("dve", sub(modc, modc.ap[:, 3, :]), sub(modraw, modraw.ap[:, 0, :, 1]))

    late_slots = [sb(100 * KB + i * 8 * KB, BF16, (8, 512)) for i in range(2)]
    late_i = [0]

    def late_piece(col0):
        s_ = late_slots[late_i[0] % 2]
        late_i[0] += 1
        loadw(s_, wada_v[:, :, col0:col0 + 512])
        return s_

    def ada_late():
        ada_cols((2, 3), 6, late_piece)
        stt("dve", sub(modc, modc.ap[:, 4, :]), sub(modraw, modraw.ap[:, 3, :, 0]), 1.0, gffn, ALU.add, ALU.mult)
        cp("dve", sub(modc, modc.ap[:, 5, :]), sub(modraw, modraw.ap[:, 2, :, 0]))
        for gt, v in ((GTm, 2), (GTf, 5)):
            load(gt, bada_d[v * 1024:(v + 1) * 1024].partition_broadcast(128))
            for half in range(2):
                s_ = late_piece(v * 1024 + half * 512)
                pb = bank(7)
                for k in range(8):
                    mm(pb, sub(scond_bc, scond_bc.ap[:, k, :]), sub(s_, s_.ap[:, k, :]), k == 0, k == 7)
                g_half = rng(gt, gt.ap[:, half * 512:(half + 1) * 512], half * 512, half * 512 + 512, 4)
                tt("dve", g_half, pb, g_half, ALU.add)
        if debug:
            tap("modc", modc, F32)
            tap("GTm", GTm, F32)

    hT = sb(HT_O, BF16, (16, 8, 128))
    qT = sb(QM_O, BF16, (16, 8, 128))
    kT2 = sb(KV_O, BF16, (4, NKT * 128))
    Vaug = sb(KV_O + 18 * KB, BF16, (NKT, 4, 192))
    hcT = sb(QM_O + 28 * KB, BF16, (2, 8, 128))
    Wq0 = sb(AT_O, BF16, (8, 512))
    Wq1 = sb(AT_O + 8 * KB, BF16, (8, 512))
    Wkv = sb(AT_O + 16 * KB, BF16, (8, 512))
    Wf = sb(AT_O + 24 * KB, BF16, (8, 512))
    ropec = sb(AT_O + 32 * KB, F32, (NKT, 64))
    ropes = sb(AT_O + 32 * KB + 4608, F32, (NKT, 64))
    win_v = win_d.rearrange("(k p) n -> p k n", p=128)
    loadw(Wkv, win_v[:, :, 1024:1536])
    loadw(Wq0, win_v[:, :, 0:512])
    loadw(Wq1, win_v[:, :, 512:1024])
    loadw(Wf, win_v[:, :, 1536:2048])
    load(ropec, ropec_d.rearrange("p (a b) -> p a b", a=NKT))
    load(ropes, ropes_d.rearrange("p (a b) -> p a b", a=NKT))
    memset("dve", Vaug, 1.0)

    o = SMF
    xbuf = [sb(o + i * 4096, F32, (1024,)) for i in range(2)]; o += 8192
    junk = sb(o, BF16, (1024,)); o += 2048
    xn = [sb(o + i * 2048, BF16, (1024,)) for i in range(2)]; o += 4096
    kr2 = [sb(o + i * 1024, BF16, (4, 2, 64)) for i in range(2)]; o += 2048
    tabs2 = [sb(o + i * 1024, F32, (4, 64)) for i in range(2)]; o += 2048
    assert o <= SM_END, o
    o = FY_O
    sq = sb(o, F32, (8, 64)); o += 2048
    qn = sb(o, F32, (8, 64)); o += 2048
    uu = sb(o, F32, (8, 64)); o += 2048
    ww = sb(o, F32, (8, 64)); o += 2048
    qr = [sb(o + i * 2048, BF16, (16, 64)) for i in range(2)]; o += 4096
    assert o <= FY_O + 16 * KB

    st_ms = lambda i: sub(stats, stats.ap[:, i:i + 1])

    def hsl(tt_i, k):
        if tt_i < NCT:
            return rng(hcT, hcT.ap[:, tt_i, k, :], (tt_i * 8 + k) * 128, (tt_i * 8 + k + 1) * 128, 2)
        t = tt_i - NCT
        return rng(hT, hT.ap[:, t, k, :], (t * 8 + k) * 128, (t * 8 + k + 1) * 128, 2)

    def P1(tt_i):
        is_ctx = tt_i < NCT
        xb = xbuf[tt_i % 2]
        src = ctx_d[tt_i * 128:(tt_i + 1) * 128, :] if is_ctx else x_d[(tt_i - NCT) * 128:(tt_i - NCT + 1) * 128, :]
        load(xb, src)
        sc = (tt_i % 2) * 3
        act(junk, xb, AF.Square, scale=1.0 / 32.0, accum=st_ms(sc))
        rstd_from_ms(st_ms(sc + 2), st_ms(sc), st_ms(sc + 1), sub(mhalf, mhalf.ap[:, 0:1]))
        xnb = xn[tt_i % 2]
        ts("dve", xnb, xb, st_ms(sc + 2), ALU.mult)
        pT = bank_bf(6)
        for k in range(8):
            tr(sub(pT, pT.ap[:, k * 128:(k + 1) * 128]), sub(xnb, xnb.ap[:, k * 128:(k + 1) * 128]), ident)
        gi, si = (2, 3) if is_ctx else (0, 1)
        for k in range(8):
            act(hsl(tt_i, k), sub(pT, pT.ap[:, k * 128:(k + 1) * 128]), AF.Identity,
                scale=sub(modc, modc.ap[:, gi, k:k + 1]), bias=sub(modc, modc.ap[:, si, k:k + 1]))

    def MM(tt_i):
        base = (tt_i % 2) * 3
        pkv = bank(base)
        for k in range(8):
            mm(pkv, hsl(tt_i, k), sub(Wkv, Wkv.ap[:, k, :]), k == 0, k == 7)
        if tt_i < NCT:
            return
        for qb, Wq in enumerate((Wq0, Wq1)):
            pq = bank(base + 1 + qb)
            for k in range(8):
                mm(pq, hsl(tt_i, k), sub(Wq, Wq.ap[:, k, :]), k == 0, k == 7)

    blk_i = [0]

    def make_block(ps_v, nh, tab_c, tab_s, tabv, dsts):
        par = blk_i[0] % 2
        blk_i[0] += 1
        b0 = 8 + par * 24
        p3 = sub(ps_v, ps_v.ap.rearrange("p (h d) -> p h d", h=nh))
        ss = sub(stats, stats.ap[:, b0:b0 + nh])
        tmp = sub(stats, stats.ap[:, b0 + 8:b0 + 8 + nh])
        rs = sub(stats, stats.ap[:, b0 + 16:b0 + 16 + nh])

        def X():
            sq3 = sub(sq, sq.ap[:, 0:nh, :])
            act(sq3, p3, AF.Square, scale=0.125)
            reduce_sum(ss, sq3)
            rstd_from_ms(rs, ss, tmp, sub(mhalf, mhalf.ap[:, 0:nh]))

        def Y():
            qn3 = sub(qn, qn.ap[:, 0:nh, :])
            tt("dve", qn3, p3, sub(rs, rs.ap.unsqueeze(2).to_broadcast([128, nh, 64])), ALU.mult)
            u3 = sub(uu, uu.ap[:, 0:nh, :])
            w3 = sub(ww, ww.ap[:, 0:nh, :])
            tt("dve", u3, qn3, sub(tabv, tab_c.unsqueeze(1).to_broadcast([128, nh, 64])), ALU.mult)
            tt("dve", w3, qn3, sub(tabv, tab_s.unsqueeze(1).to_broadcast([128, nh, 64])), ALU.mult)
            for d in dsts:
                tt("dve", sub(d, d.ap[:, :, 0:32]), sub(uu, u3.ap[:, :, 0:32]), sub(ww, w3.ap[:, :, 32:64]), ALU.subtract)
                tt("dve", sub(d, d.ap[:, :, 32:64]), sub(uu, u3.ap[:, :, 32:64]), sub(ww, w3.ap[:, :, 0:32]), ALU.add)
        return X, Y

    pending = []

    def run_block(X, Y, after=None):
        X()
        if pending:
            py, pafter = pending.pop()
            py()
            if pafter:
                pafter()
        pending.append((Y, after))

    def POST_a(tt_i):
        is_ctx = tt_i < NCT
        t = tt_i - NCT
        base = (tt_i % 2) * 3
        pkv = bank(base)
        tabv = tabs2[tt_i % 2]
        krb = kr2[tt_i % 2]
        rc = sub(ropec, ropec.ap[:, tt_i, :])
        rs_ = sub(ropes, ropes.ap[:, tt_i, :])
        if not is_ctx:
            stt("dve", sub(tabv, tabv.ap[:, 0, :]), rc, 0.125, gq_bc, ALU.mult, ALU.mult)
            stt("dve", sub(tabv, tabv.ap[:, 1, :]), rs_, 0.125, gq_bc, ALU.mult, ALU.mult)
        tt("dve", sub(tabv, tabv.ap[:, 2, :]), rc, gk_bc, ALU.mult)
        tt("dve", sub(tabv, tabv.ap[:, 3, :]), rs_, gk_bc, ALU.mult)
        kr_a = sub(krb, krb.ap[:, :, 0, :])
        kr_b = sub(krb, krb.ap[:, :, 1, :])
        Xk, Yk = make_block(sub(pkv, pkv.ap[:, 0:256]), 4, tabv.ap[:, 2, :], tabv.ap[:, 3, :], tabv, [kr_a, kr_b])

        def Xk2():
            Xk()
            vdst = rng(Vaug, Vaug.ap[:, tt_i, :, 64:128], tt_i * 768, (tt_i + 1) * 768, 2)
            cp("act", vdst, sub(pkv, pkv.ap[:, 256:512].rearrange("p (h d) -> p h d", h=4)))

        def Fk():
            pk = bank_bf(7)
            for kh in range(4):
                tr(sub(pk, pk.ap[:, kh * 128:(kh + 1) * 128]),
                   sub(krb, krb.ap[:, kh, :, :].rearrange("p a d -> p (a d)")), ident)
            kdst = sub(kT2, kT2.ap[:, :, tt_i * 128:(tt_i + 1) * 128])
            cp("act", kdst, sub(pk, pk.ap[:, 0:512].rearrange("p (h t) -> p h t", h=4)))

        run_block(Xk2, Yk, Fk)

    def POST_b(tt_i):
        if tt_i < NCT:
            return
        t = tt_i - NCT
        base = (tt_i % 2) * 3
        tabv = tabs2[tt_i % 2]
        qrb = qr[t % 2]

        def Fq():
            pqt = bank_bf(7)
            for c in range(8):
                tr(sub(pqt, pqt.ap[:, c * 128:(c + 1) * 128]),
                   sub(qrb, qrb.ap[:, 2 * c:2 * c + 2, :].rearrange("p a d -> p (a d)")), ident)
            qdst = rng(qT, qT.ap[:, t, :, :].rearrange("p c t -> p (c t)"), t * 1024, (t + 1) * 1024, 2)
            cp("act", qdst, pqt)

        for qb in range(2):
            pq = bank(base + 1 + qb)
            qd = rng(qrb, qrb.ap[:, qb * 8:(qb + 1) * 8, :], qb * 512, (qb + 1) * 512, 2)
            Xq, Yq = make_block(pq, 8, tabv.ap[:, 0, :], tabv.ap[:, 1, :], tabv, [qd])
            run_block(Xq, Yq, Fq if qb == 1 else None)

    for s_ in range(NKT + 2):
        if s_ < NKT:
            P1(s_)
        if s_ >= 2:
            POST_a(s_ - 2)
        if 1 <= s_ <= NKT:
            MM(s_ - 1)
        if s_ >= 2:
            POST_b(s_ - 2)
    while pending:
        py, pafter = pending.pop()
        py()
        if pafter:
            pafter()
    pb_i = [0]

    def next_bank():
        b = pb_i[0] % 8
        pb_i[0] += 1
        return b

    if debug:
        tap("hT", hT, BF16)
        tap("qT", qT, BF16)
        tap("kT2", kT2, BF16)
        tap("Vaug", Vaug, BF16)

    FT = sb(FY_O, BF16, (4, 2048))
    for fg in range(4):
        for tb in range(4):
            b = next_bank()
            pf = bank(b)
            rhs_all = lambda k: rng(hT, hT.ap[:, tb * 4:(tb + 1) * 4, k, :], tb * 4096, (tb + 1) * 4096, 2)
            for k in range(8):
                mm(pf, sub(Wf, Wf.ap[:, k, fg * 128:(fg + 1) * 128]), rhs_all(k), k == 0, k == 7)
            fdst = rng(FT, FT.ap[:, fg, tb * 512:(tb + 1) * 512], fg * 2048 + tb * 512, fg * 2048 + (tb + 1) * 512, 2)
            cp("act" if (fg + tb) % 2 else "dve", fdst, pf)
    if debug:
        tap("FT", FT, BF16)

    attnT = sb(AT_O, BF16, (8, 2048))
    PT = [sb(SMF + i * 2048, BF16, (1024,)) for i in range(3)]
    accsb = [sb(SMF + 6144 + i * 4096, F32, (1024,)) for i in range(2)]
    rden1 = sb(SMF + 6144 + 8192, F32, (1024,))
    assert SMF + 6144 + 12288 <= SM_END
    steps = [(c, qb, kt) for c in range(NH // 2) for qb in range(4) for kt in range(NKT)]

    def s_bank(i):
        return bank((i % 3) * 2, 2)

    def acc_bank(j):
        return bank(6, 2)

    def emit_qk(i):
        c, qb, kt = steps[i]
        kh = c // 2
        sbk = s_bank(i)
        t0 = qb * 4
        for half in range(2):
            p0, p1 = half * 64, half * 64 + 64
            lhsT = View(kT2.ap[p0:p1, kh, kt * 128:(kt + 1) * 128], "sb",
                        kT2.lo + (kh * NKT * 128 + kt * 128) * 2, kT2.lo + (kh * NKT * 128 + kt * 128 + 128) * 2)
            rhs = View(qT.ap[p0:p1, t0:t0 + 4, c, :], "sb", qT.lo + t0 * 2048, qT.lo + (t0 + 4) * 2048)
            mm(View(sbk.ap[:, half * 512:(half + 1) * 512], "ps", sbk.lo + half * 2048, sbk.lo + (half + 1) * 2048),
               lhsT, rhs, True, True)

    emit_qk(0)
    emit_qk(1)
    for i in range(len(steps)):
        c, qb, kt = steps[i]
        kh = c // 2
        if i + 2 < len(steps):
            emit_qk(i + 2)
        pt = PT[i % 3]
        act(pt, s_bank(i), AF.Exp)
        j = i // NKT
        acc = acc_bank(j)
        for half in range(2):
            c0 = 64 if half == 0 else 0
            va = View(Vaug.ap[:, kt, kh, c0:c0 + 128], "sb", Vaug.lo + (kt * 4 + kh) * 384, Vaug.lo + (kt * 4 + kh + 1) * 384)
            mm(View(acc.ap[:, half * 512:(half + 1) * 512], "ps", acc.lo + half * 2048, acc.lo + (half + 1) * 2048),
               va, sub(pt, pt.ap[:, half * 512:(half + 1) * 512]), kt == 0, kt == NKT - 1)
        if kt == NKT - 1:
            rd = rden1
            asb = accsb[j % 2]
            cp("dve", asb, acc)
            for half in range(2):
                ab = View(asb.ap[:, half * 512:(half + 1) * 512], "sb", asb.lo + half * 2048, asb.lo + (half + 1) * 2048)
                n0, d0 = (0, 64) if half == 0 else (64, 0)
                num = View(ab.ap[n0:n0 + 64, :], "sb", ab.lo, ab.hi)
                den = View(ab.ap[d0:d0 + 64, :], "sb", ab.lo, ab.hi)
                rdv = View(rd.ap[n0:n0 + 64, half * 512:(half + 1) * 512], "sb", rd.lo + half * 2048, rd.lo + (half + 1) * 2048)
                dst = View(attnT.ap[n0:n0 + 64, c, qb * 512:(qb + 1) * 512], "sb",
                           attnT.lo + (c * 2048 + qb * 512) * 2, attnT.lo + (c * 2048 + qb * 512 + 512) * 2)
                recip(rdv, den)
                tt("dve", dst, num, rdv, ALU.mult)
    if debug:
        tap("attnT", attnT, BF16)

    G = sb(QM_O, BF16, (16, 4, 256))
    YT = sb(FY_O, BF16, (4, 2048))
    for lt in range(NT):
        for pr in range(2):
            b = next_bank()
            pg = bank(b)
            for a in range(2):
                fg = pr * 2 + a
                mm(View(pg.ap[:, a * 256:(a + 1) * 256], "ps", pg.lo + a * 1024, pg.lo + (a + 1) * 1024),
                   rng(FT, FT.ap[:, fg, lt * 128:(lt + 1) * 128], fg * 2048 + lt * 128, fg * 2048 + lt * 128 + 128, 2),
                   cs_t, True, True)
            gd = rng(G, G.ap[:, lt, pr * 2:pr * 2 + 2, :].rearrange("p a c -> p (a c)"),
                     (lt * 4 + pr * 2) * 256, (lt * 4 + pr * 2 + 2) * 256, 2)
            cp("act" if (lt + pr) % 2 else "dve", gd, pg)
    dft_slots = [sb(KV_O + i * 8 * KB, BF16, (4, 2, 512)) for i in range(3)]
    di = 0
    for jb in range(4):
        banks = [bank((jb % 2) * 4 + fg) for fg in range(4)]
        for lg in range(4):
            ds = dft_slots[di % 3]
            load(ds, dftl_d[jb * 4 + lg].rearrange("p (a b c) -> p a b c", a=4, b=2))
            di += 1
            for fg in range(4):
                for a in range(4):
                    lt = lg * 4 + a
                    for cs_i in range(2):
                        mm(banks[fg],
                           rng(G, G.ap[:, lt, fg, cs_i * 128:(cs_i + 1) * 128],
                               (lt * 4 + fg) * 256 + cs_i * 128, (lt * 4 + fg) * 256 + cs_i * 128 + 128, 2),
                           sub(ds, ds.ap[:, a, cs_i, :]),
                           lt == 0 and cs_i == 0, lt == NT - 1 and cs_i == 1)
        for fg in range(4):
            yd = rng(YT, YT.ap[:, fg, jb * 512:(jb + 1) * 512], fg * 2048 + jb * 512, fg * 2048 + jb * 512 + 512, 2)
            cp("act" if fg % 2 else "dve", yd, banks[fg])
    ada_late()
    if debug:
        tap("YT", YT, BF16)

    MT = sb(QM_O, BF16, (8, 2048))
    wslots = [sb(100 * KB + i * 6 * KB, BF16, (24, 128)) for i in range(2)]
    wfps = [sb(112 * KB + i * KB, BF16, (4, 128)) for i in range(2)]
    gsl = [sb(SMF + i * 2048, F32, (512,)) for i in range(4)]
    wap_v = wap_d.rearrange("(k p) n -> p k n", p=128)
    wfp_v = wfp_d.rearrange("(k p) n -> p k n", p=128)
    for j in range(8):
        wsl = wslots[j % 2]
        wf_ = wfps[j % 2]
        loadw(rng(wsl, wsl.ap[:, 0:8, :], 0, 1024, 2), wap_v[:, :, j * 128:(j + 1) * 128])
        loadw(rng(wsl, wsl.ap[:, 8:16, :], 1024, 2048, 2), win_v[:, :, 2048 + j * 128:2048 + (j + 1) * 128])
        loadw(rng(wsl, wsl.ap[:, 16:24, :], 2048, 3072, 2), win_v[:, :, 3072 + j * 128:3072 + (j + 1) * 128])
        loadw(wf_, wfp_v[:, :, j * 128:(j + 1) * 128])
        for tb in range(4):
            base = ((j * 4 + tb) % 2) * 4
            pA, pF, pGa, pGf = bank(base), bank(base + 1), bank(base + 2), bank(base + 3)
            hsl4 = lambda k: rng(hT, hT.ap[:, tb * 4:(tb + 1) * 4, k, :], tb * 4096, (tb + 1) * 4096, 2)
            for k in range(8):
                mm(pGa, rng(wsl, wsl.ap[:, 8 + k, :], (8 + k) * 128, (9 + k) * 128, 2), hsl4(k), k == 0, k == 7)
            for k in range(8):
                mm(pGf, rng(wsl, wsl.ap[:, 16 + k, :], (16 + k) * 128, (17 + k) * 128, 2), hsl4(k), k == 0, k == 7)
            for k in range(8):
                mm(pA, rng(wsl, wsl.ap[:, k, :], k * 128, (k + 1) * 128, 2),
                   rng(attnT, attnT.ap[:, k, tb * 512:(tb + 1) * 512], k * 2048 + tb * 512, k * 2048 + tb * 512 + 512, 2),
                   k == 0, k == 7)
            for k in range(4):
                mm(pF, sub(wf_, wf_.ap[:, k, :]),
                   rng(YT, YT.ap[:, k, tb * 512:(tb + 1) * 512], k * 2048 + tb * 512, k * 2048 + tb * 512 + 512, 2),
                   k == 0, k == 3)
            act(gsl[0], pGa, AF.Sigmoid, bias=sub(bgate, bgate.ap[:, j:j + 1]))
            act(gsl[1], pGf, AF.Sigmoid, bias=sub(bgate, bgate.ap[:, 8 + j:9 + j]))
            tt("dve", gsl[2], pA, gsl[0], ALU.mult)
            tt("dve", gsl[3], pF, gsl[1], ALU.mult)
            md = rng(MT, MT.ap[:, j, tb * 512:(tb + 1) * 512], j * 2048 + tb * 512, j * 2048 + tb * 512 + 512, 2)
            tt("dve", md, gsl[2], gsl[3], ALU.add)
    if debug:
        tap("MT", MT, BF16)

    X1 = sb(AT_O, F32, (16, 1024))
    Wo = sb(FY_O, BF16, (8, 1024))
    h2T = sb(HT_O, BF16, (16, 8, 128))
    wo_v = wo_d.rearrange("(k p) n -> p k n", p=128)
    loadw(rng(Wo, Wo.ap[:, 0:4, :], 0, 4096, 2), wo_v[:, 0:4, :])
    loadw(rng(Wo, Wo.ap[:, 4:8, :], 4096, 8192, 2), wo_v[:, 4:8, :])
    o = SMF
    xbuf = [sb(o + i * 4096, F32, (1024,)) for i in range(2)]; o += 8192
    junk = sb(o, BF16, (1024,)); o += 2048
    xn = [sb(o + i * 2048, BF16, (1024,)) for i in range(2)]; o += 4096
    tmpf = sb(o, F32, (1024,)); o += 4096
    assert o <= SM_END
    for t in range(NT):
        xb = xbuf[t % 2]
        load(xb, x_d[t * 128:(t + 1) * 128, :])
        x1t = rng(X1, X1.ap[:, t, :], t * 1024, (t + 1) * 1024, 4)
        for nb in range(2):
            b = next_bank()
            po = bank(b)
            for j in range(8):
                mm(po, rng(MT, MT.ap[:, j, t * 128:(t + 1) * 128], j * 2048 + t * 128, j * 2048 + t * 128 + 128, 2),
                   rng(Wo, Wo.ap[:, j, nb * 512:(nb + 1) * 512], j * 1024 + nb * 512, j * 1024 + nb * 512 + 512, 2),
                   j == 0, j == 7)
            tmh = rng(tmpf, tmpf.ap[:, nb * 512:(nb + 1) * 512], nb * 512, nb * 512 + 512, 4)
            tt("dve", tmh, po, rng(GTm, GTm.ap[:, nb * 512:(nb + 1) * 512], nb * 512, nb * 512 + 512, 4), ALU.mult)
            tt("dve", rng(X1, X1.ap[:, t, nb * 512:(nb + 1) * 512], t * 1024 + nb * 512, t * 1024 + nb * 512 + 512, 4),
               tmh, rng(xb, xb.ap[:, nb * 512:(nb + 1) * 512], nb * 512, nb * 512 + 512, 4), ALU.add)
        sc = (t % 2) * 3
        act(junk, x1t, AF.Square, scale=1.0 / 32.0, accum=st_ms(sc))
        rstd_from_ms(st_ms(sc + 2), st_ms(sc), st_ms(sc + 1), sub(mhalf, mhalf.ap[:, 0:1]))
        xnb = xn[t % 2]
        ts("dve", xnb, x1t, st_ms(sc + 2), ALU.mult)
        bT = next_bank()
        pT = bank_bf(bT)
        for k in range(8):
            tr(sub(pT, pT.ap[:, k * 128:(k + 1) * 128]), sub(xnb, xnb.ap[:, k * 128:(k + 1) * 128]), ident)
        for k in range(8):
            dst = rng(h2T, h2T.ap[:, t, k, :], (t * 8 + k) * 128, (t * 8 + k + 1) * 128, 2)
            act(dst, sub(pT, pT.ap[:, k * 128:(k + 1) * 128]), AF.Identity,
                scale=sub(modc, modc.ap[:, 4, k:k + 1]), bias=sub(modc, modc.ap[:, 5, k:k + 1]))
    if debug:
        tap("X1", X1, F32)
        tap("h2T", h2T, BF16)

    Wd = sb(96 * KB, BF16, (NFF, 1024))
    actT = sb(140 * KB, BF16, (NFF, 512))
    gfin = sb(GTm.lo, F32, (1024,))
    junk2 = sb(162 * KB, BF16, (1024,))
    tm2 = [sb(164 * KB + i * 2048, F32, (512,)) for i in range(2)]
    wgu_slots = [sb(SMF + i * 8 * KB, BF16, (8, 2, 256)) for i in range(2)]
    sg = [sb(SMF + 16 * KB, F32, (512,)), sb(scond_bc.lo, F32, (512,))]
    assert SMF + 18 * KB <= SM_END
    wdn_v = wdn_d.rearrange("(k p) n -> p k n", p=128)
    wgu_v = wgu_d.rearrange("(k p) n -> p k n", p=128)
    load(gfin, gfin_d.partition_broadcast(128))
    pieces = [(tq, i) for tq in range(4) for i in range(NFF)]
    NG = len(pieces) // 2

    def load_group(gi):
        tq, i = pieces[gi * 2]
        sl = wgu_slots[gi % 2]
        loadw(sub(sl, sl.ap[:, :, 0, :]), wgu_v[:, :, i * 128:(i + 2) * 128])
        loadw(sub(sl, sl.ap[:, :, 1, :]), wgu_v[:, :, DFF + i * 128:DFF + (i + 2) * 128])

    load_group(0)
    load_group(1)
    for c4 in range(2):
        loadw(rng(Wd, Wd.ap[:, c4 * 11:(c4 + 1) * 11, :], c4 * 11 * 1024, (c4 + 1) * 11 * 1024, 2),
              wdn_v[:, c4 * 11:(c4 + 1) * 11, :])
    for pi, (tq, i) in enumerate(pieces):
        sl = wgu_slots[(pi // 2) % 2]
        o2 = (pi % 2) * 128
        pg_, pu_ = bank((pi % 2) * 2), bank((pi % 2) * 2 + 1)
        rhs4 = lambda k: rng(h2T, h2T.ap[:, tq * 4:(tq + 1) * 4, k, :], tq * 4096, (tq + 1) * 4096, 2)
        for k in range(8):
            mm(pg_, sub(sl, sl.ap[:, k, 0, o2:o2 + 128]), rhs4(k), k == 0, k == 7)
        for k in range(8):
            mm(pu_, sub(sl, sl.ap[:, k, 1, o2:o2 + 128]), rhs4(k), k == 0, k == 7)
        if pi % 2 == 1 and pi // 2 + 2 < NG:
            load_group(pi // 2 + 2)
        sgi = sg[pi % 2]
        act(sgi, pg_, AF.Silu)
        tt("dve", rng(actT, actT.ap[:, i, :], i * 512, (i + 1) * 512, 2), pu_, sgi, ALU.mult)
        if i != NFF - 1:
            continue
        for t4 in range(4):
            t = tq * 4 + t4
            x1t = rng(X1, X1.ap[:, t, :], t * 1024, (t + 1) * 1024, 4)
            for nb in range(2):
                pd = bank(4 + ((t4 * 2 + nb) % 4))
                for ii in range(NFF):
                    mm(pd, rng(actT, actT.ap[:, ii, t4 * 128:(t4 + 1) * 128], ii * 512 + t4 * 128, ii * 512 + t4 * 128 + 128, 2),
                       rng(Wd, Wd.ap[:, ii, nb * 512:(nb + 1) * 512], ii * 1024 + nb * 512, ii * 1024 + nb * 512 + 512, 2),
                       ii == 0, ii == NFF - 1)
                tmh = tm2[nb]
                tt("dve", tmh, pd, rng(GTf, GTf.ap[:, nb * 512:(nb + 1) * 512], nb * 512, nb * 512 + 512, 4), ALU.mult)
                x1h = rng(X1, X1.ap[:, t, nb * 512:(nb + 1) * 512], t * 1024 + nb * 512, t * 1024 + nb * 512 + 512, 4)
                tt("dve", x1h, tmh, x1h, ALU.add)
            sc = (t % 2) * 3
            act(junk2, x1t, AF.Square, scale=1.0 / 32.0, accum=st_ms(sc))
            rstd_from_ms(st_ms(sc + 2), st_ms(sc), st_ms(sc + 1), sub(mhalf, mhalf.ap[:, 0:1]))
            stt("dve", x1t, x1t, st_ms(sc + 2), gfin, ALU.mult, ALU.mult)
            store(out_d[t * 128:(t + 1) * 128, :], x1t)

    K.schedule()
    K.emit_plan()

    with nc.Block() as block:
        @block.tensor
        def _(h):
            K.replay_one(nc, "pe", h, sems, dma_sems)

        @block.scalar
        def _(h):
            K.replay_one(nc, "act", h, sems, dma_sems)

        @block.vector
        def _(h):
            K.replay_one(nc, "dve", h, sems, dma_sems)

        @block.gpsimd
        def _(h):
            K.replay_one(nc, "pool", h, sems, dma_sems)

        @block.sync
        def _(h):
            K.replay_one(nc, "sp", h, sems, dma_sems)
    es.close()
    return nc, tap_list


_CONST = {}


def _consts():
    if _CONST:
        return _CONST
    bf = ml_dtypes.bfloat16
    _CONST["ident"] = np.eye(128, dtype=np.float32).astype(bf)
    inv = (10000.0 ** (-np.arange(0, 32, 2, dtype=np.float32) / np.float32(32))).astype(np.float32)
    tpos = np.arange(S)
    t_row = (tpos // 64).astype(np.float32)
    t_col = (tpos % 64).astype(np.float32)
    ang = np.concatenate([t_row[:, None] * inv[None, :], t_col[:, None] * inv[None, :]], axis=-1).astype(np.float32)
    cos = np.cos(ang).astype(np.float32)
    sin = np.sin(ang).astype(np.float32)
    rc = np.ones((NKT, 128, 64), np.float32)
    rs = np.zeros((NKT, 128, 64), np.float32)
    rc[NCT:] = np.concatenate([cos, cos], -1).reshape(NT, 128, 64)
    rs[NCT:] = np.concatenate([sin, sin], -1).reshape(NT, 128, 64)
    _CONST["rope_c"] = np.ascontiguousarray(rc.transpose(1, 0, 2).reshape(128, NKT * 64))
    _CONST["rope_s"] = np.ascontiguousarray(rs.transpose(1, 0, 2).reshape(128, NKT * 64))
    c = np.arange(128)
    a = 2.0 * np.pi * ((c[:, None] * c[None, :]) % 128) / 128.0
    _CONST["dft_cs"] = (np.concatenate([np.cos(a), np.sin(a)], 1) / 512.0).astype(np.float32).astype(bf)
    l = np.arange(S)
    m = (l[:, None] * l[None, :]) % S
    ang2 = 2.0 * np.pi * m / float(S)
    CL = np.cos(ang2).astype(np.float32).astype(bf)
    SL = (-np.sin(ang2)).astype(np.float32).astype(bf)
    both = np.stack([CL, SL], 0)
    both = both.reshape(2, 4, 4, 128, 4, 512)
    both = both.transpose(4, 1, 3, 2, 0, 5)
    _CONST["dft_l"] = np.ascontiguousarray(both).reshape(16, 128, 4096)
    return _CONST


def _col(v, n):
    return np.ascontiguousarray(np.asarray(v, np.float32).reshape(n, 128).T)


def _prep(inputs):
    f = lambda a: np.ascontiguousarray(np.asarray(a, dtype=np.float32))
    cst = _consts()
    shared = {
        "w_ada": f(inputs["w_ada"][0]),
        "b_ada_col": _col(inputs["b_ada"][0], 48),
        "b_ada": f(inputs["b_ada"][0]),
        "gcols": np.ascontiguousarray(np.concatenate([_col(inputs["g_norm_mix"][0], 8), _col(inputs["g_norm_ffn"][0], 8),
                                                      _col(inputs["b_gate"][0], 16)], axis=1)),
        "g_q": f(inputs["g_q"][0]),
        "g_k": f(inputs["g_k"][0]),
        "g_final": f(inputs["g_final"]),
        "w_in": f(inputs["w_in"][0]),
        "w_ap": f(inputs["w_attn_proj"][0]),
        "w_fp": f(inputs["w_fourier_proj"][0]),
        "w_o": f(inputs["w_o"][0]),
        "w_gu": f(inputs["w_gate_up"][0]),
        "w_dn": f(inputs["w_down"][0]),
    }
    shared.update(cst)
    x = f(inputs["x"])
    ctx = f(inputs["ctx"])
    c = f(inputs["c"])
    cc = _col(inputs["c_ctx"], 8)
    maps = []
    for b in range(8):
        m = dict(shared)
        m["x"] = x[b]
        m["ctx"] = ctx[b]
        cond = np.stack([_col(c[b], 8), cc], axis=-1)
        m["cond"] = np.ascontiguousarray(cond.reshape(128, 16))
        maps.append(m)
    return maps


_NC = {}


def kernel(**inputs):
    if "nc" not in _NC:
        _NC["nc"] = build(False)[0]
    maps = _prep(inputs)
    res = run_bass_kernel_spmd(_NC["nc"], maps, core_ids=list(range(8)))
    out = np.stack([np.asarray(r["out"], dtype=np.float32) for r in res.results], axis=0)
    return out
```

```python
import numpy as np
import ml_dtypes
import concourse.bass as bass
import concourse.mybir as mybir
from concourse.bass_utils import run_bass_kernel_spmd

F32 = mybir.dt.float32
BF16 = mybir.dt.bfloat16
U8 = mybir.dt.uint8
AF = mybir.ActivationFunctionType
ALU = mybir.AluOpType
AX = mybir.AxisListType

D = 1024
S = 2048
CTX = 256
NT = S // 128
NCT = CTX // 128
NKT = NT + NCT
HD = 64
NH = 16
NKV = 4
DFF = 2816
NFF = DFF // 128
EPS = 1e-6
SB_BYTES = 200 * 1024

DEBUG_TAPS = []


class View:
    __slots__ = ("ap", "sp", "lo", "hi")

    def __init__(self, ap, sp, lo, hi):
        self.ap, self.sp, self.lo, self.hi = ap, sp, lo, hi


class _Node:
    __slots__ = ("id", "eng", "kind", "fn", "cost", "preds", "succs", "npend", "ready", "start", "finish",
                 "pos", "inc", "num", "dma", "nbytes", "is_output", "tk")

    def __init__(self, id_, eng, kind, fn, cost):
        self.id, self.eng, self.kind, self.fn, self.cost = id_, eng, kind, fn, cost
        self.preds = {}
        self.succs = []
        self.npend = 0
        self.ready = 0.0
        self.start = self.finish = 0.0
        self.pos = -1
        self.inc = False
        self.num = None
        self.dma = None
        self.nbytes = 0
        self.is_output = False
        self.tk = None


class _Rec:
    __slots__ = ("lo", "hi", "kind", "node")

    def __init__(self, lo, hi, kind, node):
        self.lo, self.hi, self.kind, self.node = lo, hi, kind, node


class Tracker:
    ENGS = ("pe", "act", "dve", "pool", "sp")
    SYNC_NS = 250.0

    def __init__(self):
        self.nodes = []
        self.recs = {"sb": [], "ps": []}
        self.npools = {}
        self.order = {}

    def add_pool(self, name, k):
        self.npools[name] = k

    def _edges(self, n, reads, writes):
        is_dma = n.kind == "dma"
        for v in reads:
            for r in self.recs[v.sp]:
                if r.kind == "w" and r.lo < v.hi and v.lo < r.hi:
                    self._edge(r.node, n, is_dma)
        for v in writes:
            for r in self.recs[v.sp]:
                if r.lo < v.hi and v.lo < r.hi:
                    self._edge(r.node, n, is_dma)

    def _edge(self, p, n, is_dma):
        if p is n:
            return
        need = True
        if not is_dma and p.kind != "dma" and p.eng == n.eng and n.eng == "pe":
            need = False
        if p in n.preds:
            n.preds[p] = n.preds[p] or need
        else:
            n.preds[p] = need

    def _record(self, n, reads, writes):
        for v in writes:
            lst = self.recs[v.sp]
            lst[:] = [r for r in lst if not (v.lo <= r.lo and r.hi <= v.hi)]
            lst.append(_Rec(v.lo, v.hi, "w", n))
        for v in reads:
            self.recs[v.sp].append(_Rec(v.lo, v.hi, "r", n))

    def op(self, engname, fn, reads=(), writes=(), cost=300.0):
        n = _Node(len(self.nodes), engname, "op", fn, cost)
        self._edges(n, reads, writes)
        self._record(n, reads, writes)
        self.nodes.append(n)
        return n

    def dma(self, qname, pool, out_ap, in_ap, reads=(), writes=(), is_output=False, nbytes=0):
        n = _Node(len(self.nodes), qname, "dma", None, 1000.0 if qname == "pool" else 100.0)
        n.dma = (out_ap, in_ap, pool)
        n.nbytes = nbytes
        n.is_output = is_output
        self._edges(n, reads, writes)
        self._record(n, reads, writes)
        self.nodes.append(n)
        return n

    def schedule(self):
        import heapq
        for n in self.nodes:
            n.npend = len(n.preds)
            for p in n.preds:
                p.succs.append(n)
        free = {e: 0.0 for e in self.ENGS}
        wait_h = {e: [] for e in self.ENGS}
        run_h = {e: [] for e in self.ENGS}
        order = {e: [] for e in self.ENGS}
        dma_bw_free = [0.0]
        for n in self.nodes:
            if n.npend == 0:
                heapq.heappush(wait_h[n.eng], (0.0, n.id, n))
        remaining = len(self.nodes)
        while remaining:
            best = None
            for e in self.ENGS:
                wh, rh = wait_h[e], run_h[e]
                while wh and wh[0][0] <= free[e]:
                    _, i, nd = heapq.heappop(wh)
                    heapq.heappush(rh, (i, nd))
                if rh:
                    cand = (free[e], rh[0][0], e, True)
                elif wh:
                    cand = (wh[0][0], wh[0][1], e, False)
                else:
                    continue
                if best is None or cand[:2] < best[:2]:
                    best = cand
            st, _, e, from_run = best
            if from_run:
                _, n = heapq.heappop(run_h[e])
            else:
                _, _, n = heapq.heappop(wait_h[e])
            n.start = st
            free[e] = st + n.cost
            if n.kind == "dma":
                t_begin = max(st + 1500.0, dma_bw_free[0])
                t_end = t_begin + n.nbytes / 300.0
                dma_bw_free[0] = t_end
                n.finish = t_end + 500.0
            else:
                n.finish = st + n.cost
            n.pos = len(order[e])
            order[e].append(n)
            remaining -= 1
            for sc in n.succs:
                lat = self.SYNC_NS if (sc.preds[n] or sc.eng != n.eng) else 0.0
                if sc.eng == n.eng and n.kind != "dma":
                    lat = 60.0 if sc.preds[n] else 0.0
                r = n.finish + lat
                if r > sc.ready:
                    sc.ready = r
                sc.npend -= 1
                if sc.npend == 0:
                    heapq.heappush(wait_h[sc.eng], (sc.ready, sc.id, sc))
        self.order = order
        self.est_ns = max(n.finish for n in self.nodes)

    def emit_plan(self):
        plan = {e: [] for e in self.ENGS}
        pools = {p: dict(n=k, vals=[0] * k, last=[None] * k, rr=0) for p, k in self.npools.items()}
        for e in self.ENGS:
            for n in self.order[e]:
                if n.kind == "dma":
                    p = pools[n.dma[2]]
                    si = p["rr"]
                    p["rr"] = (si + 1) % p["n"]
                    prev = p["last"][si]
                    p["vals"][si] += 16
                    n.tk = (n.dma[2], si, p["vals"][si], prev)
                    p["last"][si] = n
        out_nodes = []
        for e in self.ENGS:
            seen = {}
            pl = plan[e]

            def wait_on(p):
                if p.kind == "dma":
                    pool, si, val, _ = p.tk
                    key = (pool, si)
                    if seen.get(key, -1) >= val:
                        return
                    seen[key] = val
                    pl.append(("wait_dma", pool, si, val))
                else:
                    if seen.get(p.eng, -1) >= p.pos:
                        return
                    seen[p.eng] = p.pos
                    p.inc = True
                    pl.append(("wait_op", p))

            for n in self.order[e]:
                for p, need in n.preds.items():
                    if need:
                        wait_on(p)
                if n.kind == "dma":
                    prev = n.tk[3]
                    if prev is not None:
                        wait_on(prev)
                    if n.is_output:
                        out_nodes.append(n)
                pl.append(("node", n))
        seen = {}
        for n in out_nodes:
            pool, si, val, _ = n.tk
            if seen.get((pool, si), -1) < val:
                seen[(pool, si)] = val
                plan["sp"].append(("wait_dma", pool, si, val))
        for e in self.ENGS:
            c = 0
            for n in self.order[e]:
                if n.kind == "op" and n.inc:
                    c += 1
                    n.num = c
        self.plan = plan

    def replay_one(self, nc, name, h, sems, dma_sems):
        for it in self.plan[name]:
            if it[0] == "wait_op":
                p = it[1]
                h.wait_ge(sems[p.eng], p.num)
            elif it[0] == "wait_dma":
                _, pool, si, val = it
                h.wait_ge(dma_sems[pool][si], val)
            else:
                n = it[1]
                if n.kind == "dma":
                    out_ap, in_ap, pool = n.dma
                    h.dma_start(out=out_ap, in_=in_ap).then_inc(dma_sems[pool][n.tk[1]], 16)
                else:
                    ins = n.fn(h)
                    if n.inc:
                        ins.then_inc(sems[n.eng], 1)


def build(debug=False):
    nc = bass.Bass("TRN2", target_bir_lowering=False)
    K = Tracker()
    K.add_pool("sp", 8)
    K.add_pool("pool", 8)

    def din(name, shape, dt=F32):
        return nc.dram_tensor(name, list(shape), dt, kind="ExternalInput").ap()

    x_d = din("x", [S, D])
    ctx_d = din("ctx", [CTX, D])
    cond_d = din("cond", [128, 16])
    wada_d = din("w_ada", [D, 6 * D])
    badac_d = din("b_ada_col", [128, 48])
    bada_d = din("b_ada", [6 * D])
    gcols_d = din("gcols", [128, 32])
    gq_d = din("g_q", [HD])
    gk_d = din("g_k", [HD])
    gfin_d = din("g_final", [D])
    win_d = din("w_in", [D, 4096])
    wap_d = din("w_ap", [D, D])
    wfp_d = din("w_fp", [512, D])
    wo_d = din("w_o", [D, D])
    wgu_d = din("w_gu", [D, 2 * DFF])
    wdn_d = din("w_dn", [DFF, D])
    ident_d = din("ident", [128, 128], BF16)
    ropec_d = din("rope_c", [128, NKT * 64])
    ropes_d = din("rope_s", [128, NKT * 64])
    cs_d = din("dft_cs", [128, 256], BF16)
    dftl_d = din("dft_l", [16, 128, 4096], BF16)
    out_d = nc.dram_tensor("out", [S, D], F32, kind="ExternalOutput").ap()
    taps = {}
    tap_list = []

    from contextlib import ExitStack
    es = ExitStack()
    SB = es.enter_context(nc.sbuf_tensor("SB", [128, SB_BYTES], U8))
    PS = es.enter_context(nc.psum_tensor("PS", [128, 4096], F32))
    sems = {n: es.enter_context(nc.semaphore("s_" + n)) for n in ("pe", "act", "dve", "pool", "sp")}
    dma_sems = {p: [es.enter_context(nc.semaphore("d_%s%d" % (p, i))) for i in range(8)] for p in ("sp", "pool")}

    esz = {F32: 4, BF16: 2, U8: 1}

    def sb(off, dt, shape, p0=0, p1=128):
        n = int(np.prod(shape))
        nb = n * esz[dt]
        assert off + nb <= SB_BYTES, (off, nb)
        ap = SB[p0:p1, off:off + nb].bitcast(dt)
        if len(shape) == 2:
            ap = ap.rearrange("p (a b) -> p a b", a=shape[0])
        elif len(shape) == 3:
            ap = ap.rearrange("p (a b c) -> p a b c", a=shape[0], b=shape[1])
        elif len(shape) == 4:
            ap = ap.rearrange("p (a b c d) -> p a b c d", a=shape[0], b=shape[1], c=shape[2])
        return View(ap, "sb", off, off + nb)

    def bank(b, nb=1):
        return View(PS[:, b * 512:(b + nb) * 512], "ps", b * 2048, (b + nb) * 2048)

    def bank_bf(b):
        return View(PS[:, b * 512:(b + 1) * 512].bitcast(BF16), "ps", b * 2048, (b + 1) * 2048)

    def sub(v, ap):
        return View(ap, v.sp, v.lo, v.hi)

    def rng(v, ap, elo, ehi, es_):
        return View(ap, v.sp, v.lo + elo * es_, v.lo + ehi * es_)

    def fsz(v):
        n = 1
        for d in v.ap.shape[1:]:
            n *= int(d)
        return n

    def ecost(eng, v, per=1.04):
        if eng == "act":
            return 224.0 + 0.83 * fsz(v)
        if eng == "pool":
            return 300.0 + 2.0 * fsz(v)
        return 120.0 + per * fsz(v)

    def mm(out, lhsT, rhs, start, stop):
        return K.op("pe", lambda h: h.matmul(out.ap, lhsT=lhsT.ap, rhs=rhs.ap, start=start, stop=stop),
                    reads=[lhsT, rhs], writes=[out], cost=12.0 + 0.42 * fsz(rhs))

    def tr(out, in_, ident):
        return K.op("pe", lambda h: h.transpose(out=out.ap, in_=in_.ap, identity=ident.ap),
                    reads=[in_, ident], writes=[out], cost=66.0)

    def act(out, in_, func, scale=1.0, bias=None, accum=None):
        reads = [in_]
        kw = {}
        if isinstance(scale, View):
            reads.append(scale)
            kw["scale"] = scale.ap
        else:
            kw["scale"] = float(scale)
        if isinstance(bias, View):
            reads.append(bias)
            kw["bias"] = bias.ap
        elif bias is not None:
            kw["bias"] = float(bias)
        writes = [out]
        if accum is not None:
            writes.append(accum)
            kw["accum_out"] = accum.ap
        return K.op("act", lambda h: h.activation(out=out.ap, in_=in_.ap, func=func, **kw),
                    reads=reads, writes=writes, cost=ecost("act", in_))

    def tt(eng, out, a, b, op):
        return K.op(eng, lambda h: h.tensor_tensor(out=out.ap, in0=a.ap, in1=b.ap, op=op),
                    reads=[a, b], writes=[out], cost=(1600.0 if op == ALU.pow else ecost(eng, out)))

    def ts(eng, out, a, s1, op0, s2=None, op1=None):
        reads = [a]
        a1 = s1.ap if isinstance(s1, View) else float(s1)
        if isinstance(s1, View):
            reads.append(s1)
        a2 = None
        if s2 is not None:
            a2 = s2.ap if isinstance(s2, View) else float(s2)
            if isinstance(s2, View):
                reads.append(s2)
        if op1 is None:
            return K.op(eng, lambda h: h.tensor_scalar(out=out.ap, in0=a.ap, scalar1=a1, scalar2=None, op0=op0),
                        reads=reads, writes=[out], cost=ecost(eng, out, 0.6))
        return K.op(eng, lambda h: h.tensor_scalar(out=out.ap, in0=a.ap, scalar1=a1, scalar2=a2, op0=op0, op1=op1),
                    reads=reads, writes=[out], cost=ecost(eng, out, 0.6))

    def stt(eng, out, a, s, b, op0, op1):
        reads = [a, b]
        sa = s.ap if isinstance(s, View) else float(s)
        if isinstance(s, View):
            reads.append(s)
        return K.op(eng, lambda h: h.scalar_tensor_tensor(out=out.ap, in0=a.ap, scalar=sa, in1=b.ap, op0=op0, op1=op1),
                    reads=reads, writes=[out], cost=ecost(eng, out))

    def cp(eng, out, in_):
        if eng == "act":
            return K.op("act", lambda h: h.copy(out=out.ap, in_=in_.ap), reads=[in_], writes=[out], cost=ecost("act", in_))
        return K.op(eng, lambda h: h.tensor_copy(out=out.ap, in_=in_.ap), reads=[in_], writes=[out], cost=ecost(eng, out))

    def memset(eng, out, val):
        return K.op(eng, lambda h: h.memset(out.ap, val), writes=[out], cost=ecost(eng, out, 0.5))

    def recip(out, in_):
        return K.op("dve", lambda h: h.reciprocal(out=out.ap, in_=in_.ap), reads=[in_], writes=[out], cost=ecost("dve", out, 6.4))

    def reduce_sum(out, in_):
        return K.op("dve", lambda h: h.tensor_reduce(out=out.ap, in_=in_.ap, axis=AX.X, op=ALU.add),
                    reads=[in_], writes=[out], cost=ecost("dve", in_))

    def load(dst, src_ap, q="sp"):
        return K.dma(q, q, dst.ap, src_ap, writes=[dst], nbytes=(dst.hi - dst.lo) * 128)

    def loadw(dst, src_ap):
        return K.dma("pool", "pool", dst.ap, src_ap, writes=[dst], nbytes=fsz(dst) * 128 * 4)

    def store(dst_ap, src, is_output=True):
        return K.dma("sp", "sp", dst_ap, src.ap, reads=[src], is_output=is_output, nbytes=(src.hi - src.lo) * 128)

    def tap(name, view, dt):
        shp = [int(x) for x in view.ap.shape]
        d = nc.dram_tensor("tap_" + name, shp, dt, kind="ExternalOutput").ap()
        K.dma("sp", "sp", d, view.ap, reads=[view], is_output=True, nbytes=(view.hi - view.lo) * 128)
        tap_list.append("tap_" + name)

    def rstd_from_ms(rstd, ms, tmp, mhalf):
        act(tmp, ms, AF.Ln, bias=eps_col)
        act(rstd, tmp, AF.Exp, scale=-0.5)

    KB = 1024
    HT_O = 0
    AT_O = 32 * KB
    KV_O = 76 * KB
    FY_O = 121 * KB
    QM_O = 137 * KB
    SM_O = 169 * KB
    assert SM_O + 31 * KB <= SB_BYTES

    c_o = SM_O
    ident = sb(c_o, BF16, (128,)); c_o += 256
    cond = sb(c_o, F32, (8, 2)); c_o += 64
    scond = sb(c_o, BF16, (8, 2)); c_o += 32
    badac = sb(c_o, F32, (48,)); c_o += 192
    gcols = sb(c_o, F32, (32,)); c_o += 128
    modraw = sb(c_o, F32, (4, 8, 2)); c_o += 256
    modc = sb(c_o, F32, (6, 8)); c_o += 192
    stats = sb(c_o, F32, (64,)); c_o += 256
    mhalf = sb(c_o, F32, (16,)); c_o += 64
    eps_t = sb(c_o, F32, (16,)); c_o += 64
    gq_bc = sb(c_o, F32, (64,)); c_o += 256
    gk_bc = sb(c_o, F32, (64,)); c_o += 256
    cs_t = sb(c_o, BF16, (256,)); c_o += 512
    scond_bc = sb(c_o, BF16, (8, 128)); c_o += 2048
    GTm = sb(c_o, F32, (1024,)); c_o += 4096
    GTf = sb(c_o, F32, (1024,)); c_o += 4096
    SMF = c_o
    SM_END = SM_O + 31 * KB

    load(ident, ident_d)
    load(cond, cond_d.rearrange("p (a b) -> p a b", a=8))
    load(badac, badac_d)
    load(gcols, gcols_d)
    load(gq_bc, gq_d.partition_broadcast(128))
    load(gk_bc, gk_d.partition_broadcast(128))
    load(cs_t, cs_d)
    memset("dve", mhalf, -0.5)
    memset("dve", eps_t, EPS)
    eps_col = sub(eps_t, eps_t.ap[:, 0:1])
    act(scond, cond, AF.Silu)
    cp("dve", scond_bc, sub(scond, scond.ap[:, :, 0:1].to_broadcast([128, 8, 128])))

    wada_v = wada_d.rearrange("(k p) n -> p k n", p=128)
    ada_slots = [sb(QM_O + i * 8 * KB, BF16, (8, 512)) for i in range(4)]
    slot_i = [0]

    def ada_piece(col0):
        s = ada_slots[slot_i[0] % 4]
        slot_i[0] += 1
        loadw(s, wada_v[:, :, col0:col0 + 512])
        return s

    gmix = sub(gcols, gcols.ap[:, 0:8])
    gffn = sub(gcols, gcols.ap[:, 8:16])
    bgate = sub(gcols, gcols.ap[:, 16:32])

    def ada_cols(vis, pbank, piece_fn):
        pcol = bank(pbank)
        pcol_v = sub(pcol, pcol.ap[:, 0:64].rearrange("p (v c n) -> p v c n", v=4, c=8))
        for vi in vis:
            v = (0, 1, 3, 4)[vi]
            for half in range(2):
                s_ = piece_fn(v * 1024 + half * 512)
                for cc in range(4):
                    c = half * 4 + cc
                    for k in range(8):
                        mm(sub(pcol, pcol_v.ap[:, vi, c, :]),
                           sub(s_, s_.ap[:, k, cc * 128:(cc + 1) * 128]),
                           sub(scond, scond.ap[:, k, :]), k == 0, k == 7)
        for vi in vis:
            v = (0, 1, 3, 4)[vi]
            tt("dve", sub(modraw, modraw.ap[:, vi, :, :]), sub(pcol, pcol_v.ap[:, vi, :, :]),
               sub(badac, badac.ap[:, v * 8:(v + 1) * 8].unsqueeze(2).to_broadcast([128, 8, 2])), ALU.add)

    ada_cols((0, 1), 0, ada_piece)
    stt("dve", sub(modc, modc.ap[:, 0, :]), sub(modraw, modraw.ap[:, 1, :, 0]), 1.0, gmix, ALU.add, ALU.mult)
    cp("dve", sub(modc, modc.ap[:, 1, :]), sub(modraw, modraw.ap[:, 0, :, 0]))
    stt("dve", sub(modc, modc.ap[:, 2, :]), sub(modraw, modraw.ap[:, 1, :, 1]), 1.0, gmix, ALU.add, ALU.mult)
    cp("dve", sub(modc, modc.ap[:, 3, :]), sub(modraw, modraw.ap[:, 0, :, 1]))

    late_slots = [sb(76 * KB + i * 8 * KB, BF16, (8, 512)) for i in range(3)]
    late_i = [0]

    def late_piece(col0):
        s_ = late_slots[late_i[0] % 3]
        late_i[0] += 1
        loadw(s_, wada_v[:, :, col0:col0 + 512])
        return s_

    def ada_late():
        ada_cols((2, 3), 6, late_piece)
        stt("dve", sub(modc, modc.ap[:, 4, :]), sub(modraw, modraw.ap[:, 3, :, 0]), 1.0, gffn, ALU.add, ALU.mult)
        cp("dve", sub(modc, modc.ap[:, 5, :]), sub(modraw, modraw.ap[:, 2, :, 0]))
        for gt, v in ((GTm, 2), (GTf, 5)):
            load(gt, bada_d[v * 1024:(v + 1) * 1024].partition_broadcast(128))
            for half in range(2):
                s_ = late_piece(v * 1024 + half * 512)
                pb = bank(7)
                for k in range(8):
                    mm(pb, sub(scond_bc, scond_bc.ap[:, k, :]), sub(s_, s_.ap[:, k, :]), k == 0, k == 7)
                g_half = rng(gt, gt.ap[:, half * 512:(half + 1) * 512], half * 512, half * 512 + 512, 4)
                tt("dve", g_half, pb, g_half, ALU.add)
        if debug:
            tap("modc", modc, F32)
            tap("GTm", GTm, F32)

    hT = sb(HT_O, BF16, (16, 8, 128))
    qT = sb(QM_O, BF16, (16, 8, 128))
    kT2 = sb(KV_O, BF16, (4, NKT * 128))
    Vaug = sb(KV_O + 18 * KB, BF16, (NKT, 4, 192))
    hcT = sb(QM_O + 28 * KB, BF16, (2, 8, 128))
    Wq0 = sb(AT_O, BF16, (8, 512))
    Wq1 = sb(AT_O + 8 * KB, BF16, (8, 512))
    Wkv = sb(AT_O + 16 * KB, BF16, (8, 512))
    Wf = sb(AT_O + 24 * KB, BF16, (8, 512))
    ropec = sb(AT_O + 32 * KB, F32, (NKT, 64))
    ropes = sb(AT_O + 32 * KB + 4608, F32, (NKT, 64))
    win_v = win_d.rearrange("(k p) n -> p k n", p=128)
    loadw(Wkv, win_v[:, :, 1024:1536])
    loadw(Wq0, win_v[:, :, 0:512])
    loadw(Wq1, win_v[:, :, 512:1024])
    loadw(Wf, win_v[:, :, 1536:2048])
    load(ropec, ropec_d.rearrange("p (a b) -> p a b", a=NKT))
    load(ropes, ropes_d.rearrange("p (a b) -> p a b", a=NKT))
    memset("dve", Vaug, 1.0)

    o = SMF
    xbuf = [sb(o + i * 4096, F32, (1024,)) for i in range(2)]; o += 8192
    junk = sb(o, BF16, (1024,)); o += 2048
    xn = [sb(o + i * 2048, BF16, (1024,)) for i in range(2)]; o += 4096
    kr2 = [sb(o + i * 1024, BF16, (4, 2, 64)) for i in range(2)]; o += 2048
    tabs2 = [sb(o + i * 1024, F32, (4, 64)) for i in range(2)]; o += 2048
    assert o <= SM_END, o
    o = FY_O
    sq = sb(o, F32, (8, 64)); o += 2048
    qn = sb(o, F32, (8, 64)); o += 2048
    uu = sb(o, F32, (8, 64)); o += 2048
    ww = sb(o, F32, (8, 64)); o += 2048
    qr = [sb(o + i * 2048, BF16, (16, 64)) for i in range(2)]; o += 4096
    assert o <= FY_O + 16 * KB

    st_ms = lambda i: sub(stats, stats.ap[:, i:i + 1])

    def hsl(tt_i, k):
        if tt_i < NCT:
            return rng(hcT, hcT.ap[:, tt_i, k, :], (tt_i * 8 + k) * 128, (tt_i * 8 + k + 1) * 128, 2)
        t = tt_i - NCT
        return rng(hT, hT.ap[:, t, k, :], (t * 8 + k) * 128, (t * 8 + k + 1) * 128, 2)

    def P1(tt_i):
        is_ctx = tt_i < NCT
        xb = xbuf[tt_i % 2]
        src = ctx_d[tt_i * 128:(tt_i + 1) * 128, :] if is_ctx else x_d[(tt_i - NCT) * 128:(tt_i - NCT + 1) * 128, :]
        load(xb, src)
        sc = (tt_i % 2) * 3
        act(junk, xb, AF.Square, scale=1.0 / 32.0, accum=st_ms(sc))
        rstd_from_ms(st_ms(sc + 2), st_ms(sc), st_ms(sc + 1), sub(mhalf, mhalf.ap[:, 0:1]))
        xnb = xn[tt_i % 2]
        act(xnb, xb, AF.Copy, scale=st_ms(sc + 2))
        pT = bank_bf(6)
        for k in range(8):
            tr(sub(pT, pT.ap[:, k * 128:(k + 1) * 128]), sub(xnb, xnb.ap[:, k * 128:(k + 1) * 128]), ident)
        gi, si = (2, 3) if is_ctx else (0, 1)
        for k in range(8):
            act(hsl(tt_i, k), sub(pT, pT.ap[:, k * 128:(k + 1) * 128]), AF.Identity,
                scale=sub(modc, modc.ap[:, gi, k:k + 1]), bias=sub(modc, modc.ap[:, si, k:k + 1]))

    def MM(tt_i):
        base = (tt_i % 2) * 3
        pkv = bank(base)
        for k in range(8):
            mm(pkv, hsl(tt_i, k), sub(Wkv, Wkv.ap[:, k, :]), k == 0, k == 7)
        if tt_i < NCT:
            return
        for qb, Wq in enumerate((Wq0, Wq1)):
            pq = bank(base + 1 + qb)
            for k in range(8):
                mm(pq, hsl(tt_i, k), sub(Wq, Wq.ap[:, k, :]), k == 0, k == 7)

    blk_i = [0]

    def make_block(ps_v, nh, tab_c, tab_s, tabv, dsts):
        par = blk_i[0] % 2
        blk_i[0] += 1
        b0 = 8 + par * 24
        p3 = sub(ps_v, ps_v.ap.rearrange("p (h d) -> p h d", h=nh))
        ss = sub(stats, stats.ap[:, b0:b0 + nh])
        tmp = sub(stats, stats.ap[:, b0 + 8:b0 + 8 + nh])
        rs = sub(stats, stats.ap[:, b0 + 16:b0 + 16 + nh])

        def X():
            sq3 = sub(sq, sq.ap[:, 0:nh, :])
            act(sq3, p3, AF.Square, scale=0.125)
            reduce_sum(ss, sq3)
            rstd_from_ms(rs, ss, tmp, sub(mhalf, mhalf.ap[:, 0:nh]))

        def Y():
            qn3 = sub(qn, qn.ap[:, 0:nh, :])
            tt("dve", qn3, p3, sub(rs, rs.ap.unsqueeze(2).to_broadcast([128, nh, 64])), ALU.mult)
            u3 = sub(uu, uu.ap[:, 0:nh, :])
            w3 = sub(ww, ww.ap[:, 0:nh, :])
            tt("dve", u3, qn3, sub(tabv, tab_c.unsqueeze(1).to_broadcast([128, nh, 64])), ALU.mult)
            tt("dve", w3, qn3, sub(tabv, tab_s.unsqueeze(1).to_broadcast([128, nh, 64])), ALU.mult)
            for d in dsts:
                tt("dve", sub(d, d.ap[:, :, 0:32]), sub(uu, u3.ap[:, :, 0:32]), sub(ww, w3.ap[:, :, 32:64]), ALU.subtract)
                tt("dve", sub(d, d.ap[:, :, 32:64]), sub(uu, u3.ap[:, :, 32:64]), sub(ww, w3.ap[:, :, 0:32]), ALU.add)
        return X, Y

    pending = []

    def run_block(X, Y, after=None):
        X()
        if pending:
            py, pafter = pending.pop()
            py()
            if pafter:
                pafter()
        pending.append((Y, after))

    def POST_a(tt_i):
        is_ctx = tt_i < NCT
        t = tt_i - NCT
        base = (tt_i % 2) * 3
        pkv = bank(base)
        tabv = tabs2[tt_i % 2]
        krb = kr2[tt_i % 2]
        rc = sub(ropec, ropec.ap[:, tt_i, :])
        rs_ = sub(ropes, ropes.ap[:, tt_i, :])
        if not is_ctx:
            stt("dve", sub(tabv, tabv.ap[:, 0, :]), rc, 0.125, gq_bc, ALU.mult, ALU.mult)
            stt("dve", sub(tabv, tabv.ap[:, 1, :]), rs_, 0.125, gq_bc, ALU.mult, ALU.mult)
        tt("dve", sub(tabv, tabv.ap[:, 2, :]), rc, gk_bc, ALU.mult)
        tt("dve", sub(tabv, tabv.ap[:, 3, :]), rs_, gk_bc, ALU.mult)
        kr_a = sub(krb, krb.ap[:, :, 0, :])
        kr_b = sub(krb, krb.ap[:, :, 1, :])
        Xk, Yk = make_block(sub(pkv, pkv.ap[:, 0:256]), 4, tabv.ap[:, 2, :], tabv.ap[:, 3, :], tabv, [kr_a, kr_b])

        def Xk2():
            Xk()
            vdst = rng(Vaug, Vaug.ap[:, tt_i, :, 64:128], tt_i * 768, (tt_i + 1) * 768, 2)
            cp("act", vdst, sub(pkv, pkv.ap[:, 256:512].rearrange("p (h d) -> p h d", h=4)))

        def Fk():
            pk = bank_bf(7)
            for kh in range(4):
                tr(sub(pk, pk.ap[:, kh * 128:(kh + 1) * 128]),
                   sub(krb, krb.ap[:, kh, :, :].rearrange("p a d -> p (a d)")), ident)
            kdst = sub(kT2, kT2.ap[:, :, tt_i * 128:(tt_i + 1) * 128])
            cp("act", kdst, sub(pk, pk.ap[:, 0:512].rearrange("p (h t) -> p h t", h=4)))

        run_block(Xk2, Yk, Fk)

    def POST_b(tt_i):
        if tt_i < NCT:
            return
        t = tt_i - NCT
        base = (tt_i % 2) * 3
        tabv = tabs2[tt_i % 2]
        qrb = qr[t % 2]

        def Fq():
            pqt = bank_bf(7)
            for c in range(8):
                tr(sub(pqt, pqt.ap[:, c * 128:(c + 1) * 128]),
                   sub(qrb, qrb.ap[:, 2 * c:2 * c + 2, :].rearrange("p a d -> p (a d)")), ident)
            qdst = rng(qT, qT.ap[:, t, :, :].rearrange("p c t -> p (c t)"), t * 1024, (t + 1) * 1024, 2)
            cp("act", qdst, pqt)

        for qb in range(2):
            pq = bank(base + 1 + qb)
            qd = rng(qrb, qrb.ap[:, qb * 8:(qb + 1) * 8, :], qb * 512, (qb + 1) * 512, 2)
            Xq, Yq = make_block(pq, 8, tabv.ap[:, 0, :], tabv.ap[:, 1, :], tabv, [qd])
            run_block(Xq, Yq, Fq if qb == 1 else None)

    for s_ in range(NKT + 2):
        if s_ < NKT:
            P1(s_)
        if s_ >= 2:
            POST_a(s_ - 2)
        if 1 <= s_ <= NKT:
            MM(s_ - 1)
        if s_ >= 2:
            POST_b(s_ - 2)
    while pending:
        py, pafter = pending.pop()
        py()
        if pafter:
            pafter()
    pb_i = [0]

    def next_bank():
        b = pb_i[0] % 8
        pb_i[0] += 1
        return b

    if debug:
        tap("hT", hT, BF16)
        tap("qT", qT, BF16)
        tap("kT2", kT2, BF16)
        tap("Vaug", Vaug, BF16)

    FT = sb(FY_O, BF16, (4, 2048))
    for fg in range(4):
        for tb in range(4):
            b = next_bank()
            pf = bank(b)
            rhs_all = lambda k: rng(hT, hT.ap[:, tb * 4:(tb + 1) * 4, k, :], tb * 4096, (tb + 1) * 4096, 2)
            for k in range(8):
                mm(pf, sub(Wf, Wf.ap[:, k, fg * 128:(fg + 1) * 128]), rhs_all(k), k == 0, k == 7)
            fdst = rng(FT, FT.ap[:, fg, tb * 512:(tb + 1) * 512], fg * 2048 + tb * 512, fg * 2048 + (tb + 1) * 512, 2)
            cp("act" if (fg + tb) % 2 else "dve", fdst, pf)
    if debug:
        tap("FT", FT, BF16)

    attnT = sb(AT_O, BF16, (8, 2048))
    PT = [sb(SMF + i * 2048, BF16, (1024,)) for i in range(3)]
    accsb = [sb(SMF + 6144 + i * 4096, F32, (1024,)) for i in range(2)]
    rden1 = sb(SMF + 6144 + 8192, F32, (1024,))
    assert SMF + 6144 + 12288 <= SM_END
    steps = [(c, qb, kt) for c in range(NH // 2) for qb in range(4) for kt in range(NKT)]

    def s_bank(i):
        return bank((i % 3) * 2, 2)

    def acc_bank(j):
        return bank(6, 2)

    def emit_qk(i):
        c, qb, kt = steps[i]
        kh = c // 2
        sbk = s_bank(i)
        t0 = qb * 4
        for half in range(2):
            p0, p1 = half * 64, half * 64 + 64
            lhsT = View(kT2.ap[p0:p1, kh, kt * 128:(kt + 1) * 128], "sb",
                        kT2.lo + (kh * NKT * 128 + kt * 128) * 2, kT2.lo + (kh * NKT * 128 + kt * 128 + 128) * 2)
            rhs = View(qT.ap[p0:p1, t0:t0 + 4, c, :], "sb", qT.lo + t0 * 2048, qT.lo + (t0 + 4) * 2048)
            mm(View(sbk.ap[:, half * 512:(half + 1) * 512], "ps", sbk.lo + half * 2048, sbk.lo + (half + 1) * 2048),
               lhsT, rhs, True, True)

    emit_qk(0)
    emit_qk(1)
    for i in range(len(steps)):
        c, qb, kt = steps[i]
        kh = c // 2
        if i + 2 < len(steps):
            emit_qk(i + 2)
        pt = PT[i % 3]
        act(pt, s_bank(i), AF.Exp)
        j = i // NKT
        acc = acc_bank(j)
        for half in range(2):
            c0 = 64 if half == 0 else 0
            va = View(Vaug.ap[:, kt, kh, c0:c0 + 128], "sb", Vaug.lo + (kt * 4 + kh) * 384, Vaug.lo + (kt * 4 + kh + 1) * 384)
            mm(View(acc.ap[:, half * 512:(half + 1) * 512], "ps", acc.lo + half * 2048, acc.lo + (half + 1) * 2048),
               va, sub(pt, pt.ap[:, half * 512:(half + 1) * 512]), kt == 0, kt == NKT - 1)
        if kt == NKT - 1:
            rd = rden1
            asb = accsb[j % 2]
            cp("dve", asb, acc)
            for half in range(2):
                ab = View(asb.ap[:, half * 512:(half + 1) * 512], "sb", asb.lo + half * 2048, asb.lo + (half + 1) * 2048)
                n0, d0 = (0, 64) if half == 0 else (64, 0)
                num = View(ab.ap[n0:n0 + 64, :], "sb", ab.lo, ab.hi)
                den = View(ab.ap[d0:d0 + 64, :], "sb", ab.lo, ab.hi)
                rdv = View(rd.ap[n0:n0 + 64, half * 512:(half + 1) * 512], "sb", rd.lo + half * 2048, rd.lo + (half + 1) * 2048)
                dst = View(attnT.ap[n0:n0 + 64, c, qb * 512:(qb + 1) * 512], "sb",
                           attnT.lo + (c * 2048 + qb * 512) * 2, attnT.lo + (c * 2048 + qb * 512 + 512) * 2)
                recip(rdv, den)
                tt("dve", dst, num, rdv, ALU.mult)
    if debug:
        tap("attnT", attnT, BF16)

    G = sb(QM_O, BF16, (16, 4, 256))
    YT = sb(FY_O, BF16, (4, 2048))
    for lt in range(NT):
        for pr in range(2):
            b = next_bank()
            pg = bank(b)
            for a in range(2):
                fg = pr * 2 + a
                mm(View(pg.ap[:, a * 256:(a + 1) * 256], "ps", pg.lo + a * 1024, pg.lo + (a + 1) * 1024),
                   rng(FT, FT.ap[:, fg, lt * 128:(lt + 1) * 128], fg * 2048 + lt * 128, fg * 2048 + lt * 128 + 128, 2),
                   cs_t, True, True)
            gd = rng(G, G.ap[:, lt, pr * 2:pr * 2 + 2, :].rearrange("p a c -> p (a c)"),
                     (lt * 4 + pr * 2) * 256, (lt * 4 + pr * 2 + 2) * 256, 2)
            cp("act" if (lt + pr) % 2 else "dve", gd, pg)
    dft_slots = [sb(KV_O + i * 8 * KB, BF16, (4, 2, 512)) for i in range(3)]
    di = 0
    for jb in range(4):
        banks = [bank((jb % 2) * 4 + fg) for fg in range(4)]
        for lg in range(4):
            ds = dft_slots[di % 3]
            load(ds, dftl_d[jb * 4 + lg].rearrange("p (a b c) -> p a b c", a=4, b=2))
            di += 1
            for fg in range(4):
                for a in range(4):
                    lt = lg * 4 + a
                    for cs_i in range(2):
                        mm(banks[fg],
                           rng(G, G.ap[:, lt, fg, cs_i * 128:(cs_i + 1) * 128],
                               (lt * 4 + fg) * 256 + cs_i * 128, (lt * 4 + fg) * 256 + cs_i * 128 + 128, 2),
                           sub(ds, ds.ap[:, a, cs_i, :]),
                           lt == 0 and cs_i == 0, lt == NT - 1 and cs_i == 1)
        for fg in range(4):
            yd = rng(YT, YT.ap[:, fg, jb * 512:(jb + 1) * 512], fg * 2048 + jb * 512, fg * 2048 + jb * 512 + 512, 2)
            cp("act" if fg % 2 else "dve", yd, banks[fg])
    if debug:
        tap("YT", YT, BF16)

    MT = sb(QM_O, BF16, (8, 2048))
    wslots = [sb(100 * KB + i * 6 * KB, BF16, (24, 128)) for i in range(2)]
    wfps = [sb(112 * KB + i * KB, BF16, (4, 128)) for i in range(2)]
    gsl = [sb(SMF + i * 2048, F32, (512,)) for i in range(4)]
    wap_v = wap_d.rearrange("(k p) n -> p k n", p=128)
    wfp_v = wfp_d.rearrange("(k p) n -> p k n", p=128)
    for j in range(8):
        wsl = wslots[j % 2]
        wf_ = wfps[j % 2]
        loadw(rng(wsl, wsl.ap[:, 0:8, :], 0, 1024, 2), wap_v[:, :, j * 128:(j + 1) * 128])
        loadw(rng(wsl, wsl.ap[:, 8:16, :], 1024, 2048, 2), win_v[:, :, 2048 + j * 128:2048 + (j + 1) * 128])
        loadw(rng(wsl, wsl.ap[:, 16:24, :], 2048, 3072, 2), win_v[:, :, 3072 + j * 128:3072 + (j + 1) * 128])
        loadw(wf_, wfp_v[:, :, j * 128:(j + 1) * 128])
        for tb in range(4):
            base = ((j * 4 + tb) % 2) * 4
            pA, pF, pGa, pGf = bank(base), bank(base + 1), bank(base + 2), bank(base + 3)
            hsl4 = lambda k: rng(hT, hT.ap[:, tb * 4:(tb + 1) * 4, k, :], tb * 4096, (tb + 1) * 4096, 2)
            for k in range(8):
                mm(pGa, rng(wsl, wsl.ap[:, 8 + k, :], (8 + k) * 128, (9 + k) * 128, 2), hsl4(k), k == 0, k == 7)
            for k in range(8):
                mm(pGf, rng(wsl, wsl.ap[:, 16 + k, :], (16 + k) * 128, (17 + k) * 128, 2), hsl4(k), k == 0, k == 7)
            for k in range(8):
                mm(pA, rng(wsl, wsl.ap[:, k, :], k * 128, (k + 1) * 128, 2),
                   rng(attnT, attnT.ap[:, k, tb * 512:(tb + 1) * 512], k * 2048 + tb * 512, k * 2048 + tb * 512 + 512, 2),
                   k == 0, k == 7)
            for k in range(4):
                mm(pF, sub(wf_, wf_.ap[:, k, :]),
                   rng(YT, YT.ap[:, k, tb * 512:(tb + 1) * 512], k * 2048 + tb * 512, k * 2048 + tb * 512 + 512, 2),
                   k == 0, k == 3)
            act(gsl[0], pGa, AF.Sigmoid, bias=sub(bgate, bgate.ap[:, j:j + 1]))
            act(gsl[1], pGf, AF.Sigmoid, bias=sub(bgate, bgate.ap[:, 8 + j:9 + j]))
            tt("dve", gsl[2], pA, gsl[0], ALU.mult)
            tt("dve", gsl[3], pF, gsl[1], ALU.mult)
            md = rng(MT, MT.ap[:, j, tb * 512:(tb + 1) * 512], j * 2048 + tb * 512, j * 2048 + tb * 512 + 512, 2)
            tt("dve", md, gsl[2], gsl[3], ALU.add)
    ada_late()
    if debug:
        tap("MT", MT, BF16)

    X1 = sb(AT_O, F32, (16, 1024))
    Wo = sb(FY_O, BF16, (8, 1024))
    h2T = sb(HT_O, BF16, (16, 8, 128))
    wo_v = wo_d.rearrange("(k p) n -> p k n", p=128)
    loadw(rng(Wo, Wo.ap[:, 0:4, :], 0, 4096, 2), wo_v[:, 0:4, :])
    loadw(rng(Wo, Wo.ap[:, 4:8, :], 4096, 8192, 2), wo_v[:, 4:8, :])
    o = SMF
    xbuf = [sb(o + i * 4096, F32, (1024,)) for i in range(2)]; o += 8192
    junk = sb(o, BF16, (1024,)); o += 2048
    xn = [sb(o + i * 2048, BF16, (1024,)) for i in range(2)]; o += 4096
    tmpf = sb(o, F32, (1024,)); o += 4096
    assert o <= SM_END
    for t in range(NT):
        xb = xbuf[t % 2]
        load(xb, x_d[t * 128:(t + 1) * 128, :])
        x1t = rng(X1, X1.ap[:, t, :], t * 1024, (t + 1) * 1024, 4)
        for nb in range(2):
            b = next_bank()
            po = bank(b)
            for j in range(8):
                mm(po, rng(MT, MT.ap[:, j, t * 128:(t + 1) * 128], j * 2048 + t * 128, j * 2048 + t * 128 + 128, 2),
                   rng(Wo, Wo.ap[:, j, nb * 512:(nb + 1) * 512], j * 1024 + nb * 512, j * 1024 + nb * 512 + 512, 2),
                   j == 0, j == 7)
            tmh = rng(tmpf, tmpf.ap[:, nb * 512:(nb + 1) * 512], nb * 512, nb * 512 + 512, 4)
            tt("dve", tmh, po, rng(GTm, GTm.ap[:, nb * 512:(nb + 1) * 512], nb * 512, nb * 512 + 512, 4), ALU.mult)
            tt("dve", rng(X1, X1.ap[:, t, nb * 512:(nb + 1) * 512], t * 1024 + nb * 512, t * 1024 + nb * 512 + 512, 4),
               tmh, rng(xb, xb.ap[:, nb * 512:(nb + 1) * 512], nb * 512, nb * 512 + 512, 4), ALU.add)
        sc = (t % 2) * 3
        act(junk, x1t, AF.Square, scale=1.0 / 32.0, accum=st_ms(sc))
        rstd_from_ms(st_ms(sc + 2), st_ms(sc), st_ms(sc + 1), sub(mhalf, mhalf.ap[:, 0:1]))
        xnb = xn[t % 2]
        act(xnb, x1t, AF.Copy, scale=st_ms(sc + 2))
        bT = next_bank()
        pT = bank_bf(bT)
        for k in range(8):
            tr(sub(pT, pT.ap[:, k * 128:(k + 1) * 128]), sub(xnb, xnb.ap[:, k * 128:(k + 1) * 128]), ident)
        for k in range(8):
            dst = rng(h2T, h2T.ap[:, t, k, :], (t * 8 + k) * 128, (t * 8 + k + 1) * 128, 2)
            act(dst, sub(pT, pT.ap[:, k * 128:(k + 1) * 128]), AF.Identity,
                scale=sub(modc, modc.ap[:, 4, k:k + 1]), bias=sub(modc, modc.ap[:, 5, k:k + 1]))
    if debug:
        tap("X1", X1, F32)
        tap("h2T", h2T, BF16)

    wgu_slots = [sb(96 * KB + i * 8 * KB, BF16, (8, 2, 256)) for i in range(2)]
    Wd = sb(112 * KB, BF16, (NFF, 1024))
    actA = sb(156 * KB, BF16, (12, 512))
    actB = sb(SMF, BF16, (10, 512))
    gfin = sb(GTm.lo, F32, (1024,))
    junk2 = sb(scond_bc.lo, BF16, (1024,))
    sg = [sb(SMF + 10 * KB + i * 2048, F32, (512,)) for i in range(2)]
    tm2 = [sb(SMF + 14 * KB + i * 2048, F32, (512,)) for i in range(2)]
    assert SMF + 18 * KB <= SM_END

    def act_i(i, c0, c1):
        if i < 12:
            return rng(actA, actA.ap[:, i, c0:c1], i * 512 + c0, i * 512 + c1, 2)
        return rng(actB, actB.ap[:, i - 12, c0:c1], (i - 12) * 512 + c0, (i - 12) * 512 + c1, 2)

    wdn_v = wdn_d.rearrange("(k p) n -> p k n", p=128)
    wgu_v = wgu_d.rearrange("(k p) n -> p k n", p=128)
    load(gfin, gfin_d.partition_broadcast(128))
    pieces = [(tq, i) for tq in range(4) for i in range(NFF)]
    NG = len(pieces) // 2

    def load_group(gi):
        tq, i = pieces[gi * 2]
        sl = wgu_slots[gi % 2]
        loadw(sub(sl, sl.ap[:, :, 0, :]), wgu_v[:, :, i * 128:(i + 2) * 128])
        loadw(sub(sl, sl.ap[:, :, 1, :]), wgu_v[:, :, DFF + i * 128:DFF + (i + 2) * 128])

    load_group(0)
    load_group(1)
    for c4 in range(2):
        loadw(rng(Wd, Wd.ap[:, c4 * 11:(c4 + 1) * 11, :], c4 * 11 * 1024, (c4 + 1) * 11 * 1024, 2),
              wdn_v[:, c4 * 11:(c4 + 1) * 11, :])
    for pi, (tq, i) in enumerate(pieces):
        sl = wgu_slots[(pi // 2) % 2]
        o2 = (pi % 2) * 128
        pg_, pu_ = bank((pi % 2) * 2), bank((pi % 2) * 2 + 1)
        rhs4 = lambda k: rng(h2T, h2T.ap[:, tq * 4:(tq + 1) * 4, k, :], tq * 4096, (tq + 1) * 4096, 2)
        for k in range(8):
            mm(pg_, sub(sl, sl.ap[:, k, 0, o2:o2 + 128]), rhs4(k), k == 0, k == 7)
        for k in range(8):
            mm(pu_, sub(sl, sl.ap[:, k, 1, o2:o2 + 128]), rhs4(k), k == 0, k == 7)
        if pi % 2 == 1 and pi // 2 + 2 < NG:
            load_group(pi // 2 + 2)
        sgi = sg[pi % 2]
        act(sgi, pg_, AF.Silu)
        tt("dve", act_i(i, 0, 512), pu_, sgi, ALU.mult)
        if i != NFF - 1:
            continue
        for t4 in range(4):
            t = tq * 4 + t4
            x1t = rng(X1, X1.ap[:, t, :], t * 1024, (t + 1) * 1024, 4)
            for nb in range(2):
                pd = bank(4 + ((t4 * 2 + nb) % 4))
                for ii in range(NFF):
                    mm(pd, act_i(ii, t4 * 128, (t4 + 1) * 128),
                       rng(Wd, Wd.ap[:, ii, nb * 512:(nb + 1) * 512], ii * 1024 + nb * 512, ii * 1024 + nb * 512 + 512, 2),
                       ii == 0, ii == NFF - 1)
                tmh = tm2[nb]
                tt("dve", tmh, pd, rng(GTf, GTf.ap[:, nb * 512:(nb + 1) * 512], nb * 512, nb * 512 + 512, 4), ALU.mult)
                x1h = rng(X1, X1.ap[:, t, nb * 512:(nb + 1) * 512], t * 1024 + nb * 512, t * 1024 + nb * 512 + 512, 4)
                tt("dve", x1h, tmh, x1h, ALU.add)
            sc = (t % 2) * 3
            act(junk2, x1t, AF.Square, scale=1.0 / 32.0, accum=st_ms(sc))
            rstd_from_ms(st_ms(sc + 2), st_ms(sc), st_ms(sc + 1), sub(mhalf, mhalf.ap[:, 0:1]))
            stt("dve", x1t, x1t, st_ms(sc + 2), gfin, ALU.mult, ALU.mult)
            store(out_d[t * 128:(t + 1) * 128, :], x1t)

    K.schedule()
    K.emit_plan()

    with nc.Block() as block:
        @block.tensor
        def _(h):
            K.replay_one(nc, "pe", h, sems, dma_sems)

        @block.scalar
        def _(h):
            K.replay_one(nc, "act", h, sems, dma_sems)

        @block.vector
        def _(h):
            K.replay_one(nc, "dve", h, sems, dma_sems)

        @block.gpsimd
        def _(h):
            K.replay_one(nc, "pool", h, sems, dma_sems)

        @block.sync
        def _(h):
            K.replay_one(nc, "sp", h, sems, dma_sems)
    es.close()
    return nc, tap_list


_CONST = {}


def _consts():
    if _CONST:
        return _CONST
    bf = ml_dtypes.bfloat16
    _CONST["ident"] = np.eye(128, dtype=np.float32).astype(bf)
    inv = (10000.0 ** (-np.arange(0, 32, 2, dtype=np.float32) / np.float32(32))).astype(np.float32)
    tpos = np.arange(S)
    t_row = (tpos // 64).astype(np.float32)
    t_col = (tpos % 64).astype(np.float32)
    ang = np.concatenate([t_row[:, None] * inv[None, :], t_col[:, None] * inv[None, :]], axis=-1).astype(np.float32)
    cos = np.cos(ang).astype(np.float32)
    sin = np.sin(ang).astype(np.float32)
    rc = np.ones((NKT, 128, 64), np.float32)
    rs = np.zeros((NKT, 128, 64), np.float32)
    rc[NCT:] = np.concatenate([cos, cos], -1).reshape(NT, 128, 64)
    rs[NCT:] = np.concatenate([sin, sin], -1).reshape(NT, 128, 64)
    _CONST["rope_c"] = np.ascontiguousarray(rc.transpose(1, 0, 2).reshape(128, NKT * 64))
    _CONST["rope_s"] = np.ascontiguousarray(rs.transpose(1, 0, 2).reshape(128, NKT * 64))
    c = np.arange(128)
    a = 2.0 * np.pi * ((c[:, None] * c[None, :]) % 128) / 128.0
    _CONST["dft_cs"] = (np.concatenate([np.cos(a), np.sin(a)], 1) / 512.0).astype(np.float32).astype(bf)
    l = np.arange(S)
    m = (l[:, None] * l[None, :]) % S
    ang2 = 2.0 * np.pi * m / float(S)
    CL = np.cos(ang2).astype(np.float32).astype(bf)
    SL = (-np.sin(ang2)).astype(np.float32).astype(bf)
    both = np.stack([CL, SL], 0)
    both = both.reshape(2, 4, 4, 128, 4, 512)
    both = both.transpose(4, 1, 3, 2, 0, 5)
    _CONST["dft_l"] = np.ascontiguousarray(both).reshape(16, 128, 4096)
    return _CONST


def _col(v, n):
    return np.ascontiguousarray(np.asarray(v, np.float32).reshape(n, 128).T)


def _prep(inputs):
    f = lambda a: np.ascontiguousarray(np.asarray(a, dtype=np.float32))
    cst = _consts()
    shared = {
        "w_ada": f(inputs["w_ada"][0]),
        "b_ada_col": _col(inputs["b_ada"][0], 48),
        "b_ada": f(inputs["b_ada"][0]),
        "gcols": np.ascontiguousarray(np.concatenate([_col(inputs["g_norm_mix"][0], 8), _col(inputs["g_norm_ffn"][0], 8),
                                                      _col(inputs["b_gate"][0], 16)], axis=1)),
        "g_q": f(inputs["g_q"][0]),
        "g_k": f(inputs["g_k"][0]),
        "g_final": f(inputs["g_final"]),
        "w_in": f(inputs["w_in"][0]),
        "w_ap": f(inputs["w_attn_proj"][0]),
        "w_fp": f(inputs["w_fourier_proj"][0]),
        "w_o": f(inputs["w_o"][0]),
        "w_gu": f(inputs["w_gate_up"][0]),
        "w_dn": f(inputs["w_down"][0]),
    }
    shared.update(cst)
    x = f(inputs["x"])
    ctx = f(inputs["ctx"])
    c = f(inputs["c"])
    cc = _col(inputs["c_ctx"], 8)
    maps = []
    for b in range(8):
        m = dict(shared)
        m["x"] = x[b]
        m["ctx"] = ctx[b]
        cond = np.stack([_col(c[b], 8), cc], axis=-1)
        m["cond"] = np.ascontiguousarray(cond.reshape(128, 16))
        maps.append(m)
    return maps


_NC = {}


def kernel(**inputs):
    if "nc" not in _NC:
        _NC["nc"] = build(False)[0]
    maps = _prep(inputs)
    res = run_bass_kernel_spmd(_NC["nc"], maps, core_ids=list(range(8)))
    out = np.stack([np.asarray(r["out"], dtype=np.float32) for r in res.results], axis=0)
    return out
```

```python
import numpy as np
import ml_dtypes
import concourse.bass as bass
import concourse.mybir as mybir
from concourse.bass_utils import run_bass_kernel_spmd

F32 = mybir.dt.float32
BF16 = mybir.dt.bfloat16
U8 = mybir.dt.uint8
AF = mybir.ActivationFunctionType
ALU = mybir.AluOpType
AX = mybir.AxisListType

D = 1024
S = 2048
CTX = 256
NT = S // 128
NCT = CTX // 128
NKT = NT + NCT
HD = 64
NH = 16
NKV = 4
DFF = 2816
NFF = DFF // 128
EPS = 1e-6
SB_BYTES = 200 * 1024

DEBUG_TAPS = []


class View:
    __slots__ = ("ap", "sp", "lo", "hi")

    def __init__(self, ap, sp, lo, hi):
        self.ap, self.sp, self.lo, self.hi = ap, sp, lo, hi


class _Node:
    __slots__ = ("id", "eng", "kind", "fn", "cost", "preds", "succs", "npend", "ready", "start", "finish",
                 "pos", "inc", "num", "dma", "nbytes", "is_output", "tk", "bl")

    def __init__(self, id_, eng, kind, fn, cost):
        self.id, self.eng, self.kind, self.fn, self.cost = id_, eng, kind, fn, cost
        self.preds = {}
        self.succs = []
        self.npend = 0
        self.ready = 0.0
        self.start = self.finish = 0.0
        self.pos = -1
        self.inc = False
        self.num = None
        self.dma = None
        self.nbytes = 0
        self.is_output = False
        self.tk = None
        self.bl = 0.0


class _Rec:
    __slots__ = ("lo", "hi", "kind", "node")

    def __init__(self, lo, hi, kind, node):
        self.lo, self.hi, self.kind, self.node = lo, hi, kind, node


class Tracker:
    ENGS = ("pe", "act", "dve", "pool", "sp")
    SYNC_NS = 250.0

    def __init__(self):
        self.nodes = []
        self.phase = "init"
        self.phase_of = {}
        self.recs = {"sb": [], "ps": []}
        self.npools = {}
        self.order = {}

    def add_pool(self, name, k):
        self.npools[name] = k

    def _edges(self, n, reads, writes):
        is_dma = n.kind == "dma"
        for v in reads:
            for r in self.recs[v.sp]:
                if r.kind == "w" and r.lo < v.hi and v.lo < r.hi:
                    self._edge(r.node, n, is_dma)
        for v in writes:
            for r in self.recs[v.sp]:
                if r.lo < v.hi and v.lo < r.hi:
                    self._edge(r.node, n, is_dma)

    def _edge(self, p, n, is_dma):
        if p is n:
            return
        need = True
        if not is_dma and p.kind != "dma" and p.eng == n.eng and n.eng == "pe":
            need = False
        if p in n.preds:
            n.preds[p] = n.preds[p] or need
        else:
            n.preds[p] = need

    def _record(self, n, reads, writes):
        for v in writes:
            lst = self.recs[v.sp]
            lst[:] = [r for r in lst if not (v.lo <= r.lo and r.hi <= v.hi)]
            lst.append(_Rec(v.lo, v.hi, "w", n))
        for v in reads:
            self.recs[v.sp].append(_Rec(v.lo, v.hi, "r", n))

    def op(self, engname, fn, reads=(), writes=(), cost=300.0):
        n = _Node(len(self.nodes), engname, "op", fn, cost)
        self.phase_of[n.id] = self.phase
        self._edges(n, reads, writes)
        self._record(n, reads, writes)
        self.nodes.append(n)
        return n

    def dma(self, qname, pool, out_ap, in_ap, reads=(), writes=(), is_output=False, nbytes=0):
        n = _Node(len(self.nodes), qname, "dma", None, 1000.0 if qname == "pool" else 100.0)
        n.dma = (out_ap, in_ap, pool)
        self.phase_of[n.id] = self.phase
        n.nbytes = nbytes
        n.is_output = is_output
        self._edges(n, reads, writes)
        self._record(n, reads, writes)
        self.nodes.append(n)
        return n

    def schedule(self):
        import heapq
        for n in self.nodes:
            n.npend = len(n.preds)
            for p in n.preds:
                p.succs.append(n)
        for n in reversed(self.nodes):
            b = 0.0
            for sc in n.succs:
                if sc.bl > b:
                    b = sc.bl
            n.bl = b + (n.cost if n.kind != "dma" else 2000.0 + n.nbytes / 300.0)
        free = {e: 0.0 for e in self.ENGS}
        wait_h = {e: [] for e in self.ENGS}
        run_h = {e: [] for e in self.ENGS}
        order = {e: [] for e in self.ENGS}
        dma_bw_free = [0.0]
        for n in self.nodes:
            if n.npend == 0:
                heapq.heappush(wait_h[n.eng], (0.0, n.id, n))
        remaining = len(self.nodes)
        while remaining:
            best = None
            for e in self.ENGS:
                wh, rh = wait_h[e], run_h[e]
                while wh and wh[0][0] <= free[e]:
                    _, i, nd = heapq.heappop(wh)
                    heapq.heappush(rh, (-nd.bl, i, nd))
                if rh:
                    cand = (free[e], rh[0][0], e, True)
                elif wh:
                    cand = (wh[0][0], wh[0][1], e, False)
                else:
                    continue
                if best is None or cand[:2] < best[:2]:
                    best = cand
            st, _, e, from_run = best
            if from_run:
                _, _, n = heapq.heappop(run_h[e])
            else:
                _, _, n = heapq.heappop(wait_h[e])
            n.start = st
            free[e] = st + n.cost
            if n.kind == "dma":
                t_begin = max(st + 1500.0, dma_bw_free[0])
                t_end = t_begin + n.nbytes / 300.0
                dma_bw_free[0] = t_end
                n.finish = t_end + 500.0
            else:
                n.finish = st + n.cost
            n.pos = len(order[e])
            order[e].append(n)
            remaining -= 1
            for sc in n.succs:
                lat = self.SYNC_NS if (sc.preds[n] or sc.eng != n.eng) else 0.0
                if sc.eng == n.eng and n.kind != "dma":
                    lat = 60.0 if sc.preds[n] else 0.0
                r = n.finish + lat
                if r > sc.ready:
                    sc.ready = r
                sc.npend -= 1
                if sc.npend == 0:
                    heapq.heappush(wait_h[sc.eng], (sc.ready, sc.id, sc))
        self.order = order
        self.est_ns = max(n.finish for n in self.nodes)

    def emit_plan(self):
        plan = {e: [] for e in self.ENGS}
        pools = {p: dict(n=k, vals=[0] * k, last=[None] * k, rr=0) for p, k in self.npools.items()}
        for e in self.ENGS:
            for n in self.order[e]:
                if n.kind == "dma":
                    p = pools[n.dma[2]]
                    si = p["rr"]
                    p["rr"] = (si + 1) % p["n"]
                    prev = p["last"][si]
                    p["vals"][si] += 16
                    n.tk = (n.dma[2], si, p["vals"][si], prev)
                    p["last"][si] = n
        out_nodes = []
        for e in self.ENGS:
            seen = {}
            pl = plan[e]

            def wait_on(p):
                if p.kind == "dma":
                    pool, si, val, _ = p.tk
                    key = (pool, si)
                    if seen.get(key, -1) >= val:
                        return
                    seen[key] = val
                    pl.append(("wait_dma", pool, si, val))
                else:
                    if seen.get(p.eng, -1) >= p.pos:
                        return
                    seen[p.eng] = p.pos
                    p.inc = True
                    pl.append(("wait_op", p))

            for n in self.order[e]:
                for p, need in n.preds.items():
                    if need:
                        wait_on(p)
                if n.kind == "dma":
                    prev = n.tk[3]
                    if prev is not None:
                        wait_on(prev)
                    if n.is_output:
                        out_nodes.append(n)
                pl.append(("node", n))
        seen = {}
        for n in out_nodes:
            pool, si, val, _ = n.tk
            if seen.get((pool, si), -1) < val:
                seen[(pool, si)] = val
                plan["sp"].append(("wait_dma", pool, si, val))
        for e in self.ENGS:
            c = 0
            for n in self.order[e]:
                if n.kind == "op" and n.inc:
                    c += 1
                    n.num = c
        self.plan = plan

    def replay_one(self, nc, name, h, sems, dma_sems):
        for it in self.plan[name]:
            if it[0] == "wait_op":
                p = it[1]
                h.wait_ge(sems[p.eng], p.num)
            elif it[0] == "wait_dma":
                _, pool, si, val = it
                h.wait_ge(dma_sems[pool][si], val)
            else:
                n = it[1]
                if n.kind == "dma":
                    out_ap, in_ap, pool = n.dma
                    h.dma_start(out=out_ap, in_=in_ap).then_inc(dma_sems[pool][n.tk[1]], 16)
                else:
                    ins = n.fn(h)
                    if n.inc:
                        ins.then_inc(sems[n.eng], 1)


def build(debug=False):
    nc = bass.Bass("TRN2", target_bir_lowering=False)
    K = Tracker()
    K.add_pool("sp", 8)
    K.add_pool("pool", 8)

    def din(name, shape, dt=F32):
        return nc.dram_tensor(name, list(shape), dt, kind="ExternalInput").ap()

    x_d = din("x", [S, D])
    ctx_d = din("ctx", [CTX, D])
    cond_d = din("cond", [128, 16])
    wada_d = din("w_ada", [D, 6 * D])
    badac_d = din("b_ada_col", [128, 48])
    bada_d = din("b_ada", [6 * D])
    gcols_d = din("gcols", [128, 32])
    gq_d = din("g_q", [HD])
    gk_d = din("g_k", [HD])
    gfin_d = din("g_final", [D])
    win_d = din("w_in", [D, 4096])
    wap_d = din("w_ap", [D, D])
    wfp_d = din("w_fp", [512, D])
    wo_d = din("w_o", [D, D])
    wgu_d = din("w_gu", [D, 2 * DFF])
    wdn_d = din("w_dn", [DFF, D])
    ident_d = din("ident", [128, 128], BF16)
    ropec_d = din("rope_c", [128, NKT * 64])
    ropes_d = din("rope_s", [128, NKT * 64])
    cs_d = din("dft_cs", [128, 256], BF16)
    dftl_d = din("dft_l", [16, 128, 4096], BF16)
    out_d = nc.dram_tensor("out", [S, D], F32, kind="ExternalOutput").ap()
    taps = {}
    tap_list = []

    from contextlib import ExitStack
    es = ExitStack()
    SB = es.enter_context(nc.sbuf_tensor("SB", [128, SB_BYTES], U8))
    PS = es.enter_context(nc.psum_tensor("PS", [128, 4096], F32))
    sems = {n: es.enter_context(nc.semaphore("s_" + n)) for n in ("pe", "act", "dve", "pool", "sp")}
    dma_sems = {p: [es.enter_context(nc.semaphore("d_%s%d" % (p, i))) for i in range(8)] for p in ("sp", "pool")}

    esz = {F32: 4, BF16: 2, U8: 1}

    def sb(off, dt, shape, p0=0, p1=128):
        n = int(np.prod(shape))
        nb = n * esz[dt]
        assert off + nb <= SB_BYTES, (off, nb)
        ap = SB[p0:p1, off:off + nb].bitcast(dt)
        if len(shape) == 2:
            ap = ap.rearrange("p (a b) -> p a b", a=shape[0])
        elif len(shape) == 3:
            ap = ap.rearrange("p (a b c) -> p a b c", a=shape[0], b=shape[1])
        elif len(shape) == 4:
            ap = ap.rearrange("p (a b c d) -> p a b c d", a=shape[0], b=shape[1], c=shape[2])
        return View(ap, "sb", off, off + nb)

    def bank(b, nb=1):
        return View(PS[:, b * 512:(b + nb) * 512], "ps", b * 2048, (b + nb) * 2048)

    def bank_bf(b):
        return View(PS[:, b * 512:(b + 1) * 512].bitcast(BF16), "ps", b * 2048, (b + 1) * 2048)

    def sub(v, ap):
        return View(ap, v.sp, v.lo, v.hi)

    def rng(v, ap, elo, ehi, es_):
        return View(ap, v.sp, v.lo + elo * es_, v.lo + ehi * es_)

    def fsz(v):
        n = 1
        for d in v.ap.shape[1:]:
            n *= int(d)
        return n

    def ecost(eng, v, per=1.04):
        if eng == "act":
            return 224.0 + 0.83 * fsz(v)
        if eng == "pool":
            return 300.0 + 2.0 * fsz(v)
        return 120.0 + per * fsz(v)

    def mm(out, lhsT, rhs, start, stop):
        return K.op("pe", lambda h: h.matmul(out.ap, lhsT=lhsT.ap, rhs=rhs.ap, start=start, stop=stop),
                    reads=[lhsT, rhs], writes=[out], cost=12.0 + 0.42 * fsz(rhs))

    def tr(out, in_, ident):
        return K.op("pe", lambda h: h.transpose(out=out.ap, in_=in_.ap, identity=ident.ap),
                    reads=[in_, ident], writes=[out], cost=66.0)

    def act(out, in_, func, scale=1.0, bias=None, accum=None):
        reads = [in_]
        kw = {}
        if isinstance(scale, View):
            reads.append(scale)
            kw["scale"] = scale.ap
        else:
            kw["scale"] = float(scale)
        if isinstance(bias, View):
            reads.append(bias)
            kw["bias"] = bias.ap
        elif bias is not None:
            kw["bias"] = float(bias)
        writes = [out]
        if accum is not None:
            writes.append(accum)
            kw["accum_out"] = accum.ap
        return K.op("act", lambda h: h.activation(out=out.ap, in_=in_.ap, func=func, **kw),
                    reads=reads, writes=writes, cost=ecost("act", in_))

    def tt(eng, out, a, b, op):
        return K.op(eng, lambda h: h.tensor_tensor(out=out.ap, in0=a.ap, in1=b.ap, op=op),
                    reads=[a, b], writes=[out], cost=(1600.0 if op == ALU.pow else ecost(eng, out)))

    def ts(eng, out, a, s1, op0, s2=None, op1=None):
        reads = [a]
        a1 = s1.ap if isinstance(s1, View) else float(s1)
        if isinstance(s1, View):
            reads.append(s1)
        a2 = None
        if s2 is not None:
            a2 = s2.ap if isinstance(s2, View) else float(s2)
            if isinstance(s2, View):
                reads.append(s2)
        if op1 is None:
            return K.op(eng, lambda h: h.tensor_scalar(out=out.ap, in0=a.ap, scalar1=a1, scalar2=None, op0=op0),
                        reads=reads, writes=[out], cost=ecost(eng, out, 0.6))
        return K.op(eng, lambda h: h.tensor_scalar(out=out.ap, in0=a.ap, scalar1=a1, scalar2=a2, op0=op0, op1=op1),
                    reads=reads, writes=[out], cost=ecost(eng, out, 0.6))

    def stt(eng, out, a, s, b, op0, op1):
        reads = [a, b]
        sa = s.ap if isinstance(s, View) else float(s)
        if isinstance(s, View):
            reads.append(s)
        return K.op(eng, lambda h: h.scalar_tensor_tensor(out=out.ap, in0=a.ap, scalar=sa, in1=b.ap, op0=op0, op1=op1),
                    reads=reads, writes=[out], cost=ecost(eng, out))

    def cp(eng, out, in_):
        if eng == "act":
            return K.op("act", lambda h: h.copy(out=out.ap, in_=in_.ap), reads=[in_], writes=[out], cost=ecost("act", in_))
        return K.op(eng, lambda h: h.tensor_copy(out=out.ap, in_=in_.ap), reads=[in_], writes=[out], cost=ecost(eng, out))

    def memset(eng, out, val):
        return K.op(eng, lambda h: h.memset(out.ap, val), writes=[out], cost=ecost(eng, out, 0.5))

    def recip(out, in_):
        return K.op("dve", lambda h: h.reciprocal(out=out.ap, in_=in_.ap), reads=[in_], writes=[out], cost=ecost("dve", out, 6.4))

    def reduce_sum(out, in_):
        return K.op("dve", lambda h: h.tensor_reduce(out=out.ap, in_=in_.ap, axis=AX.X, op=ALU.add),
                    reads=[in_], writes=[out], cost=ecost("dve", in_))

    def load(dst, src_ap, q="sp"):
        return K.dma(q, q, dst.ap, src_ap, writes=[dst], nbytes=(dst.hi - dst.lo) * 128)

    def loadw(dst, src_ap):
        return K.dma("pool", "pool", dst.ap, src_ap, writes=[dst], nbytes=fsz(dst) * 128 * 4)

    def store(dst_ap, src, is_output=True):
        return K.dma("sp", "sp", dst_ap, src.ap, reads=[src], is_output=is_output, nbytes=(src.hi - src.lo) * 128)

    def tap(name, view, dt):
        shp = [int(x) for x in view.ap.shape]
        d = nc.dram_tensor("tap_" + name, shp, dt, kind="ExternalOutput").ap()
        K.dma("sp", "sp", d, view.ap, reads=[view], is_output=True, nbytes=(view.hi - view.lo) * 128)
        tap_list.append("tap_" + name)

    def rstd_from_ms(rstd, ms, tmp, mhalf):
        act(tmp, ms, AF.Ln, bias=eps_col)
        act(rstd, tmp, AF.Exp, scale=-0.5)

    KB = 1024
    HT_O = 0
    AT_O = 32 * KB
    KV_O = 76 * KB
    FY_O = 121 * KB
    QM_O = 137 * KB
    SM_O = 169 * KB
    assert SM_O + 31 * KB <= SB_BYTES

    c_o = SM_O
    ident = sb(c_o, BF16, (128,)); c_o += 256
    cond = sb(c_o, F32, (8, 2)); c_o += 64
    scond = sb(c_o, BF16, (8, 2)); c_o += 32
    badac = sb(c_o, F32, (48,)); c_o += 192
    gcols = sb(c_o, F32, (32,)); c_o += 128
    modraw = sb(c_o, F32, (4, 8, 2)); c_o += 256
    modc = sb(c_o, F32, (6, 8)); c_o += 192
    stats = sb(c_o, F32, (64,)); c_o += 256
    mhalf = sb(c_o, F32, (16,)); c_o += 64
    eps_t = sb(c_o, F32, (16,)); c_o += 64
    gq_bc = sb(c_o, F32, (64,)); c_o += 256
    gk_bc = sb(c_o, F32, (64,)); c_o += 256
    cs_t = sb(c_o, BF16, (256,)); c_o += 512
    scond_bc = sb(c_o, BF16, (8, 128)); c_o += 2048
    GTm = sb(c_o, F32, (1024,)); c_o += 4096
    GTf = sb(c_o, F32, (1024,)); c_o += 4096
    SMF = c_o
    SM_END = SM_O + 31 * KB

    load(ident, ident_d)
    load(cond, cond_d.rearrange("p (a b) -> p a b", a=8))
    load(badac, badac_d)
    load(gcols, gcols_d)
    load(gq_bc, gq_d.partition_broadcast(128))
    load(gk_bc, gk_d.partition_broadcast(128))
    load(cs_t, cs_d)
    memset("dve", mhalf, -0.5)
    memset("dve", eps_t, EPS)
    eps_col = sub(eps_t, eps_t.ap[:, 0:1])
    act(scond, cond, AF.Silu)
    cp("dve", scond_bc, sub(scond, scond.ap[:, :, 0:1].to_broadcast([128, 8, 128])))

    wada_v = wada_d.rearrange("(k p) n -> p k n", p=128)
    ada_slots = [sb(QM_O + i * 8 * KB, BF16, (8, 512)) for i in range(4)]
    slot_i = [0]

    def ada_piece(col0):
        s = ada_slots[slot_i[0] % 4]
        slot_i[0] += 1
        loadw(s, wada_v[:, :, col0:col0 + 512])
        return s

    gmix = sub(gcols, gcols.ap[:, 0:8])
    gffn = sub(gcols, gcols.ap[:, 8:16])
    bgate = sub(gcols, gcols.ap[:, 16:32])

    def ada_cols(vis, pbank, piece_fn):
        pcol = bank(pbank)
        pcol_v = sub(pcol, pcol.ap[:, 0:64].rearrange("p (v c n) -> p v c n", v=4, c=8))
        for vi in vis:
            v = (0, 1, 3, 4)[vi]
            for half in range(2):
                s_ = piece_fn(v * 1024 + half * 512)
                for cc in range(4):
                    c = half * 4 + cc
                    for k in range(8):
                        mm(sub(pcol, pcol_v.ap[:, vi, c, :]),
                           sub(s_, s_.ap[:, k, cc * 128:(cc + 1) * 128]),
                           sub(scond, scond.ap[:, k, :]), k == 0, k == 7)
        for vi in vis:
            v = (0, 1, 3, 4)[vi]
            tt("dve", sub(modraw, modraw.ap[:, vi, :, :]), sub(pcol, pcol_v.ap[:, vi, :, :]),
               sub(badac, badac.ap[:, v * 8:(v + 1) * 8].unsqueeze(2).to_broadcast([128, 8, 2])), ALU.add)

    ada_cols((0, 1), 0, ada_piece)
    stt("dve", sub(modc, modc.ap[:, 0, :]), sub(modraw, modraw.ap[:, 1, :, 0]), 1.0, gmix, ALU.add, ALU.mult)
    cp("dve", sub(modc, modc.ap[:, 1, :]), sub(modraw, modraw.ap[:, 0, :, 0]))
    stt("dve", sub(modc, modc.ap[:, 2, :]), sub(modraw, modraw.ap[:, 1, :, 1]), 1.0, gmix, ALU.add, ALU.mult)
    cp("dve", sub(modc, modc.ap[:, 3, :]), sub(modraw, modraw.ap[:, 0, :, 1]))

    late_slots = [sb(104 * KB + i * 8 * KB, BF16, (8, 512)) for i in range(2)]
    late_i = [0]

    def late_piece(col0):
        s_ = late_slots[late_i[0] % 2]
        late_i[0] += 1
        loadw(s_, wada_v[:, :, col0:col0 + 512])
        return s_

    def ada_late():
        ada_cols((2, 3), 6, late_piece)
        stt("dve", sub(modc, modc.ap[:, 4, :]), sub(modraw, modraw.ap[:, 3, :, 0]), 1.0, gffn, ALU.add, ALU.mult)
        cp("dve", sub(modc, modc.ap[:, 5, :]), sub(modraw, modraw.ap[:, 2, :, 0]))
        for gt, v in ((GTm, 2), (GTf, 5)):
            load(gt, bada_d[v * 1024:(v + 1) * 1024].partition_broadcast(128))
            for half in range(2):
                s_ = late_piece(v * 1024 + half * 512)
                pb = bank(7)
                for k in range(8):
                    mm(pb, sub(scond_bc, scond_bc.ap[:, k, :]), sub(s_, s_.ap[:, k, :]), k == 0, k == 7)
                g_half = rng(gt, gt.ap[:, half * 512:(half + 1) * 512], half * 512, half * 512 + 512, 4)
                tt("dve", g_half, pb, g_half, ALU.add)
        if debug:
            tap("modc", modc, F32)
            tap("GTm", GTm, F32)

    K.phase = "A"
    hT = sb(HT_O, BF16, (16, 8, 128))
    qT = sb(QM_O, BF16, (16, 8, 128))
    kT2 = sb(KV_O, BF16, (4, NKT * 128))
    Vaug = sb(KV_O + 18 * KB, BF16, (NKT, 4, 192))
    hcT = sb(QM_O + 28 * KB, BF16, (2, 8, 128))
    Wq0 = sb(AT_O, BF16, (8, 512))
    Wq1 = sb(AT_O + 8 * KB, BF16, (8, 512))
    Wkv = sb(AT_O + 16 * KB, BF16, (8, 512))
    Wf = sb(AT_O + 24 * KB, BF16, (8, 512))
    ropec = sb(AT_O + 32 * KB, F32, (NKT, 64))
    ropes = sb(AT_O + 32 * KB + 4608, F32, (NKT, 64))
    win_v = win_d.rearrange("(k p) n -> p k n", p=128)
    loadw(Wkv, win_v[:, :, 1024:1536])
    loadw(Wq0, win_v[:, :, 0:512])
    loadw(Wq1, win_v[:, :, 512:1024])
    loadw(Wf, win_v[:, :, 1536:2048])
    load(ropec, ropec_d.rearrange("p (a b) -> p a b", a=NKT))
    load(ropes, ropes_d.rearrange("p (a b) -> p a b", a=NKT))
    memset("dve", Vaug, 1.0)

    o = SMF
    xbuf = [sb(o + i * 4096, F32, (1024,)) for i in range(2)]; o += 8192
    junk = sb(o, BF16, (1024,)); o += 2048
    xn = [sb(o + i * 2048, BF16, (1024,)) for i in range(2)]; o += 4096
    kr2 = [sb(o + i * 1024, BF16, (4, 2, 64)) for i in range(2)]; o += 2048
    tabs2 = [sb(o + i * 1024, F32, (4, 64)) for i in range(2)]; o += 2048
    assert o <= SM_END, o
    o = FY_O
    sq = sb(o, F32, (8, 64)); o += 2048
    qn = sb(o, F32, (8, 64)); o += 2048
    uu = sb(o, F32, (8, 64)); o += 2048
    ww = sb(o, F32, (8, 64)); o += 2048
    qr = [sb(o + i * 2048, BF16, (16, 64)) for i in range(2)]; o += 4096
    assert o <= FY_O + 16 * KB

    st_ms = lambda i: sub(stats, stats.ap[:, i:i + 1])

    def hsl(tt_i, k):
        if tt_i < NCT:
            return rng(hcT, hcT.ap[:, tt_i, k, :], (tt_i * 8 + k) * 128, (tt_i * 8 + k + 1) * 128, 2)
        t = tt_i - NCT
        return rng(hT, hT.ap[:, t, k, :], (t * 8 + k) * 128, (t * 8 + k + 1) * 128, 2)

    def P1(tt_i):
        is_ctx = tt_i < NCT
        xb = xbuf[tt_i % 2]
        src = ctx_d[tt_i * 128:(tt_i + 1) * 128, :] if is_ctx else x_d[(tt_i - NCT) * 128:(tt_i - NCT + 1) * 128, :]
        load(xb, src)
        sc = (tt_i % 2) * 3
        act(junk, xb, AF.Square, scale=1.0 / 32.0, accum=st_ms(sc))
        rstd_from_ms(st_ms(sc + 2), st_ms(sc), st_ms(sc + 1), sub(mhalf, mhalf.ap[:, 0:1]))
        xnb = xn[tt_i % 2]
        act(xnb, xb, AF.Copy, scale=st_ms(sc + 2))
        pT = bank_bf(6)
        for k in range(8):
            tr(sub(pT, pT.ap[:, k * 128:(k + 1) * 128]), sub(xnb, xnb.ap[:, k * 128:(k + 1) * 128]), ident)
        gi, si = (2, 3) if is_ctx else (0, 1)
        for k in range(8):
            act(hsl(tt_i, k), sub(pT, pT.ap[:, k * 128:(k + 1) * 128]), AF.Identity,
                scale=sub(modc, modc.ap[:, gi, k:k + 1]), bias=sub(modc, modc.ap[:, si, k:k + 1]))

    def MM(tt_i):
        base = (tt_i % 2) * 3
        pkv = bank(base)
        for k in range(8):
            mm(pkv, hsl(tt_i, k), sub(Wkv, Wkv.ap[:, k, :]), k == 0, k == 7)
        if tt_i < NCT:
            return
        for qb, Wq in enumerate((Wq0, Wq1)):
            pq = bank(base + 1 + qb)
            for k in range(8):
                mm(pq, hsl(tt_i, k), sub(Wq, Wq.ap[:, k, :]), k == 0, k == 7)

    blk_i = [0]

    def make_block(ps_v, nh, tab_c, tab_s, tabv, dsts):
        par = blk_i[0] % 2
        blk_i[0] += 1
        b0 = 8 + par * 24
        p3 = sub(ps_v, ps_v.ap.rearrange("p (h d) -> p h d", h=nh))
        ss = sub(stats, stats.ap[:, b0:b0 + nh])
        tmp = sub(stats, stats.ap[:, b0 + 8:b0 + 8 + nh])
        rs = sub(stats, stats.ap[:, b0 + 16:b0 + 16 + nh])

        def X():
            sq3 = sub(sq, sq.ap[:, 0:nh, :])
            act(sq3, p3, AF.Square, scale=0.125)
            reduce_sum(ss, sq3)
            rstd_from_ms(rs, ss, tmp, sub(mhalf, mhalf.ap[:, 0:nh]))

        def Y():
            qn3 = sub(qn, qn.ap[:, 0:nh, :])
            tt("dve", qn3, p3, sub(rs, rs.ap.unsqueeze(2).to_broadcast([128, nh, 64])), ALU.mult)
            u3 = sub(uu, uu.ap[:, 0:nh, :])
            w3 = sub(ww, ww.ap[:, 0:nh, :])
            tt("dve", u3, qn3, sub(tabv, tab_c.unsqueeze(1).to_broadcast([128, nh, 64])), ALU.mult)
            tt("dve", w3, qn3, sub(tabv, tab_s.unsqueeze(1).to_broadcast([128, nh, 64])), ALU.mult)
            for d in dsts:
                tt("dve", sub(d, d.ap[:, :, 0:32]), sub(uu, u3.ap[:, :, 0:32]), sub(ww, w3.ap[:, :, 32:64]), ALU.subtract)
                tt("dve", sub(d, d.ap[:, :, 32:64]), sub(uu, u3.ap[:, :, 32:64]), sub(ww, w3.ap[:, :, 0:32]), ALU.add)
        return X, Y

    pending = []

    def run_block(X, Y, after=None):
        X()
        if pending:
            py, pafter = pending.pop()
            py()
            if pafter:
                pafter()
        pending.append((Y, after))

    def POST_a(tt_i):
        is_ctx = tt_i < NCT
        t = tt_i - NCT
        base = (tt_i % 2) * 3
        pkv = bank(base)
        tabv = tabs2[tt_i % 2]
        krb = kr2[tt_i % 2]
        rc = sub(ropec, ropec.ap[:, tt_i, :])
        rs_ = sub(ropes, ropes.ap[:, tt_i, :])
        if not is_ctx:
            stt("dve", sub(tabv, tabv.ap[:, 0, :]), rc, 0.125, gq_bc, ALU.mult, ALU.mult)
            stt("dve", sub(tabv, tabv.ap[:, 1, :]), rs_, 0.125, gq_bc, ALU.mult, ALU.mult)
        tt("dve", sub(tabv, tabv.ap[:, 2, :]), rc, gk_bc, ALU.mult)
        tt("dve", sub(tabv, tabv.ap[:, 3, :]), rs_, gk_bc, ALU.mult)
        kr_a = sub(krb, krb.ap[:, :, 0, :])
        kr_b = sub(krb, krb.ap[:, :, 1, :])
        Xk, Yk = make_block(sub(pkv, pkv.ap[:, 0:256]), 4, tabv.ap[:, 2, :], tabv.ap[:, 3, :], tabv, [kr_a, kr_b])

        def Xk2():
            Xk()
            vdst = rng(Vaug, Vaug.ap[:, tt_i, :, 64:128], tt_i * 768, (tt_i + 1) * 768, 2)
            cp("act", vdst, sub(pkv, pkv.ap[:, 256:512].rearrange("p (h d) -> p h d", h=4)))

        def Fk():
            pk = bank_bf(7)
            for kh in range(4):
                tr(sub(pk, pk.ap[:, kh * 128:(kh + 1) * 128]),
                   sub(krb, krb.ap[:, kh, :, :].rearrange("p a d -> p (a d)")), ident)
            kdst = sub(kT2, kT2.ap[:, :, tt_i * 128:(tt_i + 1) * 128])
            cp("act", kdst, sub(pk, pk.ap[:, 0:512].rearrange("p (h t) -> p h t", h=4)))

        run_block(Xk2, Yk, Fk)

    def POST_b(tt_i):
        if tt_i < NCT:
            return
        t = tt_i - NCT
        base = (tt_i % 2) * 3
        tabv = tabs2[tt_i % 2]
        qrb = qr[t % 2]

        def Fq():
            pqt = bank_bf(7)
            for c in range(8):
                tr(sub(pqt, pqt.ap[:, c * 128:(c + 1) * 128]),
                   sub(qrb, qrb.ap[:, 2 * c:2 * c + 2, :].rearrange("p a d -> p (a d)")), ident)
            qdst = rng(qT, qT.ap[:, t, :, :].rearrange("p c t -> p (c t)"), t * 1024, (t + 1) * 1024, 2)
            cp("act", qdst, pqt)

        for qb in range(2):
            pq = bank(base + 1 + qb)
            qd = rng(qrb, qrb.ap[:, qb * 8:(qb + 1) * 8, :], qb * 512, (qb + 1) * 512, 2)
            Xq, Yq = make_block(pq, 8, tabv.ap[:, 0, :], tabv.ap[:, 1, :], tabv, [qd])
            run_block(Xq, Yq, Fq if qb == 1 else None)

    for s_ in range(NKT + 2):
        if s_ < NKT:
            P1(s_)
        if s_ >= 2:
            POST_a(s_ - 2)
        if 1 <= s_ <= NKT:
            MM(s_ - 1)
        if s_ >= 2:
            POST_b(s_ - 2)
    while pending:
        py, pafter = pending.pop()
        py()
        if pafter:
            pafter()
    pb_i = [0]

    def next_bank():
        b = pb_i[0] % 8
        pb_i[0] += 1
        return b

    if debug:
        tap("hT", hT, BF16)
        tap("qT", qT, BF16)
        tap("kT2", kT2, BF16)
        tap("Vaug", Vaug, BF16)

    K.phase = "A2"
    FT = sb(FY_O, BF16, (4, 2048))
    for fg in range(4):
        for tb in range(4):
            b = next_bank()
            pf = bank(b)
            rhs_all = lambda k: rng(hT, hT.ap[:, tb * 4:(tb + 1) * 4, k, :], tb * 4096, (tb + 1) * 4096, 2)
            for k in range(8):
                mm(pf, sub(Wf, Wf.ap[:, k, fg * 128:(fg + 1) * 128]), rhs_all(k), k == 0, k == 7)
            fdst = rng(FT, FT.ap[:, fg, tb * 512:(tb + 1) * 512], fg * 2048 + tb * 512, fg * 2048 + (tb + 1) * 512, 2)
            cp("act" if (fg + tb) % 2 else "dve", fdst, pf)
    if debug:
        tap("FT", FT, BF16)

    K.phase = "B"
    attnT = sb(AT_O, BF16, (8, 2048))
    PT = [sb(SMF + i * 2048, BF16, (1024,)) for i in range(3)]
    accsb = [sb(SMF + 6144 + i * 4096, F32, (1024,)) for i in range(2)]
    rden1 = sb(SMF + 6144 + 8192, F32, (1024,))
    assert SMF + 6144 + 12288 <= SM_END
    steps = [(c, qb, kt) for c in range(NH // 2) for qb in range(4) for kt in range(NKT)]

    def s_bank(i):
        return bank((i % 3) * 2, 2)

    def acc_bank(j):
        return bank(6, 2)

    def emit_qk(i):
        c, qb, kt = steps[i]
        kh = c // 2
        sbk = s_bank(i)
        t0 = qb * 4
        for half in range(2):
            p0, p1 = half * 64, half * 64 + 64
            lhsT = View(kT2.ap[p0:p1, kh, kt * 128:(kt + 1) * 128], "sb",
                        kT2.lo + (kh * NKT * 128 + kt * 128) * 2, kT2.lo + (kh * NKT * 128 + kt * 128 + 128) * 2)
            rhs = View(qT.ap[p0:p1, t0:t0 + 4, c, :], "sb", qT.lo + t0 * 2048, qT.lo + (t0 + 4) * 2048)
            mm(View(sbk.ap[:, half * 512:(half + 1) * 512], "ps", sbk.lo + half * 2048, sbk.lo + (half + 1) * 2048),
               lhsT, rhs, True, True)

    emit_qk(0)
    emit_qk(1)
    for i in range(len(steps)):
        c, qb, kt = steps[i]
        kh = c // 2
        if i + 2 < len(steps):
            emit_qk(i + 2)
        pt = PT[i % 3]
        act(pt, s_bank(i), AF.Exp)
        j = i // NKT
        acc = acc_bank(j)
        for half in range(2):
            c0 = 64 if half == 0 else 0
            va = View(Vaug.ap[:, kt, kh, c0:c0 + 128], "sb", Vaug.lo + (kt * 4 + kh) * 384, Vaug.lo + (kt * 4 + kh + 1) * 384)
            mm(View(acc.ap[:, half * 512:(half + 1) * 512], "ps", acc.lo + half * 2048, acc.lo + (half + 1) * 2048),
               va, sub(pt, pt.ap[:, half * 512:(half + 1) * 512]), kt == 0, kt == NKT - 1)
        if kt == NKT - 1:
            rd = rden1
            asb = accsb[j % 2]
            cp("dve", asb, acc)
            for half in range(2):
                ab = View(asb.ap[:, half * 512:(half + 1) * 512], "sb", asb.lo + half * 2048, asb.lo + (half + 1) * 2048)
                n0, d0 = (0, 64) if half == 0 else (64, 0)
                num = View(ab.ap[n0:n0 + 64, :], "sb", ab.lo, ab.hi)
                den = View(ab.ap[d0:d0 + 64, :], "sb", ab.lo, ab.hi)
                rdv = View(rd.ap[n0:n0 + 64, half * 512:(half + 1) * 512], "sb", rd.lo + half * 2048, rd.lo + (half + 1) * 2048)
                dst = View(attnT.ap[n0:n0 + 64, c, qb * 512:(qb + 1) * 512], "sb",
                           attnT.lo + (c * 2048 + qb * 512) * 2, attnT.lo + (c * 2048 + qb * 512 + 512) * 2)
                recip(rdv, den)
                tt("dve", dst, num, rdv, ALU.mult)
    if debug:
        tap("attnT", attnT, BF16)

    K.phase = "C"
    G = sb(QM_O, BF16, (16, 4, 256))
    YT = sb(FY_O, BF16, (4, 2048))
    for lt in range(NT):
        for pr in range(2):
            b = next_bank()
            pg = bank(b)
            for a in range(2):
                fg = pr * 2 + a
                mm(View(pg.ap[:, a * 256:(a + 1) * 256], "ps", pg.lo + a * 1024, pg.lo + (a + 1) * 1024),
                   rng(FT, FT.ap[:, fg, lt * 128:(lt + 1) * 128], fg * 2048 + lt * 128, fg * 2048 + lt * 128 + 128, 2),
                   cs_t, True, True)
            gd = rng(G, G.ap[:, lt, pr * 2:pr * 2 + 2, :].rearrange("p a c -> p (a c)"),
                     (lt * 4 + pr * 2) * 256, (lt * 4 + pr * 2 + 2) * 256, 2)
            cp("act" if (lt + pr) % 2 else "dve", gd, pg)
    dft_slots = [sb(KV_O + i * 8 * KB, BF16, (4, 2, 512)) for i in range(3)]
    di = 0
    for jb in range(4):
        banks = [bank((jb % 2) * 4 + fg) for fg in range(4)]
        for lg in range(4):
            ds = dft_slots[di % 3]
            load(ds, dftl_d[jb * 4 + lg].rearrange("p (a b c) -> p a b c", a=4, b=2))
            di += 1
            for fg in range(4):
                for a in range(4):
                    lt = lg * 4 + a
                    for cs_i in range(2):
                        mm(banks[fg],
                           rng(G, G.ap[:, lt, fg, cs_i * 128:(cs_i + 1) * 128],
                               (lt * 4 + fg) * 256 + cs_i * 128, (lt * 4 + fg) * 256 + cs_i * 128 + 128, 2),
                           sub(ds, ds.ap[:, a, cs_i, :]),
                           lt == 0 and cs_i == 0, lt == NT - 1 and cs_i == 1)
        for fg in range(4):
            yd = rng(YT, YT.ap[:, fg, jb * 512:(jb + 1) * 512], fg * 2048 + jb * 512, fg * 2048 + jb * 512 + 512, 2)
            cp("act" if fg % 2 else "dve", yd, banks[fg])
    if debug:
        tap("YT", YT, BF16)

    K.phase = "D"
    MT = sb(QM_O, BF16, (8, 2048))
    wslots = [sb(76 * KB + i * 12 * KB, BF16, (24, 256)) for i in range(2)]
    wfps = [sb(100 * KB + i * 2 * KB, BF16, (4, 256)) for i in range(2)]
    gsl = [sb(SMF + i * 2048, F32, (512,)) for i in range(4)]
    wap_v = wap_d.rearrange("(k p) n -> p k n", p=128)
    wfp_v = wfp_d.rearrange("(k p) n -> p k n", p=128)
    for j in range(8):
        wsl = wslots[(j // 2) % 2]
        wf_ = wfps[(j // 2) % 2]
        jo = (j % 2) * 128
        if j % 2 == 0:
            loadw(rng(wsl, wsl.ap[:, 0:8, :], 0, 2048, 2), wap_v[:, :, j * 128:(j + 2) * 128])
            loadw(rng(wsl, wsl.ap[:, 8:16, :], 2048, 4096, 2), win_v[:, :, 2048 + j * 128:2048 + (j + 2) * 128])
            loadw(rng(wsl, wsl.ap[:, 16:24, :], 4096, 6144, 2), win_v[:, :, 3072 + j * 128:3072 + (j + 2) * 128])
            loadw(wf_, wfp_v[:, :, j * 128:(j + 2) * 128])
        for tb in range(4):
            base = ((j * 4 + tb) % 2) * 4
            pA, pF, pGa, pGf = bank(base), bank(base + 1), bank(base + 2), bank(base + 3)
            hsl4 = lambda k: rng(hT, hT.ap[:, tb * 4:(tb + 1) * 4, k, :], tb * 4096, (tb + 1) * 4096, 2)
            for k in range(8):
                mm(pGa, rng(wsl, wsl.ap[:, 8 + k, jo:jo + 128], (8 + k) * 256, (9 + k) * 256, 2), hsl4(k), k == 0, k == 7)
            for k in range(8):
                mm(pGf, rng(wsl, wsl.ap[:, 16 + k, jo:jo + 128], (16 + k) * 256, (17 + k) * 256, 2), hsl4(k), k == 0, k == 7)
            for k in range(8):
                mm(pA, rng(wsl, wsl.ap[:, k, jo:jo + 128], k * 256, (k + 1) * 256, 2),
                   rng(attnT, attnT.ap[:, k, tb * 512:(tb + 1) * 512], k * 2048 + tb * 512, k * 2048 + tb * 512 + 512, 2),
                   k == 0, k == 7)
            for k in range(4):
                mm(pF, sub(wf_, wf_.ap[:, k, jo:jo + 128]),
                   rng(YT, YT.ap[:, k, tb * 512:(tb + 1) * 512], k * 2048 + tb * 512, k * 2048 + tb * 512 + 512, 2),
                   k == 0, k == 3)
            act(gsl[0], pGa, AF.Sigmoid, bias=sub(bgate, bgate.ap[:, j:j + 1]))
            act(gsl[1], pGf, AF.Sigmoid, bias=sub(bgate, bgate.ap[:, 8 + j:9 + j]))
            tt("dve", gsl[2], pA, gsl[0], ALU.mult)
            tt("dve", gsl[3], pF, gsl[1], ALU.mult)
            md = rng(MT, MT.ap[:, j, tb * 512:(tb + 1) * 512], j * 2048 + tb * 512, j * 2048 + tb * 512 + 512, 2)
            tt("dve", md, gsl[2], gsl[3], ALU.add)
    ada_late()
    if debug:
        tap("MT", MT, BF16)

    K.phase = "E1"
    X1 = sb(AT_O, F32, (16, 1024))
    Wo = sb(FY_O, BF16, (8, 1024))
    h2T = sb(HT_O, BF16, (16, 8, 128))
    wo_v = wo_d.rearrange("(k p) n -> p k n", p=128)
    loadw(rng(Wo, Wo.ap[:, 0:4, :], 0, 4096, 2), wo_v[:, 0:4, :])
    loadw(rng(Wo, Wo.ap[:, 4:8, :], 4096, 8192, 2), wo_v[:, 4:8, :])
    o = SMF
    xbuf = [sb(o + i * 4096, F32, (1024,)) for i in range(2)]; o += 8192
    junk = sb(o, BF16, (1024,)); o += 2048
    xn = [sb(o + i * 2048, BF16, (1024,)) for i in range(2)]; o += 4096
    tmpf = sb(o, F32, (1024,)); o += 4096
    assert o <= SM_END
    for t in range(NT):
        xb = xbuf[t % 2]
        load(xb, x_d[t * 128:(t + 1) * 128, :])
        x1t = rng(X1, X1.ap[:, t, :], t * 1024, (t + 1) * 1024, 4)
        for nb in range(2):
            b = next_bank()
            po = bank(b)
            for j in range(8):
                mm(po, rng(MT, MT.ap[:, j, t * 128:(t + 1) * 128], j * 2048 + t * 128, j * 2048 + t * 128 + 128, 2),
                   rng(Wo, Wo.ap[:, j, nb * 512:(nb + 1) * 512], j * 1024 + nb * 512, j * 1024 + nb * 512 + 512, 2),
                   j == 0, j == 7)
            tmh = rng(tmpf, tmpf.ap[:, nb * 512:(nb + 1) * 512], nb * 512, nb * 512 + 512, 4)
            tt("dve", tmh, po, rng(GTm, GTm.ap[:, nb * 512:(nb + 1) * 512], nb * 512, nb * 512 + 512, 4), ALU.mult)
            tt("dve", rng(X1, X1.ap[:, t, nb * 512:(nb + 1) * 512], t * 1024 + nb * 512, t * 1024 + nb * 512 + 512, 4),
               tmh, rng(xb, xb.ap[:, nb * 512:(nb + 1) * 512], nb * 512, nb * 512 + 512, 4), ALU.add)
        sc = (t % 2) * 3
        act(junk, x1t, AF.Square, scale=1.0 / 32.0, accum=st_ms(sc))
        rstd_from_ms(st_ms(sc + 2), st_ms(sc), st_ms(sc + 1), sub(mhalf, mhalf.ap[:, 0:1]))
        xnb = xn[t % 2]
        act(xnb, x1t, AF.Copy, scale=st_ms(sc + 2))
        bT = next_bank()
        pT = bank_bf(bT)
        for k in range(8):
            tr(sub(pT, pT.ap[:, k * 128:(k + 1) * 128]), sub(xnb, xnb.ap[:, k * 128:(k + 1) * 128]), ident)
        for k in range(8):
            dst = rng(h2T, h2T.ap[:, t, k, :], (t * 8 + k) * 128, (t * 8 + k + 1) * 128, 2)
            act(dst, sub(pT, pT.ap[:, k * 128:(k + 1) * 128]), AF.Identity,
                scale=sub(modc, modc.ap[:, 4, k:k + 1]), bias=sub(modc, modc.ap[:, 5, k:k + 1]))
    if debug:
        tap("X1", X1, F32)
        tap("h2T", h2T, BF16)

    K.phase = "E2"
    wgu_slots = [sb(96 * KB + i * 8 * KB, BF16, (8, 2, 256)) for i in range(2)]
    Wd = sb(112 * KB, BF16, (NFF, 1024))
    actA = sb(156 * KB, BF16, (12, 512))
    actB = sb(SMF, BF16, (10, 512))
    gfin = sb(GTm.lo, F32, (1024,))
    junk2 = sb(scond_bc.lo, BF16, (1024,))
    sg = [sb(SMF + 10 * KB + i * 2048, F32, (512,)) for i in range(2)]
    tm2 = [sb(SMF + 14 * KB + i * 2048, F32, (512,)) for i in range(2)]
    assert SMF + 18 * KB <= SM_END

    def act_i(i, c0, c1):
        if i < 12:
            return rng(actA, actA.ap[:, i, c0:c1], i * 512 + c0, i * 512 + c1, 2)
        return rng(actB, actB.ap[:, i - 12, c0:c1], (i - 12) * 512 + c0, (i - 12) * 512 + c1, 2)

    wdn_v = wdn_d.rearrange("(k p) n -> p k n", p=128)
    wgu_v = wgu_d.rearrange("(k p) n -> p k n", p=128)
    load(gfin, gfin_d.partition_broadcast(128))
    pieces = [(tq, i) for tq in range(4) for i in range(NFF)]
    NG = len(pieces) // 2

    def load_group(gi):
        tq, i = pieces[gi * 2]
        sl = wgu_slots[gi % 2]
        loadw(sub(sl, sl.ap[:, :, 0, :]), wgu_v[:, :, i * 128:(i + 2) * 128])
        loadw(sub(sl, sl.ap[:, :, 1, :]), wgu_v[:, :, DFF + i * 128:DFF + (i + 2) * 128])

    load_group(0)
    load_group(1)
    for c4 in range(2):
        loadw(rng(Wd, Wd.ap[:, c4 * 11:(c4 + 1) * 11, :], c4 * 11 * 1024, (c4 + 1) * 11 * 1024, 2),
              wdn_v[:, c4 * 11:(c4 + 1) * 11, :])
    for pi, (tq, i) in enumerate(pieces):
        sl = wgu_slots[(pi // 2) % 2]
        o2 = (pi % 2) * 128
        pg_, pu_ = bank((pi % 2) * 2), bank((pi % 2) * 2 + 1)
        rhs4 = lambda k: rng(h2T, h2T.ap[:, tq * 4:(tq + 1) * 4, k, :], tq * 4096, (tq + 1) * 4096, 2)
        for k in range(8):
            mm(pg_, sub(sl, sl.ap[:, k, 0, o2:o2 + 128]), rhs4(k), k == 0, k == 7)
        for k in range(8):
            mm(pu_, sub(sl, sl.ap[:, k, 1, o2:o2 + 128]), rhs4(k), k == 0, k == 7)
        if pi % 2 == 1 and pi // 2 + 2 < NG:
            load_group(pi // 2 + 2)
        sgi = sg[pi % 2]
        act(sgi, pg_, AF.Silu)
        tt("dve", act_i(i, 0, 512), pu_, sgi, ALU.mult)
        if i != NFF - 1:
            continue
        for t4 in range(4):
            t = tq * 4 + t4
            x1t = rng(X1, X1.ap[:, t, :], t * 1024, (t + 1) * 1024, 4)
            for nb in range(2):
                pd = bank(4 + ((t4 * 2 + nb) % 4))
                for ii in range(NFF):
                    mm(pd, act_i(ii, t4 * 128, (t4 + 1) * 128),
                       rng(Wd, Wd.ap[:, ii, nb * 512:(nb + 1) * 512], ii * 1024 + nb * 512, ii * 1024 + nb * 512 + 512, 2),
                       ii == 0, ii == NFF - 1)
                tmh = tm2[nb]
                tt("dve", tmh, pd, rng(GTf, GTf.ap[:, nb * 512:(nb + 1) * 512], nb * 512, nb * 512 + 512, 4), ALU.mult)
                x1h = rng(X1, X1.ap[:, t, nb * 512:(nb + 1) * 512], t * 1024 + nb * 512, t * 1024 + nb * 512 + 512, 4)
                tt("dve", x1h, tmh, x1h, ALU.add)
            sc = (t % 2) * 3
            act(junk2, x1t, AF.Square, scale=1.0 / 32.0, accum=st_ms(sc))
            rstd_from_ms(st_ms(sc + 2), st_ms(sc), st_ms(sc + 1), sub(mhalf, mhalf.ap[:, 0:1]))
            stt("dve", x1t, x1t, st_ms(sc + 2), gfin, ALU.mult, ALU.mult)
            store(out_d[t * 128:(t + 1) * 128, :], x1t)

    K.schedule()
    K.emit_plan()

    with nc.Block() as block:
        @block.tensor
        def _(h):
            K.replay_one(nc, "pe", h, sems, dma_sems)

        @block.scalar
        def _(h):
            K.replay_one(nc, "act", h, sems, dma_sems)

        @block.vector
        def _(h):
            K.replay_one(nc, "dve", h, sems, dma_sems)

        @block.gpsimd
        def _(h):
            K.replay_one(nc, "pool", h, sems, dma_sems)

        @block.sync
        def _(h):
            K.replay_one(nc, "sp", h, sems, dma_sems)
    es.close()
    return nc, tap_list


_CONST = {}


def _consts():
    if _CONST:
        return _CONST
    bf = ml_dtypes.bfloat16
    _CONST["ident"] = np.eye(128, dtype=np.float32).astype(bf)
    inv = (10000.0 ** (-np.arange(0, 32, 2, dtype=np.float32) / np.float32(32))).astype(np.float32)
    tpos = np.arange(S)
    t_row = (tpos // 64).astype(np.float32)
    t_col = (tpos % 64).astype(np.float32)
    ang = np.concatenate([t_row[:, None] * inv[None, :], t_col[:, None] * inv[None, :]], axis=-1).astype(np.float32)
    cos = np.cos(ang).astype(np.float32)
    sin = np.sin(ang).astype(np.float32)
    rc = np.ones((NKT, 128, 64), np.float32)
    rs = np.zeros((NKT, 128, 64), np.float32)
    rc[NCT:] = np.concatenate([cos, cos], -1).reshape(NT, 128, 64)
    rs[NCT:] = np.concatenate([sin, sin], -1).reshape(NT, 128, 64)
    _CONST["rope_c"] = np.ascontiguousarray(rc.transpose(1, 0, 2).reshape(128, NKT * 64))
    _CONST["rope_s"] = np.ascontiguousarray(rs.transpose(1, 0, 2).reshape(128, NKT * 64))
    c = np.arange(128)
    a = 2.0 * np.pi * ((c[:, None] * c[None, :]) % 128) / 128.0
    _CONST["dft_cs"] = (np.concatenate([np.cos(a), np.sin(a)], 1) / 512.0).astype(np.float32).astype(bf)
    l = np.arange(S)
    m = (l[:, None] * l[None, :]) % S
    ang2 = 2.0 * np.pi * m / float(S)
    CL = np.cos(ang2).astype(np.float32).astype(bf)
    SL = (-np.sin(ang2)).astype(np.float32).astype(bf)
    both = np.stack([CL, SL], 0)
    both = both.reshape(2, 4, 4, 128, 4, 512)
    both = both.transpose(4, 1, 3, 2, 0, 5)
    _CONST["dft_l"] = np.ascontiguousarray(both).reshape(16, 128, 4096)
    return _CONST


def _col(v, n):
    return np.ascontiguousarray(np.asarray(v, np.float32).reshape(n, 128).T)


def _prep(inputs):
    f = lambda a: np.ascontiguousarray(np.asarray(a, dtype=np.float32))
    cst = _consts()
    shared = {
        "w_ada": f(inputs["w_ada"][0]),
        "b_ada_col": _col(inputs["b_ada"][0], 48),
        "b_ada": f(inputs["b_ada"][0]),
        "gcols": np.ascontiguousarray(np.concatenate([_col(inputs["g_norm_mix"][0], 8), _col(inputs["g_norm_ffn"][0], 8),
                                                      _col(inputs["b_gate"][0], 16)], axis=1)),
        "g_q": f(inputs["g_q"][0]),
        "g_k": f(inputs["g_k"][0]),
        "g_final": f(inputs["g_final"]),
        "w_in": f(inputs["w_in"][0]),
        "w_ap": f(inputs["w_attn_proj"][0]),
        "w_fp": f(inputs["w_fourier_proj"][0]),
        "w_o": f(inputs["w_o"][0]),
        "w_gu": f(inputs["w_gate_up"][0]),
        "w_dn": f(inputs["w_down"][0]),
    }
    shared.update(cst)
    x = f(inputs["x"])
    ctx = f(inputs["ctx"])
    c = f(inputs["c"])
    cc = _col(inputs["c_ctx"], 8)
    maps = []
    for b in range(8):
        m = dict(shared)
        m["x"] = x[b]
        m["ctx"] = ctx[b]
        cond = np.stack([_col(c[b], 8), cc], axis=-1)
        m["cond"] = np.ascontiguousarray(cond.reshape(128, 16))
        maps.append(m)
    return maps


_NC = {}


def kernel(**inputs):
    if "nc" not in _NC:
        _NC["nc"] = build(False)[0]
    maps = _prep(inputs)
    res = run_bass_kernel_spmd(_NC["nc"], maps, core_ids=list(range(8)))
    out = np.stack([np.asarray(r["out"], dtype=np.float32) for r in res.results], axis=0)
    return out
```
